# Optimizing a Trainium2 kernel written in Bass

```python
import math
import jax, jax.numpy as jnp
from jax import lax
import numpy as np

D_MODEL = 1024
BATCH = 8
SEQ = 8192
DEPTH = 2
DEC_BATCH = 16
DEC_SEQ = 2048
PAST_LEN = 128

GRID_W = 64
N_META = 16
WIN_R = 8
WIN_C = 16
NA_HEADS = 6
NA_HD = 64
DA_HEADS = 6
DA_HD = 32
DA_VD = 64
DA_ROT = DA_HD // 4
MLA_HEADS = 4
MLA_NOPE = 64
MLA_ROPE = 32
MLA_VD = 64
Q_LORA = 256
KV_LORA = 128
ROPE_THETA = 500000.0
D_FF = 2816
Q_BLOCK = 128
EPS = 1e-6
NA_W = NA_HEADS * NA_HD
DA_W = DA_HEADS * DA_VD
MLA_W = MLA_HEADS * MLA_VD
MIX_W = NA_W + DA_W + MLA_W
DA_QK_W = DA_HEADS * 2 * DA_HD
IN_W = 3 * NA_W + 2 * DA_QK_W + DA_W + Q_LORA + KV_LORA + MLA_ROPE

kernel_name = 'hybrid_bidir_encoder_na_diff_mla'


def rms_norm(x, g):
    xf = x.astype(jnp.float32)
    y = xf * lax.rsqrt(jnp.mean(xf * xf, axis=-1, keepdims=True) + EPS)
    return (y * g.astype(jnp.float32)).astype(x.dtype)


def swiglu(x, w_gate, w_up, w_down):
    return (jax.nn.silu(x @ w_gate) * (x @ w_up)) @ w_down


def rope_tables(pos, dim, dtype):
    inv = ROPE_THETA ** (-(jnp.arange(0, dim, 2, dtype=jnp.float32) / dim))
    ang = pos[:, None] * inv[None, :]
    return jnp.cos(ang).astype(dtype), jnp.sin(ang).astype(dtype)


def apply_rope(x, cos, sin):
    half = x.shape[-1] // 2
    x1, x2 = x[..., :half], x[..., half:]
    return jnp.concatenate([x1 * cos - x2 * sin, x2 * cos + x1 * sin], axis=-1)


def sweep_queries(attend, qs):
    meta_out = attend(*[q[:, :N_META] for q in qs])
    real = [q[:, N_META:] for q in qs]
    b, n = real[0].shape[0], real[0].shape[1]
    nb = n // Q_BLOCK
    blocks = tuple(jnp.moveaxis(q.reshape((b, nb, Q_BLOCK) + q.shape[2:]), 1, 0) for q in real)
    out = lax.map(lambda xs: attend(*xs), blocks)
    out = jnp.moveaxis(out, 0, 1).reshape((b, n) + out.shape[3:])
    return jnp.concatenate([meta_out, out], axis=1)


def neighbourhood_attention(q, k, v, rel_bias, rows):
    b = q.shape[0]
    n = rows * GRID_W
    wr = min(WIN_R, rows)
    scale = NA_HD ** -0.5
    q = q.reshape(b, N_META + n, NA_HEADS, NA_HD)
    k = k.reshape(b, N_META + n, NA_HEADS, NA_HD)
    v = v.reshape(b, N_META + n, NA_HEADS, NA_HD)
    qm, km, vm = q[:, :N_META], k[:, :N_META], v[:, :N_META]
    sm = jnp.einsum('bqhd,bkhd->bhqk', qm, km, preferred_element_type=jnp.float32) * scale
    om = jnp.einsum('bhqk,bkhd->bqhd', jax.nn.softmax(sm, axis=-1).astype(v.dtype), vm)
    qg = q[:, N_META:].reshape(b, rows, GRID_W, NA_HEADS, NA_HD)
    kg = k[:, N_META:].reshape(b, rows, GRID_W, NA_HEADS, NA_HD)
    vg = v[:, N_META:].reshape(b, rows, GRID_W, NA_HEADS, NA_HD)
    row_start = jnp.clip(jnp.arange(rows) - WIN_R // 2, 0, rows - wr)
    col_idx = jnp.clip(jnp.arange(GRID_W) - WIN_C // 2, 0, GRID_W - WIN_C)[:, None] + jnp.arange(WIN_C)[None, :]
    col_off = col_idx - jnp.arange(GRID_W)[:, None] + (WIN_C - 1)
    bias_c = rel_bias[:, :, col_off]

    def row_block(r):
        rs = row_start[r]
        q_row = lax.dynamic_index_in_dim(qg, r, axis=1, keepdims=False)
        k_win = lax.dynamic_slice_in_dim(kg, rs, wr, axis=1)[:, :, col_idx]
        v_win = lax.dynamic_slice_in_dim(vg, rs, wr, axis=1)[:, :, col_idx]
        bias = bias_c[:, rs + jnp.arange(wr) - r + (WIN_R - 1)]
        s_win = jnp.einsum('bchd,bwckhd->bhcwk', q_row, k_win, preferred_element_type=jnp.float32) * scale
        s_win = s_win + jnp.transpose(bias, (0, 2, 1, 3)).astype(jnp.float32)
        s_meta = jnp.einsum('bchd,bmhd->bhcm', q_row, km, preferred_element_type=jnp.float32) * scale
        s = jnp.concatenate([s_win.reshape(b, NA_HEADS, GRID_W, wr * WIN_C), s_meta], axis=-1)
        p = jax.nn.softmax(s, axis=-1).astype(v.dtype)
        p_win = p[..., :wr * WIN_C].reshape(b, NA_HEADS, GRID_W, wr, WIN_C)
        p_meta = p[..., wr * WIN_C:]
        return (jnp.einsum('bhcwk,bwckhd->bchd', p_win, v_win)
                + jnp.einsum('bhcm,bmhd->bchd', p_meta, vm))

    og = lax.map(row_block, jnp.arange(rows))
    og = jnp.moveaxis(og, 0, 1).reshape(b, n, NA_W)
    return jnp.concatenate([om.reshape(b, N_META, NA_W), og], axis=1)


def differential_attention(q, k, v, lam_params, subln_g, layer, cos, sin):
    b, t = q.shape[0], q.shape[1]
    q = q.reshape(b, t, DA_HEADS, 2, DA_HD)
    k = k.reshape(b, t, DA_HEADS, 2, DA_HD)
    v = v.reshape(b, t, DA_HEADS, DA_VD)
    c, s_ = cos[:, None, None, :], sin[:, None, None, :]
    q = jnp.concatenate([apply_rope(q[..., :DA_ROT], c, s_), q[..., DA_ROT:]], axis=-1)
    k = jnp.concatenate([apply_rope(k[..., :DA_ROT], c, s_), k[..., DA_ROT:]], axis=-1)
    lp = lam_params.astype(jnp.float32)
    lam_init = 0.8 - 0.6 * math.exp(-0.3 * layer)
    lam = jnp.exp(jnp.sum(lp[0] * lp[1])) - jnp.exp(jnp.sum(lp[2] * lp[3])) + lam_init
    scale = DA_HD ** -0.5

    def attend(qb):
        tq = qb.shape[1]
        s = jnp.einsum('bqhmd,bkhmd->bhmqk', qb, k, preferred_element_type=jnp.float32) * scale
        p = jax.nn.softmax(s, axis=-1)
        a = (p[:, :, 0] - lam * p[:, :, 1]).astype(v.dtype)
        o = jnp.einsum('bhqk,bkhe->bqhe', a, v)
        o = rms_norm(o, subln_g) * (1.0 - lam_init)
        return o.reshape(b, tq, DA_W)

    return sweep_queries(attend, (q,))


def latent_attention(cq, ckv, krope, q_norm_g, kv_norm_g, w_uq, w_ukv, cos, sin):
    b, t = cq.shape[0], cq.shape[1]
    q = (rms_norm(cq, q_norm_g) @ w_uq).reshape(b, t, MLA_HEADS, MLA_NOPE + MLA_ROPE)
    q_nope = q[..., :MLA_NOPE]
    q_rope = apply_rope(q[..., MLA_NOPE:], cos[:, None, :], sin[:, None, :])
    kv = (rms_norm(ckv, kv_norm_g) @ w_ukv).reshape(b, t, MLA_HEADS, MLA_NOPE + MLA_VD)
    k_nope, v = kv[..., :MLA_NOPE], kv[..., MLA_NOPE:]
    k_rope = apply_rope(krope, cos, sin)
    scale = (MLA_NOPE + MLA_ROPE) ** -0.5

    def attend(qn, qr):
        tq = qn.shape[1]
        s = (jnp.einsum('bqhd,bkhd->bhqk', qn, k_nope, preferred_element_type=jnp.float32)
             + jnp.einsum('bqhr,bkr->bhqk', qr, k_rope, preferred_element_type=jnp.float32)) * scale
        p = jax.nn.softmax(s, axis=-1).astype(v.dtype)
        return jnp.einsum('bhqk,bkhe->bqhe', p, v).reshape(b, tq, MLA_W)

    return sweep_queries(attend, (q_nope, q_rope))


def trunk(x, meta_tokens, norm_g, final_norm_g, ffn_w_gate, ffn_w_up, ffn_w_down, w_in, w_out,
          na_rel_bias, da_lambda, da_subln_g, mla_q_norm_g, mla_kv_norm_g, mla_w_uq, mla_w_ukv):
    b, n, _ = x.shape
    rows = n // GRID_W
    meta = jnp.broadcast_to(meta_tokens[None].astype(x.dtype), (b, N_META, D_MODEL))
    h = jnp.concatenate([meta, x], axis=1)
    pos = jnp.arange(N_META + n, dtype=jnp.float32)
    cos_da, sin_da = rope_tables(pos, DA_ROT, x.dtype)
    cos_mla, sin_mla = rope_tables(pos, MLA_ROPE, x.dtype)
    sizes = (NA_W, NA_W, NA_W, DA_QK_W, DA_QK_W, DA_W, Q_LORA, KV_LORA, MLA_ROPE)
    split_idx = np.cumsum(sizes)[:-1].tolist()
    for l in range(DEPTH):
        h = h + 0.5 * swiglu(rms_norm(h, norm_g[l, 0]), ffn_w_gate[l, 0], ffn_w_up[l, 0], ffn_w_down[l, 0])
        u = rms_norm(h, norm_g[l, 1]) @ w_in[l]
        na_q, na_k, na_v, da_q, da_k, da_v, mla_cq, mla_ckv, mla_kr = jnp.split(u, split_idx, axis=-1)
        o_na = neighbourhood_attention(na_q, na_k, na_v, na_rel_bias[l], rows)
        o_da = differential_attention(da_q, da_k, da_v, da_lambda[l], da_subln_g[l], l, cos_da, sin_da)
        o_mla = latent_attention(mla_cq, mla_ckv, mla_kr, mla_q_norm_g[l], mla_kv_norm_g[l],
                                 mla_w_uq[l], mla_w_ukv[l], cos_mla, sin_mla)
        h = h + jnp.concatenate([o_na, o_da, o_mla], axis=-1) @ w_out[l]
        h = h + 0.5 * swiglu(rms_norm(h, norm_g[l, 2]), ffn_w_gate[l, 1], ffn_w_up[l, 1], ffn_w_down[l, 1])
    return rms_norm(h, final_norm_g)[:, N_META:]


def setup_inputs(seed: int = 0) -> dict:
    key = jax.random.key(seed)
    ks = jax.random.split(key, 17)
    nrm = lambda k, shape, s: jax.random.normal(k, shape, jnp.float32) * s
    return {
        'x_prompt': nrm(ks[0], (BATCH, SEQ, D_MODEL), 1.0),
        'x_sample': nrm(ks[1], (DEC_BATCH, DEC_SEQ, D_MODEL), 1.0),
        'meta_tokens': nrm(ks[2], (N_META, D_MODEL), 1.0),
        'norm_g': 1.0 + nrm(ks[3], (DEPTH, 3, D_MODEL), 0.05),
        'final_norm_g': 1.0 + nrm(ks[4], (D_MODEL,), 0.05),
        'ffn_w_gate': nrm(ks[5], (DEPTH, 2, D_MODEL, D_FF), D_MODEL ** -0.5),
        'ffn_w_up': nrm(ks[6], (DEPTH, 2, D_MODEL, D_FF), D_MODEL ** -0.5),
        'ffn_w_down': nrm(ks[7], (DEPTH, 2, D_FF, D_MODEL), D_FF ** -0.5),
        'w_in': nrm(ks[8], (DEPTH, D_MODEL, IN_W), D_MODEL ** -0.5),
        'w_out': nrm(ks[9], (DEPTH, MIX_W, D_MODEL), MIX_W ** -0.5),
        'na_rel_bias': nrm(ks[10], (DEPTH, NA_HEADS, 2 * WIN_R - 1, 2 * WIN_C - 1), 0.1),
        'da_lambda': nrm(ks[11], (DEPTH, 4, DA_HD), 0.1),
        'da_subln_g': 1.0 + nrm(ks[12], (DEPTH, DA_VD), 0.05),
        'mla_q_norm_g': 1.0 + nrm(ks[13], (DEPTH, Q_LORA), 0.05),
        'mla_kv_norm_g': 1.0 + nrm(ks[14], (DEPTH, KV_LORA), 0.05),
        'mla_w_uq': nrm(ks[15], (DEPTH, Q_LORA, MLA_HEADS * (MLA_NOPE + MLA_ROPE)), Q_LORA ** -0.5),
        'mla_w_ukv': nrm(ks[16], (DEPTH, KV_LORA, MLA_HEADS * (MLA_NOPE + MLA_VD)), KV_LORA ** -0.5),
    }


def reference(x_prompt, x_sample, meta_tokens, norm_g, final_norm_g, ffn_w_gate, ffn_w_up, ffn_w_down,
              w_in, w_out, na_rel_bias, da_lambda, da_subln_g, mla_q_norm_g, mla_kv_norm_g, mla_w_uq, mla_w_ukv):
    y_prompt = trunk(x_prompt, meta_tokens, norm_g, final_norm_g, ffn_w_gate, ffn_w_up, ffn_w_down, w_in, w_out,
                     na_rel_bias, da_lambda, da_subln_g, mla_q_norm_g, mla_kv_norm_g, mla_w_uq, mla_w_ukv)
    y_sample = trunk(x_sample, meta_tokens, norm_g, final_norm_g, ffn_w_gate, ffn_w_up, ffn_w_down, w_in, w_out,
                     na_rel_bias, da_lambda, da_subln_g, mla_q_norm_g, mla_kv_norm_g, mla_w_uq, mla_w_ukv)
    return (y_prompt, y_sample)
```

```python
import math
from contextlib import ExitStack
import numpy as np
import concourse.bass as bass
import concourse.mybir as mybir
from concourse.bass_utils import run_bass_kernel_spmd

F32 = mybir.dt.float32
BF16 = mybir.dt.bfloat16
AF = mybir.ActivationFunctionType
ALU = mybir.AluOpType

D = 1024
NMETA = 16
EPS = 1e-6
NEG = -30000.0
NCORES = 8
ROPE_THETA = 500000.0
DEBUG = False
STOP = 5
SUB = 9


class Buf:
    def __init__(self, name, t=None, dram=False):
        self.name = name
        self.t = t
        self.dram = dram
        self.w = {}
        self.rd = {}
        self.sem = None
        self.cnt = 0

    def add_rd(self, tok):
        sid = id(tok[0])
        if self.rd.get(sid, (None, 0))[1] < tok[1]:
            self.rd[sid] = tok

    def set_w(self, tok):
        sid = id(tok[0])
        if self.w.get(sid, (None, 0))[1] < tok[1]:
            self.w[sid] = tok


class Ring:
    def __init__(self, bufs):
        self.bufs = bufs
        self.i = 0

    def next(self):
        b = self.bufs[self.i % len(self.bufs)]
        self.i += 1
        return b


class Sched:
    def __init__(self, nc, stack):
        self.nc = nc
        self.stack = stack
        self.E = {"pe": nc.tensor, "act": nc.scalar, "dve": nc.vector, "pool": nc.gpsimd, "sp": nc.sync}
        self.esem = {}
        for e in ("pe", "act", "dve", "pool"):
            self.esem[e] = stack.enter_context(nc.semaphore("es_" + e))
        self.cnt = {e: 0 for e in self.E}
        self.seen = {e: {} for e in self.E}
        self.semcnt = {}
        self.free_sems = []
        self.nsem = 4
        self.ps = None
        self.psi = 0

    def _waits(self, eng, reads, writes):
        toks = {}
        own = self.esem.get(eng)

        def add(d, raw):
            for sid, (sem, val) in d.items():
                if sem is own and (eng == "pe" or not raw):
                    continue
                if toks.get(sid, (None, 0))[1] < val:
                    toks[sid] = (sem, val)

        for b in reads:
            add(b.w, True)
        for b in writes:
            add(b.w, False)
            add(b.rd, False)
        e = self.E[eng]
        seen = self.seen[eng]
        for sid, (sem, val) in toks.items():
            if seen.get(sid, 0) >= val:
                continue
            seen[sid] = val
            e.wait_ge(sem, val)

    def op(self, eng, fn, reads=(), writes=(), sig=True):
        self._waits(eng, reads, writes)
        ins = fn(self.E[eng])
        if sig:
            self.cnt[eng] += 1
            ins.then_inc(self.esem[eng], 1)
            idx = self.cnt[eng]
        else:
            idx = self.cnt[eng] + 1
        tok = (self.esem[eng], idx)
        for b in reads:
            b.add_rd(tok)
        for b in writes:
            b.set_w(tok)

    def dma(self, eng, out, in_, reads, writes, sb):
        self._waits(eng, reads, writes)
        if sb.sem is None:
            if self.free_sems:
                sb.sem, sb.cnt = self.free_sems.pop()
            else:
                sb.sem = self.stack.enter_context(self.nc.semaphore("ds_%d" % self.nsem))
                self.nsem += 1
        self.E[eng].dma_start(out=out, in_=in_).then_inc(sb.sem, 16)
        sb.cnt += 16
        self.semcnt[id(sb.sem)] = (sb.sem, sb.cnt)
        tok = (sb.sem, sb.cnt)
        for b in reads:
            b.add_rd(tok)
        for b in writes:
            b.set_w(tok)

    def barrier(self):
        for eng, e in self.E.items():
            seen = self.seen[eng]
            for o, sem in self.esem.items():
                v = self.cnt[o]
                if v > 0 and seen.get(id(sem), 0) < v and o != eng:
                    seen[id(sem)] = v
                    e.wait_ge(sem, v)
            for sid, (sem, v) in self.semcnt.items():
                if seen.get(sid, 0) < v:
                    seen[sid] = v
                    e.wait_ge(sem, v)

    def release(self, bufs):
        for b in bufs:
            if b.sem is not None:
                self.free_sems.append((b.sem, b.cnt))
                b.sem = None

    def psum(self):
        b = self.ps[self.psi % 8]
        self.psi += 1
        return b


class Cfg:
    def __init__(self, NP, NS, DFF):
        self.NP, self.NS, self.DFF = NP, NS, DFF
        self.FC = DFF // 128
        self.seqn = [NP, NS, NS]
        self.start = [0, NP, NP + NS]
        self.R = NP + 2 * NS
        self.TT = self.R + 3 * NMETA
        self.NMAX = max(self.seqn)


NA_Q, NA_K, NA_V, DA_Q, DA_K, DA_V, M_CQ, M_CKV, M_KR = 0, 384, 768, 1152, 1536, 1920, 2304, 2560, 2688
NCH_IN = 23


def _win_cols():
    idx = -np.ones((NCH_IN, 128), np.int64)
    r = np.arange(128)
    for i in range(3):
        idx[i] = NA_Q + 128 * i + r
        idx[3 + i] = NA_K + 128 * i + r
        dd = r % 32
        pr = np.where(dd < 4, r + 4, np.where(dd < 8, r - 4, r))
        idx[6 + i] = DA_Q + 128 * i + r
        idx[9 + i] = DA_Q + 128 * i + pr
        idx[12 + i] = DA_K + 128 * i + r
        idx[15 + i] = DA_K + 128 * i + pr
    idx[18] = M_CQ + r
    idx[19] = M_CQ + 128 + r
    idx[20] = M_CKV + r
    idx[21, :32] = M_KR + np.arange(32)
    idx[22, :16] = M_KR + np.arange(16) + 16
    idx[22, 16:32] = M_KR + np.arange(16)
    return idx.reshape(-1)


def _gather_cols(w, idx):
    out = w[:, np.maximum(idx, 0)]
    out = np.where(idx[None, :] >= 0, out, np.float32(0.0))
    return np.ascontiguousarray(out.astype(np.float32))


def _host_shared(cfg, inp):
    FC = cfg.FC
    sh = {}
    g, u, dn = inp["ffn_w_gate"], inp["ffn_w_up"], inp["ffn_w_down"]
    wgu = np.empty((4, 2, FC, 128, 8 * 128), np.float32)
    wd = np.empty((4, 8, 128, FC * 128), np.float32)
    for l in range(2):
        for j in range(2):
            lj = l * 2 + j
            wgu[lj, 0] = g[l, j].reshape(8, 128, FC, 128).transpose(2, 1, 0, 3).reshape(FC, 128, 1024)
            wgu[lj, 1] = u[l, j].reshape(8, 128, FC, 128).transpose(2, 1, 0, 3).reshape(FC, 128, 1024)
            wd[lj] = dn[l, j].reshape(FC, 128, 8, 128).transpose(2, 1, 0, 3).reshape(8, 128, FC * 128)
    sh["wgu"] = wgu.reshape(4 * 2 * FC * 128, 1024)
    sh["wd"] = wd.reshape(4 * 8 * 128, FC * 128)
    idx = _win_cols()
    win = np.empty((2, NCH_IN, 128, 1024), np.float32)
    wv = np.empty((2, 128, 8 * 768), np.float32)
    wuq = np.empty((2, 8, 128, 256), np.float32)
    wukv = np.empty((2, 128, 512), np.float32)
    wo = np.empty((2, 8, 128, 1024), np.float32)
    f = np.arange(128)
    for l in range(2):
        w = inp["w_in"][l]
        we = _gather_cols(w, idx)
        win[l] = we.reshape(8, 128, NCH_IN, 128).transpose(2, 1, 0, 3).reshape(NCH_IN, 128, 1024)
        vcols = np.concatenate([NA_V + np.arange(384), DA_V + np.arange(384)])
        wv[l] = w[:, vcols].reshape(8, 128, 768).transpose(1, 0, 2).reshape(128, 8 * 768)
        uq = inp["mla_w_uq"][l]
        for h in range(4):
            im = np.where(f < 96, h * 96 + f, -1)
            ip = np.where((f >= 64) & (f < 80), h * 96 + f + 16, np.where((f >= 80) & (f < 96), h * 96 + f - 16, -1))
            for k, ii in ((0, im), (1, ip)):
                m = _gather_cols(uq, ii)
                wuq[l, h * 2 + k] = m.reshape(2, 128, 128).transpose(1, 0, 2).reshape(128, 256)
        ukv = inp["mla_w_ukv"][l]
        kc = np.concatenate([h * 128 + np.arange(64) for h in range(4)])
        vc = np.concatenate([h * 128 + 64 + np.arange(64) for h in range(4)])
        wukv[l] = np.concatenate([ukv[:, kc], ukv[:, vc]], axis=1)
        wo[l] = inp["w_out"][l].reshape(8, 128, 8, 128).transpose(2, 1, 0, 3).reshape(8, 128, 1024)
    sh["win"] = win.reshape(2 * NCH_IN * 128, 1024)
    sh["wv"] = wv.reshape(2 * 128, 8 * 768)
    sh["wuq"] = wuq.reshape(2 * 8 * 128, 256)
    sh["wukv"] = wukv.reshape(2 * 128, 512)
    sh["wo"] = wo.reshape(2 * 8 * 128, 1024)
    cols = []
    for l in range(2):
        for i in range(3):
            cols.append(inp["norm_g"][l, i].reshape(8, 128).T)
    cols.append(inp["final_norm_g"].reshape(8, 128).T)
    for l in range(2):
        cols.append(inp["mla_q_norm_g"][l].reshape(2, 128).T)
    for l in range(2):
        cols.append(inp["mla_kv_norm_g"][l].reshape(1, 128).T)
    for l in range(2):
        c = np.zeros((128, 1), np.float32)
        c[:64, 0] = inp["da_subln_g"][l]
        cols.append(c)
    sh["gcols"] = np.ascontiguousarray(np.concatenate(cols, axis=1).astype(np.float32))
    lam = np.empty((2, 128), np.float32)
    for l in range(2):
        lp = inp["da_lambda"][l]
        lam[l] = np.concatenate([lp[0], lp[2], lp[1], lp[3]])
    sh["dalam"] = lam
    kc = np.arange(64)[:, None]
    qc = np.arange(64)[None, :]
    cs = np.clip(qc - 8, 0, 48)
    valid = (kc >= cs) & (kc < cs + 16)
    ci = np.clip(kc - qc + 15, 0, 30)
    rb = inp["na_rel_bias"]
    rbx = rb[:, :, ::-1, :][:, :, :, ci]
    rbx = np.where(valid[None, None, None], rbx, np.float32(NEG)).astype(np.float32)
    sh["rbx"] = np.ascontiguousarray(rbx.reshape(2 * 6 * 15 * 64, 64))
    pos = np.empty(cfg.TT, np.float32)
    for s in range(3):
        pos[cfg.start[s]:cfg.start[s] + cfg.seqn[s]] = NMETA + np.arange(cfg.seqn[s], dtype=np.float32)
        pos[cfg.R + NMETA * s:cfg.R + NMETA * (s + 1)] = np.arange(NMETA, dtype=np.float32)

    def tables(dim):
        inv = (np.float32(ROPE_THETA) ** (-(np.arange(0, dim, 2, dtype=np.float32) / np.float32(dim)))).astype(np.float32)
        ang = pos[:, None] * inv[None, :]
        return np.cos(ang).astype(np.float32).T, np.sin(ang).astype(np.float32).T

    c8, s8 = tables(8)
    t = np.zeros((2, 32, cfg.TT), np.float32)
    t[0, :] = 1.0
    t[0, 0:4] = c8
    t[0, 4:8] = c8
    t[1, 0:4] = -s8
    t[1, 4:8] = s8
    sh["ropeda"] = np.ascontiguousarray(t.reshape(64, cfg.TT))
    c32, s32 = tables(32)
    t = np.zeros((2, 32, cfg.TT), np.float32)
    t[0, 0:16] = c32
    t[0, 16:32] = c32
    t[1, 0:16] = -s32
    t[1, 16:32] = s32
    sh["ropeml"] = np.ascontiguousarray(t.reshape(64, cfg.TT))
    sh["ident"] = np.eye(128, dtype=np.float32)
    sh["metatok"] = np.ascontiguousarray(inp["meta_tokens"].astype(np.float32))
    return sh


def build_program(cfg):
    FC, R, TT = cfg.FC, cfg.R, cfg.TT
    nc = bass.Bass("TRN2", target_bir_lowering=False)
    stack = ExitStack()
    with stack:
        def din(name, shape):
            return nc.dram_tensor(name, list(shape), F32, kind="ExternalInput").ap()

        x_d = din("xtok", [R, D])
        meta_d = din("metatok", [NMETA, D])
        wgu_f = din("wgu", [4 * 2 * FC * 128, 1024])
        wd_f = din("wd", [4 * 8 * 128, FC * 128])
        win_f = din("win", [2 * NCH_IN * 128, 1024])
        wv_f = din("wv", [2 * 128, 8 * 768])
        wuq_f = din("wuq", [2 * 8 * 128, 256])
        wukv_f = din("wukv", [2 * 128, 512])
        wo_f = din("wo", [2 * 8 * 128, 1024])
        gcols_d = din("gcols", [128, 64])
        dalam_d = din("dalam", [2, 128])
        rbx_d = din("rbx", [2 * 6 * 15 * 64, 64])
        ropeda_d = din("ropeda", [64, TT])
        ropeml_d = din("ropeml", [64, TT])
        ident_d = din("ident", [128, 128])
        y_d = nc.dram_tensor("y", [R, D], F32, kind="ExternalOutput").ap()

        def dscr(name, shape, dt=BF16):
            if DEBUG and name.endswith("_s"):
                return nc.dram_tensor(name, list(shape), dt, kind="ExternalOutput").ap()
            return nc.dram_tensor(name, list(shape), dt).ap()

        wgu_b = dscr("wgu_b", [4 * 2 * FC * 128, 1024])
        wd_b = dscr("wd_b", [4 * 8 * 128, FC * 128])
        win_b = dscr("win_b", [2 * NCH_IN * 128, 1024])
        wv_b = dscr("wv_b", [2 * 128, 8 * 768])
        wuq_b = dscr("wuq_b", [2 * 8 * 128, 256])
        wukv_b = dscr("wukv_b", [2 * 128, 512])
        wo_b = dscr("wo_b", [2 * 8 * 128, 1024])
        hT_d = dscr("hT_s", [D, TT], F32)
        oT_d = dscr("oT_s", [D, TT])
        naq_d = dscr("naq_s", [384, TT])
        nak_d = dscr("nak_s", [384, TT])
        daq_d = dscr("daq_s", [384, TT])
        dak_d = dscr("dak_s", [384, TT])
        mlq_d = dscr("mlq_s", [384, TT])
        mlk_d = dscr("mlk_s", [256, TT])
        mlr_d = dscr("mlr_s", [32, TT])
        v_d = dscr("v_s", [TT, 1024])

        S = Sched(nc, stack)
        S.ps = [Buf("ps%d" % i, stack.enter_context(nc.psum_tensor("ps%d" % i, [128, 512], F32))) for i in range(8)]

        phase_bufs = []

        uniq = [0]

        def sb(st, name, shape, dt):
            uniq[0] += 1
            name = "s%d_%s" % (uniq[0], name)
            b = Buf(name, st.enter_context(nc.sbuf_tensor(name, list(shape), dt)))
            if st is not stack:
                phase_bufs.append(b)
            return b

        Bx = Buf("x", dram=True)
        Bconst = Buf("const", dram=True)
        Bw = {k: Buf("w_" + k, dram=True) for k in ("gu0", "gu1", "gu2", "gu3", "d0", "d1", "d2", "d3", "in0", "in1", "misc")}
        BhT = {}
        BoT = {}
        Bqk = Buf("qk", dram=True)
        Bv = Buf("v", dram=True)
        By = Buf("y", dram=True)

        ident = sb(stack, "ident", [128, 128], F32)
        onesb = sb(stack, "onesb", [128, 128], BF16)
        ones64 = sb(stack, "ones64", [64, 64], F32)
        sel65 = sb(stack, "sel65", [65, 64], F32)
        onesrow = sb(stack, "onesrow", [1, 64], F32)
        gcols = sb(stack, "gcols", [128, 64], F32)
        neglam = sb(stack, "neglam", [64, 2], F32)
        gsub = sb(stack, "gsub", [64, 2], F32)
        lamrow = sb(stack, "lamrow", [1, 256], F32)
        lamtmp = sb(stack, "lamtmp", [1, 128], F32)
        epscol = sb(stack, "epscol", [128, 1], F32)
        S.op("pool", lambda e: e.memset(epscol.t[:], EPS), [], [epscol])
        S.dma("sp", ident.t[:], ident_d[:, :], [Bconst], [ident], ident)
        S.dma("sp", gcols.t[:], gcols_d[:, :], [Bconst], [gcols], gcols)
        S.dma("sp", lamrow.t[0:1, :], dalam_d.rearrange("l f -> (l f)").rearrange("(o f) -> o f", o=1), [Bconst], [lamrow], lamrow)
        S.op("pool", lambda e: e.memset(onesb.t[:], 1.0), [], [onesb])
        S.op("pool", lambda e: e.memset(ones64.t[:], 1.0), [], [ones64])
        S.op("pool", lambda e: e.memset(sel65.t[:], 0.0), [], [sel65])
        S.op("pool", lambda e: e.memset(sel65.t[64:65, :], 1.0), [], [sel65])
        S.op("pool", lambda e: e.memset(onesrow.t[:], 1.0), [], [onesrow])

        def convert(dst, src, rows, key, r0=0):
            r = r0
            while r < r0 + rows:
                n = min(128, r0 + rows - r)
                S.dma("pool", dst[r:r + n, :], src[r:r + n, :], [Bconst], [Bw[key]], Bw[key])
                r += n

        def conv_ffn(lj):
            convert(wgu_b, wgu_f, 2 * FC * 128, "gu%d" % lj, lj * 2 * FC * 128)
            convert(wd_b, wd_f, 8 * 128, "d%d" % lj, lj * 8 * 128)

        conv_ffn(0)
        convert(win_b, win_f, NCH_IN * 128, "in0", 0)
        convert(wv_b, wv_f, 2 * 128, "misc")
        convert(wuq_b, wuq_f, 2 * 8 * 128, "misc")
        convert(wukv_b, wukv_f, 2 * 128, "misc")
        convert(wo_b, wo_f, 2 * 8 * 128, "misc")
        conv_ffn(1)
        conv_ffn(2)
        convert(win_b, win_f, NCH_IN * 128, "in1", NCH_IN * 128)
        conv_ffn(3)

        for l in range(2):
            lam_init = 0.8 - 0.6 * math.exp(-0.3 * l)
            A = lamrow.t[0:1, l * 128:l * 128 + 64]
            Bm = lamrow.t[0:1, l * 128 + 64:l * 128 + 128]
            S.op("dve", lambda e: e.tensor_tensor(lamtmp.t[0:1, 0:64], A, Bm, ALU.mult), [lamrow], [lamtmp])
            S.op("dve", lambda e: e.tensor_reduce(lamtmp.t[0:1, 64:66], lamtmp.t[0:1, 0:64].rearrange("o (g f) -> o g f", g=2),
                                                   mybir.AxisListType.X, ALU.add), [lamtmp], [lamtmp])
            S.op("act", lambda e: e.activation(out=lamtmp.t[0:1, 66:68], in_=lamtmp.t[0:1, 64:66], func=AF.Exp), [lamtmp], [lamtmp])
            S.op("dve", lambda e: e.tensor_tensor(lamtmp.t[0:1, 68:69], lamtmp.t[0:1, 67:68], lamtmp.t[0:1, 66:67], ALU.subtract), [lamtmp], [lamtmp])
            S.op("dve", lambda e: e.tensor_scalar(lamtmp.t[0:1, 69:70], lamtmp.t[0:1, 68:69], -lam_init, None, ALU.add), [lamtmp], [lamtmp])
            p = S.psum()
            S.op("pe", lambda e: e.matmul(p.t[0:64, 0:1], onesrow.t[0:1, 0:64], lamtmp.t[0:1, 69:70], start=True, stop=True), [onesrow, lamtmp], [p])
            S.op("dve", lambda e: e.tensor_copy(neglam.t[:, l:l + 1], p.t[0:64, 0:1]), [p], [neglam])
            S.op("dve", lambda e: e.tensor_scalar(gsub.t[:, l:l + 1], gcols.t[0:64, 62 + l:63 + l], 1.0 - lam_init, None, ALU.mult), [gcols], [gsub])

        cp_flip = [0]

        def evac(out_ap, in_ap, reads, writes):
            cp_flip[0] ^= 1
            if cp_flip[0]:
                S.op("act", lambda e: e.activation(out=out_ap, in_=in_ap, func=AF.Copy), reads, writes)
            else:
                S.op("dve", lambda e: e.tensor_copy(out_ap, in_ap), reads, writes)

        def rms_rstd(srcs, W, Dn, sqring, rsring):
            ps = S.psum()
            n = len(srcs)
            for c, (b, ap) in enumerate(srcs):
                sq = sqring.next()
                S.op("act", lambda e: e.activation(out=sq.t[:, :W], in_=ap, func=AF.Square), [b], [sq])
                S.op("pe", lambda e: e.matmul(ps.t[:, :W], onesb.t[:, :], sq.t[:, :W], start=(c == 0), stop=(c == n - 1)),
                     [sq, onesb], [ps], sig=True)
            rs = rsring.next()
            S.op("act", lambda e: e.activation(out=rs.t[:, :W], in_=ps.t[:, :W], func=AF.Sqrt, bias=epscol.t[:, 0:1], scale=1.0 / Dn), [ps, epscol], [rs])
            S.op("dve", lambda e: e.reciprocal(rs.t[:, :W], rs.t[:, :W]), [rs], [rs])
            return rs

        def tok_phase(stage):
            with ExitStack() as ph:
                NS_ = 2
                h = [sb(ph, "h%d" % i, [128, 8, 512], F32) for i in range(NS_)]
                xn = [sb(ph, "xn%d" % i, [128, 8, 512], BF16) for i in range(NS_)]
                hid = [sb(ph, "hid%d" % i, [128, FC, 512], BF16) for i in range(NS_)]
                sqring = Ring([sb(ph, "sq%d" % i, [128, 512], BF16) for i in range(2)])
                rsring = Ring([sb(ph, "rs%d" % i, [128, 512], F32) for i in range(2)])
                sgring = Ring([sb(ph, "sg%d" % i, [128, 512], F32) for i in range(4)])
                guring = Ring([sb(ph, "gu%d" % i, [128, 2, 8, 128], BF16) for i in range(3)])
                wdring = Ring([sb(ph, "wdr%d" % i, [128, FC, 128], BF16) for i in range(2)])
                if stage >= 1:
                    o1 = sb(ph, "o1", [128, 8, 512], BF16)
                    woring = Ring([sb(ph, "wor%d" % i, [128, 8, 128], BF16) for i in range(2)])
                if stage == 0:
                    xsring = Ring([sb(ph, "xs%d" % i, [128, 4, 1024], F32) for i in range(1)])
                if stage <= 1:
                    winring = Ring([sb(ph, "winr%d" % i, [128, 8, 128], BF16) for i in range(2)])
                    wvt = sb(ph, "wvt", [128, 8, 768], BF16)
                    wuqt = sb(ph, "wuqt", [128, 8, 2, 128], BF16)
                    wukvt = sb(ph, "wukvt", [128, 512], BF16)
                    rda = [sb(ph, "rda%d" % i, [128, 2, 512], BF16) for i in range(NS_)]
                    rml = [sb(ph, "rml%d" % i, [96, 2, 512], BF16) for i in range(NS_)]
                    uoring = Ring([sb(ph, "uo%d" % i, [128, 512], BF16) for i in range(3)])
                    t1ring = sgring
                    t2ring = sgring
                    cq1 = sb(ph, "cq1", [128, 3, 512], F32)
                    cqn1 = sb(ph, "cqn1", [128, 3, 512], BF16)
                    cq = [cq1, cq1]
                    cqn = [cqn1, cqn1]
                    vtring = Ring([sb(ph, "vt%d" % i, [128, 1024], BF16) for i in range(2)])
                if stage == 2:
                    hn = [sb(ph, "hn%d" % i, [128, 8, 512], F32) for i in range(NS_)]
                    yring = Ring([sb(ph, "yt%d" % i, [128, 1024], F32) for i in range(2)])

                blocks = []
                for c0 in range(0, R, 1024):
                    blocks.append([(c0, 512), (c0 + 512, 512)])
                if stage <= 1:
                    blocks.append([(R, 3 * NMETA)])

                hT_v = hT_d.rearrange("(c p) t -> p c t", p=128)
                oT_v = oT_d.rearrange("(c p) t -> p c t", p=128)

                def ffn(lj, subs, gbase):
                    for i, (c0, W) in enumerate(subs):
                        rs = rms_rstd([(h[i], h[i].t[:, c, :W]) for c in range(8)], W, D, sqring, rsring)
                        if SUB < 1.4:
                            continue
                        for c in range(8):
                            S.op("dve", lambda e: e.scalar_tensor_tensor(out=xn[i].t[:, c, :W], in0=h[i].t[:, c, :W],
                                                                          scalar=gcols.t[:, gbase + c:gbase + c + 1], in1=rs.t[:, :W],
                                                                          op0=ALU.mult, op1=ALU.mult), [h[i], rs, gcols], [xn[i]])
                    if SUB < 1.6:
                        return
                    Bg = Bw["gu%d" % lj]
                    Bd = Bw["d%d" % lj]
                    for fc in range(FC):
                        w = guring.next()
                        for k in range(2):
                            r0 = ((lj * 2 + k) * FC + fc) * 128
                            S.dma("sp", w.t[:, k].rearrange("p c f -> p (c f)"), wgu_b[r0:r0 + 128, :], [Bg], [w], w)
                        for i, (c0, W) in enumerate(subs):
                            pg = S.psum()
                            pu = S.psum()
                            for c in range(8):
                                S.op("pe", lambda e: e.matmul(pg.t[:, :W], w.t[:, 0, c, :], xn[i].t[:, c, :W], start=(c == 0), stop=(c == 7)),
                                     [w, xn[i]], [pg], sig=(c == 7))
                            for c in range(8):
                                S.op("pe", lambda e: e.matmul(pu.t[:, :W], w.t[:, 1, c, :], xn[i].t[:, c, :W], start=(c == 0), stop=(c == 7)),
                                     [w, xn[i]], [pu], sig=(c == 7))
                            sg = sgring.next()
                            S.op("act", lambda e: e.activation(out=sg.t[:, :W], in_=pg.t[:, :W], func=AF.Silu), [pg], [sg])
                            S.op("dve", lambda e: e.tensor_tensor(hid[i].t[:, fc, :W], pu.t[:, :W], sg.t[:, :W], ALU.mult), [pu, sg], [hid[i]])
                    if SUB < 1.8:
                        return
                    for dc in range(8):
                        w = wdring.next()
                        r0 = (lj * 8 + dc) * 128
                        S.dma("sp", w.t[:].rearrange("p c f -> p (c f)"), wd_b[r0:r0 + 128, :], [Bd], [w], w)
                        for i, (c0, W) in enumerate(subs):
                            py = S.psum()
                            for fc in range(FC):
                                S.op("pe", lambda e: e.matmul(py.t[:, :W], w.t[:, fc, :], hid[i].t[:, fc, :W], start=(fc == 0), stop=(fc == FC - 1)),
                                     [w, hid[i]], [py], sig=(fc == FC - 1))
                            S.op("dve", lambda e: e.scalar_tensor_tensor(out=h[i].t[:, dc, :W], in0=py.t[:, :W], scalar=0.5, in1=h[i].t[:, dc, :W],
                                                                          op0=ALU.mult, op1=ALU.add), [py, h[i]], [h[i]])

                def wout(l, subs):
                    for i, (c0, W) in enumerate(subs):
                        S.dma("sp", o1.t[:, :, :W], oT_v[:, :, c0:c0 + W], [BoT[c0]], [o1], o1)
                        for dc in range(8):
                            w = woring.next()
                            r0 = (l * 8 + dc) * 128
                            S.dma("sp", w.t[:].rearrange("p c f -> p (c f)"), wo_b[r0:r0 + 128, :], [Bw["misc"]], [w], w)
                            py = S.psum()
                            for fc in range(8):
                                S.op("pe", lambda e: e.matmul(py.t[:, :W], w.t[:, fc, :], o1.t[:, fc, :W], start=(fc == 0), stop=(fc == 7)),
                                     [w, o1], [py], sig=(fc == 7))
                            S.op("dve", lambda e: e.tensor_tensor(h[i].t[:, dc, :W], py.t[:, :W], h[i].t[:, dc, :W], ALU.add), [py, h[i]], [h[i]])

                def proj_in(l, subs):
                    gb = (l * 3 + 1) * 8
                    for i, (c0, W) in enumerate(subs):
                        rs = rms_rstd([(h[i], h[i].t[:, c, :W]) for c in range(8)], W, D, sqring, rsring)
                        for c in range(8):
                            S.op("dve", lambda e: e.scalar_tensor_tensor(out=xn[i].t[:, c, :W], in0=h[i].t[:, c, :W],
                                                                          scalar=gcols.t[:, gb + c:gb + c + 1], in1=rs.t[:, :W],
                                                                          op0=ALU.mult, op1=ALU.mult), [h[i], rs, gcols], [xn[i]])
                        for gq in range(4):
                            S.dma("pool", rda[i].t[gq * 32:(gq + 1) * 32, :, :W],
                                  ropeda_d.rearrange("(a r) t -> r a t", a=2)[:, :, c0:c0 + W], [Bconst], [rda[i]], rda[i])
                        for base in (0, 64):
                            S.dma("pool", rml[i].t[base:base + 32, :, :W],
                                  ropeml_d.rearrange("(a r) t -> r a t", a=2)[:, :, c0:c0 + W], [Bconst], [rml[i]], rml[i])
                    Bi = Bw["in%d" % l]
                    S.dma("sp", wvt.t[:].rearrange("p c f -> p (c f)"), wv_b[l * 128:(l + 1) * 128, :], [Bw["misc"]], [wvt], wvt)
                    S.dma("sp", wuqt.t[:], wuq_b[l * 1024:(l + 1) * 1024, :].rearrange("(j p) (c f) -> p j c f", p=128, c=2),
                          [Bw["misc"]], [wuqt], wuqt)
                    S.dma("sp", wukvt.t[:], wukv_b[l * 128:(l + 1) * 128, :], [Bw["misc"]], [wukvt], wukvt)

                    def load_chunk(j):
                        w = winring.next()
                        r0 = (l * NCH_IN + j) * 128
                        S.dma("sp", w.t[:].rearrange("p c f -> p (c f)"), win_b[r0:r0 + 128, :], [Bi], [w], w)
                        return w

                    def mm_chunk(w, i, W, M):
                        p = S.psum()
                        for c in range(8):
                            S.op("pe", lambda e: e.matmul(p.t[:M, :W], w.t[:, c, :M], xn[i].t[:, c, :W], start=(c == 0), stop=(c == 7)),
                                 [w, xn[i]], [p], sig=(c == 7))
                        return p

                    def store_rows(dst, row0, M, uo, c0, W):
                        S.dma("pool", dst[row0:row0 + M, c0:c0 + W], uo.t[:M, :W], [uo], [Bqk], uo)

                    for j in range(6):
                        w = load_chunk(j)
                        for i, (c0, W) in enumerate(subs):
                            p = mm_chunk(w, i, W, 128)
                            uo = uoring.next()
                            evac(uo.t[:, :W], p.t[:, :W], [p], [uo])
                            store_rows(naq_d if j < 3 else nak_d, (j % 3) * 128, 128, uo, c0, W)
                    for grp, dst in ((6, daq_d), (12, dak_d)):
                        for jj in range(3):
                            wm = load_chunk(grp + jj)
                            wp = load_chunk(grp + 3 + jj)
                            for i, (c0, W) in enumerate(subs):
                                pm = mm_chunk(wm, i, W, 128)
                                pp = mm_chunk(wp, i, W, 128)
                                t1 = t1ring.next()
                                t2 = t2ring.next()
                                S.op("dve", lambda e: e.tensor_tensor(t1.t[:, :W], pm.t[:, :W], rda[i].t[:, 0, :W], ALU.mult), [pm, rda[i]], [t1])
                                S.op("dve", lambda e: e.tensor_tensor(t2.t[:, :W], pp.t[:, :W], rda[i].t[:, 1, :W], ALU.mult), [pp, rda[i]], [t2])
                                uo = uoring.next()
                                S.op("pool", lambda e: e.tensor_tensor(uo.t[:, :W], t1.t[:, :W], t2.t[:, :W], ALU.add), [t1, t2], [uo])
                                store_rows(dst, jj * 128, 128, uo, c0, W)
                    wm = load_chunk(21)
                    wp = load_chunk(22)
                    for i, (c0, W) in enumerate(subs):
                        pm = mm_chunk(wm, i, W, 32)
                        pp = mm_chunk(wp, i, W, 32)
                        t1 = t1ring.next()
                        t2 = t2ring.next()
                        S.op("dve", lambda e: e.tensor_tensor(t1.t[:32, :W], pm.t[:32, :W], rml[i].t[0:32, 0, :W], ALU.mult), [pm, rml[i]], [t1])
                        S.op("dve", lambda e: e.tensor_tensor(t2.t[:32, :W], pp.t[:32, :W], rml[i].t[0:32, 1, :W], ALU.mult), [pp, rml[i]], [t2])
                        uo = uoring.next()
                        S.op("pool", lambda e: e.tensor_tensor(uo.t[:32, :W], t1.t[:32, :W], t2.t[:32, :W], ALU.add), [t1, t2], [uo])
                        store_rows(mlr_d, 0, 32, uo, c0, W)
                    for i, (c0, W) in enumerate(subs):
                        for jj in range(3):
                            w = load_chunk(18 + jj)
                            p = mm_chunk(w, i, W, 128)
                            evac(cq[i].t[:, jj, :W], p.t[:, :W], [p], [cq[i]])
                        rs = rms_rstd([(cq[i], cq[i].t[:, c, :W]) for c in range(2)], W, 256, sqring, rsring)
                        for c in range(2):
                            S.op("dve", lambda e: e.scalar_tensor_tensor(out=cqn[i].t[:, c, :W], in0=cq[i].t[:, c, :W],
                                                                          scalar=gcols.t[:, 56 + 2 * l + c:57 + 2 * l + c], in1=rs.t[:, :W],
                                                                          op0=ALU.mult, op1=ALU.mult), [cq[i], rs, gcols], [cqn[i]])
                        rs = rms_rstd([(cq[i], cq[i].t[:, 2, :W])], W, 128, sqring, rsring)
                        S.op("dve", lambda e: e.scalar_tensor_tensor(out=cqn[i].t[:, 2, :W], in0=cq[i].t[:, 2, :W],
                                                                      scalar=gcols.t[:, 60 + l:61 + l], in1=rs.t[:, :W],
                                                                      op0=ALU.mult, op1=ALU.mult), [cq[i], rs, gcols], [cqn[i]])
                        for hh in range(4):
                            pm = S.psum()
                            pp = S.psum()
                            for c in range(2):
                                S.op("pe", lambda e: e.matmul(pm.t[:96, :W], wuqt.t[:, 2 * hh, c, 0:96], cqn[i].t[:, c, :W], start=(c == 0), stop=(c == 1)),
                                     [wuqt, cqn[i]], [pm], sig=(c == 1))
                            for c in range(2):
                                S.op("pe", lambda e: e.matmul(pp.t[:96, :W], wuqt.t[:, 2 * hh + 1, c, 0:96], cqn[i].t[:, c, :W], start=(c == 0), stop=(c == 1)),
                                     [wuqt, cqn[i]], [pp], sig=(c == 1))
                            uo = uoring.next()
                            t1 = t1ring.next()
                            t2 = t2ring.next()
                            S.op("act", lambda e: e.activation(out=uo.t[0:64, :W], in_=pm.t[0:64, :W], func=AF.Copy), [pm], [uo])
                            S.op("dve", lambda e: e.tensor_tensor(t1.t[64:96, :W], pm.t[64:96, :W], rml[i].t[64:96, 0, :W], ALU.mult), [pm, rml[i]], [t1])
                            S.op("dve", lambda e: e.tensor_tensor(t2.t[64:96, :W], pp.t[64:96, :W], rml[i].t[64:96, 1, :W], ALU.mult), [pp, rml[i]], [t2])
                            S.op("pool", lambda e: e.tensor_tensor(uo.t[64:96, :W], t1.t[64:96, :W], t2.t[64:96, :W], ALU.add), [t1, t2, uo], [uo])
                            store_rows(mlq_d, hh * 96, 96, uo, c0, W)
                        for kk in range(2):
                            p = S.psum()
                            S.op("pe", lambda e: e.matmul(p.t[:, :W], wukvt.t[:, kk * 128:(kk + 1) * 128], cqn[i].t[:, 2, :W], start=True, stop=True),
                                 [wukvt, cqn[i]], [p])
                            uo = uoring.next()
                            evac(uo.t[:, :W], p.t[:, :W], [p], [uo])
                            store_rows(mlk_d, kk * 128, 128, uo, c0, W)
                        ng = (W + 127) // 128
                        for g in range(ng):
                            gw = min(128, W - g * 128)
                            vt = vtring.next()
                            for half in range(2):
                                p = S.psum()
                                for c in range(8):
                                    S.op("pe", lambda e: e.matmul(p.t[:gw, :384], xn[i].t[:, c, g * 128:g * 128 + gw], wvt.t[:, c, half * 384:(half + 1) * 384],
                                                                   start=(c == 0), stop=(c == 7)), [xn[i], wvt], [p], sig=(c == 7))
                                evac(vt.t[:gw, half * 384:(half + 1) * 384], p.t[:gw, :384], [p], [vt])
                            p = S.psum()
                            S.op("pe", lambda e: e.matmul(p.t[:gw, :256], cqn[i].t[:, 2, g * 128:g * 128 + gw], wukvt.t[:, 256:512], start=True, stop=True),
                                 [cqn[i], wukvt], [p])
                            evac(vt.t[:gw, 768:1024], p.t[:gw, :256], [p], [vt])
                            S.dma("pool", v_d[c0 + g * 128:c0 + g * 128 + gw, :], vt.t[:gw, :], [vt], [Bv], vt)

                for bi, subs in enumerate(blocks):
                    is_meta = subs[0][0] >= R
                    for i, (c0, W) in enumerate(subs):
                        key = c0
                        if key not in BhT:
                            BhT[key] = Buf("hT%d" % key, dram=True)
                            BoT[key] = Buf("oT%d" % key, dram=True)
                        if stage == 0:
                            xs = xsring.next()
                            if is_meta:
                                for s in range(3):
                                    S.dma("sp", xs.t[s * NMETA:(s + 1) * NMETA, 0, :], meta_d[:, :], [Bconst], [xs], xs)
                            else:
                                S.dma("sp", xs.t[:, :, :], x_d[c0:c0 + W, :].rearrange("(g p) d -> p g d", p=128), [Bx], [xs], xs)
                            ng = (W + 127) // 128
                            for c in range(8):
                                p = S.psum()
                                for g in range(ng):
                                    gw = min(128, W - g * 128)
                                    S.op("pe", lambda e: e.transpose(p.t[:, g * 128:g * 128 + gw], xs.t[:gw, g, c * 128:(c + 1) * 128], ident.t[:gw, :gw]),
                                         [xs, ident], [p], sig=(g == ng - 1))
                                evac(h[i].t[:, c, :W], p.t[:, :W], [p], [h[i]])
                        else:
                            S.dma("sp", h[i].t[:, :, :W], hT_v[:, :, c0:c0 + W], [BhT[key]], [h[i]], h[i])
                    if stage >= 1:
                        wout(stage - 1, subs)
                        ffn((stage - 1) * 2 + 1, subs, ((stage - 1) * 3 + 2) * 8)
                    if stage <= 1:
                        if SUB >= 1.2:
                            ffn(stage * 2, subs, (stage * 3) * 8)
                        if SUB >= 3:
                            proj_in(stage, subs)
                        for i, (c0, W) in enumerate(subs):
                            S.dma("pool", hT_v[:, :, c0:c0 + W], h[i].t[:, :, :W], [h[i]], [BhT[c0]], h[i])
                    else:
                        for i, (c0, W) in enumerate(subs):
                            rs = rms_rstd([(h[i], h[i].t[:, c, :W]) for c in range(8)], W, D, sqring, rsring)
                            for c in range(8):
                                S.op("dve", lambda e: e.scalar_tensor_tensor(out=hn[i].t[:, c, :W], in0=h[i].t[:, c, :W],
                                                                              scalar=gcols.t[:, 48 + c:49 + c], in1=rs.t[:, :W],
                                                                              op0=ALU.mult, op1=ALU.mult), [h[i], rs, gcols], [hn[i]])
                            for g in range(W // 128):
                                yt = yring.next()
                                for half in range(2):
                                    p = S.psum()
                                    for cc in range(4):
                                        c = half * 4 + cc
                                        S.op("pe", lambda e: e.transpose(p.t[:, cc * 128:(cc + 1) * 128], hn[i].t[:, c, g * 128:(g + 1) * 128], ident.t[:, :]),
                                             [hn[i], ident], [p], sig=(cc == 3))
                                    evac(yt.t[:, half * 512:(half + 1) * 512], p.t[:, :], [p], [yt])
                                S.dma("pool", y_d[c0 + g * 128:c0 + (g + 1) * 128, :], yt.t[:, :], [yt], [By], yt)
                S.barrier()
                S.release(phase_bufs)
                del phase_bufs[:]

        def att_phase(l, last):
            with ExitStack() as ph:
                NT = cfg.NMAX // 128
                ktring = Ring([sb(ph, "kt%d" % i, [96, cfg.NMAX], BF16) for i in range(2)])
                kmring = Ring([sb(ph, "km%d" % i, [96, NMETA], BF16) for i in range(2)])
                vring = Ring([sb(ph, "vv%d" % i, [128, NT, 65], BF16) for i in range(2)])
                vmring = Ring([sb(ph, "vm%d" % i, [NMETA, 65], BF16) for i in range(2)])
                qring = Ring([sb(ph, "qq%d" % i, [96, 512], BF16) for i in range(3)])
                ptring = Ring([sb(ph, "pt%d" % i, [128, 512], BF16) for i in range(6)])
                tmring = Ring([sb(ph, "tm%d" % i, [128, 512], F32) for i in range(3)])
                mb = sb(ph, "mb", [128, 3, 8, 512], F32)
                osring = Ring([sb(ph, "os%d" % i, [65, 512], F32) for i in range(4)])
                rdring = Ring([sb(ph, "rd%d" % i, [64, 512], F32) for i in range(4)])
                aring = Ring([sb(ph, "aa%d" % i, [64, 512], F32) for i in range(4)])
                ooring = Ring([sb(ph, "oo%d" % i, [64, 512], BF16) for i in range(3)])
                spool = Ring(S.ps[0:4])
                opool = Ring(S.ps[4:8])
                for b in vring.bufs:
                    S.op("pool", lambda e: e.memset(b.t[:, :, 64:65], 1.0), [], [b])
                for b in vmring.bufs:
                    S.op("pool", lambda e: e.memset(b.t[:, 64:65], 1.0), [], [b])

                def run_qtile(streams, Q, Nq, ktl, scale):
                    O = [opool.next() for _ in streams]
                    n = len(ktl)

                    def qk(i):
                        res = []
                        Kb, kfn, Vb, vap, nk, mbap = ktl[i]
                        for (r0, r1) in streams:
                            ps = spool.next()
                            S.op("pe", lambda e: e.matmul(ps.t[:nk, :Nq], kfn(r0, r1), Q.t[r0:r1, :Nq], start=True, stop=True), [Kb, Q], [ps])
                            pt = ptring.next()
                            if mbap is not None:
                                tm = tmring.next()
                                S.op("dve", lambda e: e.scalar_tensor_tensor(out=tm.t[:nk, :Nq], in0=ps.t[:nk, :Nq], scalar=scale, in1=mbap[:nk, :Nq],
                                                                              op0=ALU.mult, op1=ALU.add), [ps, mb], [tm])
                                S.op("act", lambda e: e.activation(out=pt.t[:nk, :Nq], in_=tm.t[:nk, :Nq], func=AF.Exp), [tm], [pt])
                            else:
                                S.op("act", lambda e: e.activation(out=pt.t[:nk, :Nq], in_=ps.t[:nk, :Nq], func=AF.Exp, scale=scale), [ps], [pt])
                            res.append(pt)
                        return res

                    cur = qk(0)
                    for i in range(n):
                        nxt = qk(i + 1) if i + 1 < n else None
                        Kb, kfn, Vb, vap, nk, mbap = ktl[i]
                        for si in range(len(streams)):
                            S.op("pe", lambda e: e.matmul(O[si].t[:65, :Nq], vap, cur[si].t[:nk, :Nq], start=(i == 0), stop=(i == n - 1)),
                                 [Vb, cur[si]], [O[si]], sig=True)
                        cur = nxt
                    return O

                def normalize(Ob, Nq):
                    osb = osring.next()
                    evac(osb.t[:65, :Nq], Ob.t[:65, :Nq], [Ob], [osb])
                    dps = spool.next()
                    S.op("pe", lambda e: e.matmul(dps.t[:64, :Nq], sel65.t[:65, :64], osb.t[:65, :Nq], start=True, stop=True), [sel65, osb], [dps])
                    rd = rdring.next()
                    S.op("dve", lambda e: e.reciprocal(rd.t[:64, :Nq], dps.t[:64, :Nq]), [dps], [rd])
                    return osb, rd

                heads = [("na", hh) for hh in range(6)] + [("da", hh) for hh in range(6)] + [("ml", hh) for hh in range(4)]
                for kind, hh in heads:
                    if kind == "na":
                        scale = 64 ** -0.5
                        S.op("pool", lambda e: e.memset(mb.t[:].rearrange("p a b c -> p (a b c)"), NEG), [], [mb])
                        for v in range(3):
                            delta = (0, 4, 8)[v]
                            for kr in range(16):
                                qs = []
                                for qr in range(8):
                                    if v == 0:
                                        lo = max(qr - 4, 0)
                                    elif v == 1:
                                        lo = qr
                                    else:
                                        lo = 8 + min(qr - 4, 0)
                                    if lo <= kr < lo + 8:
                                        qs.append(qr)
                                if not qs:
                                    continue
                                qa, qb = qs[0], qs[-1] + 1
                                dra = 7 - kr + qa + delta
                                row0 = ((l * 6 + hh) * 15 + dra) * 64
                                src = rbx_d[row0:row0 + (qb - qa) * 64, :].rearrange("(q k) c -> k q c", k=64)
                                krb = kr % 2
                                S.dma("sp", mb.t[krb * 64:(krb + 1) * 64, v, kr // 2, qa * 64:qb * 64].rearrange("p (q c) -> p q c", c=64),
                                      src, [Bconst], [mb], mb)
                        qsrc, ksrc, row0, nrows, voff = naq_d, nak_d, hh * 64, 64, hh * 64
                        streams = [(0, 64)]
                    elif kind == "da":
                        scale = 32 ** -0.5
                        qsrc, ksrc, row0, nrows, voff = daq_d, dak_d, hh * 64, 64, 384 + hh * 64
                        streams = [(0, 32), (32, 64)]
                    else:
                        scale = 96 ** -0.5
                        qsrc, ksrc, row0, nrows, voff = mlq_d, mlk_d, hh * 96, 96, 768 + hh * 64
                        streams = [(0, 96)]
                    orow = voff
                    for s in range(3):
                        n = cfg.seqn[s]
                        st = cfg.start[s]
                        mcol = R + NMETA * s
                        nt = n // 128
                        Kt = ktring.next()
                        Km = kmring.next()
                        Vv = vring.next()
                        Vm = vmring.next()
                        if kind == "ml":
                            S.dma("sp", Kt.t[0:64, :n], mlk_d[hh * 64:(hh + 1) * 64, st:st + n], [Bqk], [Kt], Kt)
                            S.dma("sp", Kt.t[64:96, :n], mlr_d[0:32, st:st + n], [Bqk], [Kt], Kt)
                            S.dma("sp", Km.t[0:64, :], mlk_d[hh * 64:(hh + 1) * 64, mcol:mcol + NMETA], [Bqk], [Km], Km)
                            S.dma("sp", Km.t[64:96, :], mlr_d[0:32, mcol:mcol + NMETA], [Bqk], [Km], Km)
                        else:
                            S.dma("sp", Kt.t[0:64, :n], ksrc[row0:row0 + 64, st:st + n], [Bqk], [Kt], Kt)
                            S.dma("sp", Km.t[0:64, :], ksrc[row0:row0 + 64, mcol:mcol + NMETA], [Bqk], [Km], Km)
                        for t0 in range(0, nt, 16):
                            t1_ = min(nt, t0 + 16)
                            S.dma("sp", Vv.t[:, t0:t1_, 0:64],
                                  v_d[st + t0 * 128:st + t1_ * 128, voff:voff + 64].rearrange("(t p) e -> p t e", p=128), [Bv], [Vv], Vv)
                        S.dma("sp", Vm.t[:, 0:64], v_d[mcol:mcol + NMETA, voff:voff + 64], [Bv], [Vm], Vm)

                        qtiles = [(st + q0, 512, q0 // 512) for q0 in range(0, n, 512)]
                        if not last:
                            qtiles.append((mcol, NMETA, -1))
                        nblk = n // 512
                        rows = n // 64
                        for (qc0, Nq, qb_) in qtiles:
                            Q = qring.next()
                            S.dma("sp", Q.t[:nrows, :Nq], qsrc[row0:row0 + nrows, qc0:qc0 + Nq], [Bqk], [Q], Q)
                            meta_kt = (Km, (lambda r0, r1, Km=Km: Km.t[r0:r1, 0:NMETA]), Vm, Vm.t[:NMETA, 0:65], NMETA, None)
                            if kind == "na":
                                if qb_ < 0:
                                    ktl = [meta_kt]
                                else:
                                    v = 0 if qb_ == 0 else (2 if qb_ == nblk - 1 else 1)
                                    w0 = min(max(8 * qb_ - 4, 0), rows - 16)
                                    kt0 = w0 // 2
                                    ktl = []
                                    for j in range(8):
                                        kt = kt0 + j
                                        ktl.append((Kt, (lambda r0, r1, kt=kt, Kt=Kt: Kt.t[r0:r1, kt * 128:(kt + 1) * 128]), Vv, Vv.t[:, kt, 0:65], 128,
                                                    mb.t[:, v, j, :]))
                                    ktl.append(meta_kt)
                            else:
                                ktl = [(Kt, (lambda r0, r1, kt=kt, Kt=Kt: Kt.t[r0:r1, kt * 128:(kt + 1) * 128]), Vv, Vv.t[:, kt, 0:65], 128, None)
                                       for kt in range(nt)]
                                ktl.append(meta_kt)
                            O = run_qtile(streams, Q, Nq, ktl, scale)
                            oo = ooring.next()
                            if kind == "da":
                                os0, rd0 = normalize(O[0], Nq)
                                os1, rd1 = normalize(O[1], Nq)
                                a = aring.next()
                                b = aring.next()
                                S.op("dve", lambda e: e.tensor_tensor(a.t[:64, :Nq], os0.t[:64, :Nq], rd0.t[:64, :Nq], ALU.mult), [os0, rd0], [a])
                                S.op("pool", lambda e: e.tensor_tensor(b.t[:64, :Nq], os1.t[:64, :Nq], rd1.t[:64, :Nq], ALU.mult), [os1, rd1], [b])
                                S.op("dve", lambda e: e.scalar_tensor_tensor(out=a.t[:64, :Nq], in0=b.t[:64, :Nq], scalar=neglam.t[:64, l:l + 1], in1=a.t[:64, :Nq],
                                                                              op0=ALU.mult, op1=ALU.add), [a, b, neglam], [a])
                                sq = aring.next()
                                S.op("act", lambda e: e.activation(out=sq.t[:64, :Nq], in_=a.t[:64, :Nq], func=AF.Square), [a], [sq])
                                sp_ = spool.next()
                                S.op("pe", lambda e: e.matmul(sp_.t[:64, :Nq], ones64.t[:64, :64], sq.t[:64, :Nq], start=True, stop=True), [ones64, sq], [sp_])
                                rs = rdring.next()
                                S.op("act", lambda e: e.activation(out=rs.t[:64, :Nq], in_=sp_.t[:64, :Nq], func=AF.Sqrt, bias=epscol.t[:64, 0:1], scale=1.0 / 64), [sp_, epscol], [rs])
                                S.op("dve", lambda e: e.reciprocal(rs.t[:64, :Nq], rs.t[:64, :Nq]), [rs], [rs])
                                S.op("dve", lambda e: e.scalar_tensor_tensor(out=oo.t[:64, :Nq], in0=a.t[:64, :Nq], scalar=gsub.t[:64, l:l + 1], in1=rs.t[:64, :Nq],
                                                                              op0=ALU.mult, op1=ALU.mult), [a, rs, gsub], [oo])
                            else:
                                os0, rd0 = normalize(O[0], Nq)
                                S.op("dve", lambda e: e.tensor_tensor(oo.t[:64, :Nq], os0.t[:64, :Nq], rd0.t[:64, :Nq], ALU.mult), [os0, rd0], [oo])
                            key = qc0 if qc0 < R else R
                            S.dma("pool", oT_d[orow:orow + 64, qc0:qc0 + Nq], oo.t[:64, :Nq], [oo], [BoT[key]], oo)
                S.barrier()
                S.release(phase_bufs)
                del phase_bufs[:]

        if STOP >= 1:
            tok_phase(0)
        if STOP >= 2:
            att_phase(0, False)
        if STOP >= 3:
            tok_phase(1)
        if STOP >= 4:
            att_phase(1, True)
        if STOP >= 5:
            tok_phase(2)
        S.barrier()
    return nc


_CACHE = {}


def run(cfg, inp):
    sh = _host_shared(cfg, inp)
    if "nc" not in _CACHE or _CACHE.get("cfg") != (cfg.NP, cfg.NS, cfg.DFF):
        _CACHE["nc"] = build_program(cfg)
        _CACHE["cfg"] = (cfg.NP, cfg.NS, cfg.DFF)
    nc = _CACHE["nc"]
    xp, xs = inp["x_prompt"], inp["x_sample"]
    in_maps = []
    for c in range(NCORES):
        m = dict(sh)
        m["xtok"] = np.ascontiguousarray(np.concatenate([xp[c], xs[2 * c], xs[2 * c + 1]], axis=0).astype(np.float32))
        in_maps.append(m)
    res = run_bass_kernel_spmd(nc, in_maps, core_ids=list(range(NCORES)))
    _CACHE["res"] = res
    yp = np.empty(xp.shape, np.float32)
    ys = np.empty(xs.shape, np.float32)
    for c in range(NCORES):
        y = res.results[c]["y"]
        yp[c] = y[:cfg.NP]
        ys[2 * c] = y[cfg.NP:cfg.NP + cfg.NS]
        ys[2 * c + 1] = y[cfg.NP + cfg.NS:]
    return yp, ys


def kernel(**inputs):
    inp = {k: np.asarray(v) for k, v in inputs.items()}
    cfg = Cfg(inp["x_prompt"].shape[1], inp["x_sample"].shape[1], inp["ffn_w_gate"].shape[-1])
    return run(cfg, inp)
```

```python
import math
from contextlib import ExitStack
import numpy as np
import concourse.bass as bass
import concourse.mybir as mybir
from concourse.bass_utils import run_bass_kernel_spmd

F32 = mybir.dt.float32
BF16 = mybir.dt.bfloat16
AF = mybir.ActivationFunctionType
ALU = mybir.AluOpType

D = 1024
NMETA = 16
EPS = 1e-6
NEG = -30000.0
NCORES = 8
ROPE_THETA = 500000.0
DEBUG = False
STOP = 5
SUB = 9


class Buf:
    def __init__(self, name, t=None, dram=False):
        self.name = name
        self.t = t
        self.dram = dram
        self.w = {}
        self.rd = {}
        self.sem = None
        self.cnt = 0

    def add_rd(self, tok):
        sid = id(tok[0])
        if self.rd.get(sid, (None, 0))[1] < tok[1]:
            self.rd[sid] = tok

    def set_w(self, tok):
        sid = id(tok[0])
        if self.w.get(sid, (None, 0))[1] < tok[1]:
            self.w[sid] = tok


class Ring:
    def __init__(self, bufs):
        self.bufs = bufs
        self.i = 0

    def next(self):
        b = self.bufs[self.i % len(self.bufs)]
        self.i += 1
        return b


class Sched:
    def __init__(self, nc, stack):
        self.nc = nc
        self.stack = stack
        self.E = {"pe": nc.tensor, "act": nc.scalar, "dve": nc.vector, "pool": nc.gpsimd, "sp": nc.sync}
        self.esem = {}
        for e in ("pe", "act", "dve", "pool"):
            self.esem[e] = stack.enter_context(nc.semaphore("es_" + e))
        self.cnt = {e: 0 for e in self.E}
        self.seen = {e: {} for e in self.E}
        self.semcnt = {}
        self.free_sems = []
        self.nsem = 4
        self.ps = None
        self.psi = 0

    def _waits(self, eng, reads, writes):
        toks = {}
        own = self.esem.get(eng)

        def add(d, raw):
            for sid, (sem, val) in d.items():
                if sem is own and (eng == "pe" or (not raw and eng != "pool")):
                    continue
                if toks.get(sid, (None, 0))[1] < val:
                    toks[sid] = (sem, val)

        for b in reads:
            add(b.w, True)
        for b in writes:
            add(b.w, False)
            add(b.rd, False)
        e = self.E[eng]
        seen = self.seen[eng]
        for sid, (sem, val) in toks.items():
            if seen.get(sid, 0) >= val:
                continue
            seen[sid] = val
            e.wait_ge(sem, val)

    def op(self, eng, fn, reads=(), writes=(), sig=True):
        self._waits(eng, reads, writes)
        ins = fn(self.E[eng])
        if sig:
            self.cnt[eng] += 1
            ins.then_inc(self.esem[eng], 1)
            idx = self.cnt[eng]
        else:
            idx = self.cnt[eng] + 1
        tok = (self.esem[eng], idx)
        for b in reads:
            b.add_rd(tok)
        for b in writes:
            b.set_w(tok)

    def dma(self, eng, out, in_, reads, writes, sb):
        self._waits(eng, reads, writes)
        if sb.sem is None:
            if self.free_sems:
                sb.sem, sb.cnt = self.free_sems.pop()
            else:
                sb.sem = self.stack.enter_context(self.nc.semaphore("ds_%d" % self.nsem))
                self.nsem += 1
        self.E[eng].dma_start(out=out, in_=in_).then_inc(sb.sem, 16)
        sb.cnt += 16
        self.semcnt[id(sb.sem)] = (sb.sem, sb.cnt)
        tok = (sb.sem, sb.cnt)
        for b in reads:
            b.add_rd(tok)
        for b in writes:
            b.set_w(tok)

    def barrier(self):
        for eng, e in self.E.items():
            seen = self.seen[eng]
            for o, sem in self.esem.items():
                v = self.cnt[o]
                if v > 0 and seen.get(id(sem), 0) < v and o != eng:
                    seen[id(sem)] = v
                    e.wait_ge(sem, v)
            for sid, (sem, v) in self.semcnt.items():
                if seen.get(sid, 0) < v:
                    seen[sid] = v
                    e.wait_ge(sem, v)

    def release(self, bufs):
        for b in bufs:
            if b.sem is not None:
                self.free_sems.append((b.sem, b.cnt))
                b.sem = None

    def psum(self):
        b = self.ps[self.psi % 8]
        self.psi += 1
        return b


class Cfg:
    def __init__(self, NP, NS, DFF):
        self.NP, self.NS, self.DFF = NP, NS, DFF
        self.FC = DFF // 128
        self.seqn = [NP, NS, NS]
        self.start = [0, NP, NP + NS]
        self.R = NP + 2 * NS
        self.TT = self.R + 3 * NMETA
        self.NMAX = max(self.seqn)


NA_Q, NA_K, NA_V, DA_Q, DA_K, DA_V, M_CQ, M_CKV, M_KR = 0, 384, 768, 1152, 1536, 1920, 2304, 2560, 2688
NCH_IN = 23


def _win_cols():
    idx = -np.ones((NCH_IN, 128), np.int64)
    r = np.arange(128)
    for i in range(3):
        idx[i] = NA_Q + 128 * i + r
        idx[3 + i] = NA_K + 128 * i + r
        dd = r % 32
        pr = np.where(dd < 4, r + 4, np.where(dd < 8, r - 4, r))
        idx[6 + i] = DA_Q + 128 * i + r
        idx[9 + i] = DA_Q + 128 * i + pr
        idx[12 + i] = DA_K + 128 * i + r
        idx[15 + i] = DA_K + 128 * i + pr
    idx[18] = M_CQ + r
    idx[19] = M_CQ + 128 + r
    idx[20] = M_CKV + r
    idx[21, :32] = M_KR + np.arange(32)
    idx[22, :16] = M_KR + np.arange(16) + 16
    idx[22, 16:32] = M_KR + np.arange(16)
    return idx.reshape(-1)


def _gather_cols(w, idx):
    out = w[:, np.maximum(idx, 0)]
    out = np.where(idx[None, :] >= 0, out, np.float32(0.0))
    return np.ascontiguousarray(out.astype(np.float32))


def _host_shared(cfg, inp):
    FC = cfg.FC
    sh = {}
    g, u, dn = inp["ffn_w_gate"], inp["ffn_w_up"], inp["ffn_w_down"]
    wgu = np.empty((4, 2, FC, 128, 8 * 128), np.float32)
    wd = np.empty((4, 8, 128, FC * 128), np.float32)
    for l in range(2):
        for j in range(2):
            lj = l * 2 + j
            wgu[lj, 0] = g[l, j].reshape(8, 128, FC, 128).transpose(2, 1, 0, 3).reshape(FC, 128, 1024)
            wgu[lj, 1] = u[l, j].reshape(8, 128, FC, 128).transpose(2, 1, 0, 3).reshape(FC, 128, 1024)
            wd[lj] = dn[l, j].reshape(FC, 128, 8, 128).transpose(2, 1, 0, 3).reshape(8, 128, FC * 128)
    sh["wgu"] = wgu.reshape(4 * 2 * FC * 128, 1024)
    sh["wd"] = wd.reshape(4 * 8 * 128, FC * 128)
    idx = _win_cols()
    win = np.empty((2, NCH_IN, 128, 1024), np.float32)
    wv = np.empty((2, 128, 8 * 768), np.float32)
    wuq = np.empty((2, 8, 128, 256), np.float32)
    wukv = np.empty((2, 128, 512), np.float32)
    wo = np.empty((2, 8, 128, 1024), np.float32)
    f = np.arange(128)
    for l in range(2):
        w = inp["w_in"][l]
        we = _gather_cols(w, idx)
        win[l] = we.reshape(8, 128, NCH_IN, 128).transpose(2, 1, 0, 3).reshape(NCH_IN, 128, 1024)
        vcols = np.concatenate([NA_V + np.arange(384), DA_V + np.arange(384)])
        wv[l] = w[:, vcols].reshape(8, 128, 768).transpose(1, 0, 2).reshape(128, 8 * 768)
        uq = inp["mla_w_uq"][l]
        for h in range(4):
            im = np.where(f < 96, h * 96 + f, -1)
            ip = np.where((f >= 64) & (f < 80), h * 96 + f + 16, np.where((f >= 80) & (f < 96), h * 96 + f - 16, -1))
            for k, ii in ((0, im), (1, ip)):
                m = _gather_cols(uq, ii)
                wuq[l, h * 2 + k] = m.reshape(2, 128, 128).transpose(1, 0, 2).reshape(128, 256)
        ukv = inp["mla_w_ukv"][l]
        kc = np.concatenate([h * 128 + np.arange(64) for h in range(4)])
        vc = np.concatenate([h * 128 + 64 + np.arange(64) for h in range(4)])
        wukv[l] = np.concatenate([ukv[:, kc], ukv[:, vc]], axis=1)
        wo[l] = inp["w_out"][l].reshape(8, 128, 8, 128).transpose(2, 1, 0, 3).reshape(8, 128, 1024)
    sh["win"] = win.reshape(2 * NCH_IN * 128, 1024)
    sh["wv"] = wv.reshape(2 * 128, 8 * 768)
    sh["wuq"] = wuq.reshape(2 * 8 * 128, 256)
    sh["wukv"] = wukv.reshape(2 * 128, 512)
    sh["wo"] = wo.reshape(2 * 8 * 128, 1024)
    cols = []
    for l in range(2):
        for i in range(3):
            cols.append(inp["norm_g"][l, i].reshape(8, 128).T)
    cols.append(inp["final_norm_g"].reshape(8, 128).T)
    for l in range(2):
        cols.append(inp["mla_q_norm_g"][l].reshape(2, 128).T)
    for l in range(2):
        cols.append(inp["mla_kv_norm_g"][l].reshape(1, 128).T)
    for l in range(2):
        c = np.zeros((128, 1), np.float32)
        c[:64, 0] = inp["da_subln_g"][l]
        cols.append(c)
    sh["gcols"] = np.ascontiguousarray(np.concatenate(cols, axis=1).astype(np.float32))
    lam = np.empty((2, 128), np.float32)
    for l in range(2):
        lp = inp["da_lambda"][l]
        lam[l] = np.concatenate([lp[0], lp[2], lp[1], lp[3]])
    sh["dalam"] = lam
    kc = np.arange(64)[:, None]
    qc = np.arange(64)[None, :]
    cs = np.clip(qc - 8, 0, 48)
    valid = (kc >= cs) & (kc < cs + 16)
    ci = np.clip(kc - qc + 15, 0, 30)
    rb = inp["na_rel_bias"]
    rbx = rb[:, :, ::-1, :][:, :, :, ci]
    rbx = np.where(valid[None, None, None], rbx, np.float32(NEG)).astype(np.float32)
    sh["rbx"] = np.ascontiguousarray(rbx.reshape(2 * 6 * 15 * 64, 64))
    pos = np.empty(cfg.TT, np.float32)
    for s in range(3):
        pos[cfg.start[s]:cfg.start[s] + cfg.seqn[s]] = NMETA + np.arange(cfg.seqn[s], dtype=np.float32)
        pos[cfg.R + NMETA * s:cfg.R + NMETA * (s + 1)] = np.arange(NMETA, dtype=np.float32)

    def tables(dim):
        inv = (np.float32(ROPE_THETA) ** (-(np.arange(0, dim, 2, dtype=np.float32) / np.float32(dim)))).astype(np.float32)
        ang = pos[:, None] * inv[None, :]
        return np.cos(ang).astype(np.float32).T, np.sin(ang).astype(np.float32).T

    c8, s8 = tables(8)
    t = np.zeros((2, 32, cfg.TT), np.float32)
    t[0, :] = 1.0
    t[0, 0:4] = c8
    t[0, 4:8] = c8
    t[1, 0:4] = -s8
    t[1, 4:8] = s8
    sh["ropeda"] = np.ascontiguousarray(t.reshape(64, cfg.TT))
    c32, s32 = tables(32)
    t = np.zeros((2, 32, cfg.TT), np.float32)
    t[0, 0:16] = c32
    t[0, 16:32] = c32
    t[1, 0:16] = -s32
    t[1, 16:32] = s32
    sh["ropeml"] = np.ascontiguousarray(t.reshape(64, cfg.TT))
    sh["ident"] = np.eye(128, dtype=np.float32)
    sh["metatok"] = np.ascontiguousarray(inp["meta_tokens"].astype(np.float32))
    return sh


def build_program(cfg):
    FC, R, TT = cfg.FC, cfg.R, cfg.TT
    nc = bass.Bass("TRN2", target_bir_lowering=False)
    stack = ExitStack()
    with stack:
        def din(name, shape):
            return nc.dram_tensor(name, list(shape), F32, kind="ExternalInput").ap()

        x_d = din("xtok", [R, D])
        meta_d = din("metatok", [NMETA, D])
        wgu_f = din("wgu", [4 * 2 * FC * 128, 1024])
        wd_f = din("wd", [4 * 8 * 128, FC * 128])
        win_f = din("win", [2 * NCH_IN * 128, 1024])
        wv_f = din("wv", [2 * 128, 8 * 768])
        wuq_f = din("wuq", [2 * 8 * 128, 256])
        wukv_f = din("wukv", [2 * 128, 512])
        wo_f = din("wo", [2 * 8 * 128, 1024])
        gcols_d = din("gcols", [128, 64])
        dalam_d = din("dalam", [2, 128])
        rbx_d = din("rbx", [2 * 6 * 15 * 64, 64])
        ropeda_d = din("ropeda", [64, TT])
        ropeml_d = din("ropeml", [64, TT])
        ident_d = din("ident", [128, 128])
        y_d = nc.dram_tensor("y", [R, D], F32, kind="ExternalOutput").ap()

        def dscr(name, shape, dt=BF16):
            if DEBUG and name.endswith("_s"):
                return nc.dram_tensor(name, list(shape), dt, kind="ExternalOutput").ap()
            return nc.dram_tensor(name, list(shape), dt).ap()

        wgu_b = dscr("wgu_b", [4 * 2 * FC * 128, 1024])
        wd_b = dscr("wd_b", [4 * 8 * 128, FC * 128])
        win_b = dscr("win_b", [2 * NCH_IN * 128, 1024])
        wv_b = dscr("wv_b", [2 * 128, 8 * 768])
        wuq_b = dscr("wuq_b", [2 * 8 * 128, 256])
        wukv_b = dscr("wukv_b", [2 * 128, 512])
        wo_b = dscr("wo_b", [2 * 8 * 128, 1024])
        hT_d = dscr("hT_s", [D, TT], F32)
        oT_d = dscr("oT_s", [D, TT])
        naq_d = dscr("naq_s", [384, TT])
        nak_d = dscr("nak_s", [384, TT])
        daq_d = dscr("daq_s", [384, TT])
        dak_d = dscr("dak_s", [384, TT])
        mlq_d = dscr("mlq_s", [384, TT])
        mlk_d = dscr("mlk_s", [256, TT])
        mlr_d = dscr("mlr_s", [32, TT])
        v_d = dscr("v_s", [TT, 1024])

        S = Sched(nc, stack)
        S.ps = [Buf("ps%d" % i, stack.enter_context(nc.psum_tensor("ps%d" % i, [128, 512], F32))) for i in range(8)]

        phase_bufs = []

        uniq = [0]

        def sb(st, name, shape, dt):
            uniq[0] += 1
            name = "s%d_%s" % (uniq[0], name)
            b = Buf(name, st.enter_context(nc.sbuf_tensor(name, list(shape), dt)))
            if st is not stack:
                phase_bufs.append(b)
            return b

        Bx = Buf("x", dram=True)
        Bconst = Buf("const", dram=True)
        Bw = {k: Buf("w_" + k, dram=True) for k in ("gu0", "gu1", "gu2", "gu3", "d0", "d1", "d2", "d3", "in0", "in1", "misc")}
        BhT = {}
        BoT = {}
        Bqk = Buf("qk", dram=True)
        Bv = Buf("v", dram=True)
        By = Buf("y", dram=True)

        ident = sb(stack, "ident", [128, 128], F32)
        onesb = sb(stack, "onesb", [128, 128], BF16)
        ones64 = sb(stack, "ones64", [64, 64], F32)
        sel65 = sb(stack, "sel65", [65, 64], F32)
        onesrow = sb(stack, "onesrow", [1, 64], F32)
        gcols = sb(stack, "gcols", [128, 64], F32)
        neglam = sb(stack, "neglam", [64, 2], F32)
        gsub = sb(stack, "gsub", [64, 2], F32)
        lamrow = sb(stack, "lamrow", [1, 256], F32)
        lamtmp = sb(stack, "lamtmp", [1, 128], F32)
        epscol = sb(stack, "epscol", [128, 1], F32)
        S.op("pool", lambda e: e.memset(epscol.t[:], EPS), [], [epscol])
        S.dma("sp", ident.t[:], ident_d[:, :], [Bconst], [ident], ident)
        S.dma("sp", gcols.t[:], gcols_d[:, :], [Bconst], [gcols], gcols)
        S.dma("sp", lamrow.t[0:1, :], dalam_d.rearrange("l f -> (l f)").rearrange("(o f) -> o f", o=1), [Bconst], [lamrow], lamrow)
        S.op("pool", lambda e: e.memset(onesb.t[:], 1.0), [], [onesb])
        S.op("pool", lambda e: e.memset(ones64.t[:], 1.0), [], [ones64])
        S.op("pool", lambda e: e.memset(sel65.t[:], 0.0), [], [sel65])
        S.op("pool", lambda e: e.memset(sel65.t[64:65, :], 1.0), [], [sel65])
        S.op("pool", lambda e: e.memset(onesrow.t[:], 1.0), [], [onesrow])

        def convert(dst, src, rows, key, r0=0):
            r = r0
            while r < r0 + rows:
                n = min(128, r0 + rows - r)
                S.dma("pool", dst[r:r + n, :], src[r:r + n, :], [Bconst], [Bw[key]], Bw[key])
                r += n

        def conv_ffn(lj):
            convert(wgu_b, wgu_f, 2 * FC * 128, "gu%d" % lj, lj * 2 * FC * 128)
            convert(wd_b, wd_f, 8 * 128, "d%d" % lj, lj * 8 * 128)

        conv_ffn(0)
        convert(win_b, win_f, NCH_IN * 128, "in0", 0)
        convert(wv_b, wv_f, 2 * 128, "misc")
        convert(wuq_b, wuq_f, 2 * 8 * 128, "misc")
        convert(wukv_b, wukv_f, 2 * 128, "misc")
        convert(wo_b, wo_f, 2 * 8 * 128, "misc")
        conv_ffn(1)
        conv_ffn(2)
        convert(win_b, win_f, NCH_IN * 128, "in1", NCH_IN * 128)
        conv_ffn(3)

        for l in range(2):
            lam_init = 0.8 - 0.6 * math.exp(-0.3 * l)
            A = lamrow.t[0:1, l * 128:l * 128 + 64]
            Bm = lamrow.t[0:1, l * 128 + 64:l * 128 + 128]
            S.op("dve", lambda e: e.tensor_tensor(lamtmp.t[0:1, 0:64], A, Bm, ALU.mult), [lamrow], [lamtmp])
            S.op("dve", lambda e: e.tensor_reduce(lamtmp.t[0:1, 64:66], lamtmp.t[0:1, 0:64].rearrange("o (g f) -> o g f", g=2),
                                                   mybir.AxisListType.X, ALU.add), [lamtmp], [lamtmp])
            S.op("act", lambda e: e.activation(out=lamtmp.t[0:1, 66:68], in_=lamtmp.t[0:1, 64:66], func=AF.Exp), [lamtmp], [lamtmp])
            S.op("dve", lambda e: e.tensor_tensor(lamtmp.t[0:1, 68:69], lamtmp.t[0:1, 67:68], lamtmp.t[0:1, 66:67], ALU.subtract), [lamtmp], [lamtmp])
            S.op("dve", lambda e: e.tensor_scalar(lamtmp.t[0:1, 69:70], lamtmp.t[0:1, 68:69], -lam_init, None, ALU.add), [lamtmp], [lamtmp])
            p = S.psum()
            S.op("pe", lambda e: e.matmul(p.t[0:64, 0:1], onesrow.t[0:1, 0:64], lamtmp.t[0:1, 69:70], start=True, stop=True), [onesrow, lamtmp], [p])
            S.op("dve", lambda e: e.tensor_copy(neglam.t[:, l:l + 1], p.t[0:64, 0:1]), [p], [neglam])
            S.op("dve", lambda e: e.tensor_scalar(gsub.t[:, l:l + 1], gcols.t[0:64, 62 + l:63 + l], 1.0 - lam_init, None, ALU.mult), [gcols], [gsub])

        cp_flip = [0]

        def evac(out_ap, in_ap, reads, writes):
            cp_flip[0] ^= 1
            if cp_flip[0]:
                S.op("act", lambda e: e.activation(out=out_ap, in_=in_ap, func=AF.Copy), reads, writes)
            else:
                S.op("dve", lambda e: e.tensor_copy(out_ap, in_ap), reads, writes)

        def rms_rstd(srcs, W, Dn, sqring, rsring):
            ps = S.psum()
            n = len(srcs)
            for c, (b, ap) in enumerate(srcs):
                sq = sqring.next()
                S.op("act", lambda e: e.activation(out=sq.t[:, :W], in_=ap, func=AF.Square), [b], [sq])
                S.op("pe", lambda e: e.matmul(ps.t[:, :W], onesb.t[:, :], sq.t[:, :W], start=(c == 0), stop=(c == n - 1)),
                     [sq, onesb], [ps], sig=True)
            rs = rsring.next()
            S.op("act", lambda e: e.activation(out=rs.t[:, :W], in_=ps.t[:, :W], func=AF.Ln, bias=epscol.t[:, 0:1], scale=1.0 / Dn), [ps, epscol], [rs])
            rs2 = rsring.next()
            S.op("act", lambda e: e.activation(out=rs2.t[:, :W], in_=rs.t[:, :W], func=AF.Exp, scale=-0.5), [rs], [rs2])
            return rs2

        def tok_phase(stage):
            with ExitStack() as ph:
                NS_ = 2
                h = [sb(ph, "h%d" % i, [128, 8, 512], F32) for i in range(NS_)]
                xn = [sb(ph, "xn%d" % i, [128, 8, 512], BF16) for i in range(NS_)]
                hid = [sb(ph, "hid%d" % i, [128, FC, 512], BF16) for i in range(NS_)]
                sqring = Ring([sb(ph, "sq%d" % i, [128, 512], BF16) for i in range(2)])
                rsring = Ring([sb(ph, "rs%d" % i, [128, 512], F32) for i in range(3)])
                sgring = Ring([sb(ph, "sg%d" % i, [128, 512], F32) for i in range(4)])
                guring = Ring([sb(ph, "gu%d" % i, [128, 2, 8, 128], BF16) for i in range(3)])
                wdring = Ring([sb(ph, "wdr%d" % i, [128, FC, 128], BF16) for i in range(2)])
                if stage >= 1:
                    o1 = sb(ph, "o1", [128, 8, 512], BF16)
                    woring = Ring([sb(ph, "wor%d" % i, [128, 8, 128], BF16) for i in range(2)])
                if stage == 0:
                    xsring = Ring([sb(ph, "xs%d" % i, [128, 4, 1024], F32) for i in range(1)])
                if stage <= 1:
                    winring = Ring([sb(ph, "winr%d" % i, [128, 8, 128], BF16) for i in range(2)])
                    wvt = sb(ph, "wvt", [128, 8, 768], BF16)
                    wuqt = sb(ph, "wuqt", [128, 8, 2, 128], BF16)
                    wukvt = sb(ph, "wukvt", [128, 512], BF16)
                    rda = [sb(ph, "rda%d" % i, [128, 2, 512], BF16) for i in range(NS_)]
                    rml = [sb(ph, "rml%d" % i, [96, 2, 512], BF16) for i in range(NS_)]
                    uoring = Ring([sb(ph, "uo%d" % i, [128, 512], BF16) for i in range(3)])
                    t1ring = sgring
                    t2ring = sgring
                    cq1 = sb(ph, "cq1", [128, 3, 512], F32)
                    cqn1 = sb(ph, "cqn1", [128, 3, 512], BF16)
                    cq = [cq1, cq1]
                    cqn = [cqn1, cqn1]
                    vtring = Ring([sb(ph, "vt%d" % i, [128, 1024], BF16) for i in range(2)])
                if stage == 2:
                    hn = [sb(ph, "hn%d" % i, [128, 8, 512], F32) for i in range(NS_)]
                    yring = Ring([sb(ph, "yt%d" % i, [128, 1024], F32) for i in range(2)])

                blocks = []
                for c0 in range(0, R, 1024):
                    blocks.append([(c0, 512), (c0 + 512, 512)])
                if stage <= 1:
                    blocks.append([(R, 3 * NMETA)])

                hT_v = hT_d.rearrange("(c p) t -> p c t", p=128)
                oT_v = oT_d.rearrange("(c p) t -> p c t", p=128)

                def ffn(lj, subs, gbase):
                    for i, (c0, W) in enumerate(subs):
                        rs = rms_rstd([(h[i], h[i].t[:, c, :W]) for c in range(8)], W, D, sqring, rsring)
                        if SUB < 1.4:
                            continue
                        for c in range(8):
                            S.op("dve", lambda e: e.scalar_tensor_tensor(out=xn[i].t[:, c, :W], in0=h[i].t[:, c, :W],
                                                                          scalar=gcols.t[:, gbase + c:gbase + c + 1], in1=rs.t[:, :W],
                                                                          op0=ALU.mult, op1=ALU.mult), [h[i], rs, gcols], [xn[i]])
                    if SUB < 1.6:
                        return
                    Bg = Bw["gu%d" % lj]
                    Bd = Bw["d%d" % lj]
                    for fc in range(FC):
                        w = guring.next()
                        for k in range(2):
                            r0 = ((lj * 2 + k) * FC + fc) * 128
                            S.dma("sp", w.t[:, k].rearrange("p c f -> p (c f)"), wgu_b[r0:r0 + 128, :], [Bg], [w], w)
                        for i, (c0, W) in enumerate(subs):
                            pg = S.psum()
                            pu = S.psum()
                            for c in range(8):
                                S.op("pe", lambda e: e.matmul(pg.t[:, :W], w.t[:, 0, c, :], xn[i].t[:, c, :W], start=(c == 0), stop=(c == 7)),
                                     [w, xn[i]], [pg], sig=(c == 7))
                            for c in range(8):
                                S.op("pe", lambda e: e.matmul(pu.t[:, :W], w.t[:, 1, c, :], xn[i].t[:, c, :W], start=(c == 0), stop=(c == 7)),
                                     [w, xn[i]], [pu], sig=(c == 7))
                            sg = sgring.next()
                            S.op("act", lambda e: e.activation(out=sg.t[:, :W], in_=pg.t[:, :W], func=AF.Silu), [pg], [sg])
                            S.op("dve", lambda e: e.tensor_tensor(hid[i].t[:, fc, :W], pu.t[:, :W], sg.t[:, :W], ALU.mult), [pu, sg], [hid[i]])
                    if SUB < 1.8:
                        return
                    for dc in range(8):
                        w = wdring.next()
                        r0 = (lj * 8 + dc) * 128
                        S.dma("sp", w.t[:].rearrange("p c f -> p (c f)"), wd_b[r0:r0 + 128, :], [Bd], [w], w)
                        for i, (c0, W) in enumerate(subs):
                            py = S.psum()
                            for fc in range(FC):
                                S.op("pe", lambda e: e.matmul(py.t[:, :W], w.t[:, fc, :], hid[i].t[:, fc, :W], start=(fc == 0), stop=(fc == FC - 1)),
                                     [w, hid[i]], [py], sig=(fc == FC - 1))
                            S.op("dve", lambda e: e.scalar_tensor_tensor(out=h[i].t[:, dc, :W], in0=py.t[:, :W], scalar=0.5, in1=h[i].t[:, dc, :W],
                                                                          op0=ALU.mult, op1=ALU.add), [py, h[i]], [h[i]])

                def wout(l, subs):
                    for i, (c0, W) in enumerate(subs):
                        S.dma("sp", o1.t[:, :, :W], oT_v[:, :, c0:c0 + W], [BoT[c0]], [o1], o1)
                        for dc in range(8):
                            w = woring.next()
                            r0 = (l * 8 + dc) * 128
                            S.dma("sp", w.t[:].rearrange("p c f -> p (c f)"), wo_b[r0:r0 + 128, :], [Bw["misc"]], [w], w)
                            py = S.psum()
                            for fc in range(8):
                                S.op("pe", lambda e: e.matmul(py.t[:, :W], w.t[:, fc, :], o1.t[:, fc, :W], start=(fc == 0), stop=(fc == 7)),
                                     [w, o1], [py], sig=(fc == 7))
                            S.op("dve", lambda e: e.tensor_tensor(h[i].t[:, dc, :W], py.t[:, :W], h[i].t[:, dc, :W], ALU.add), [py, h[i]], [h[i]])

                def proj_in(l, subs):
                    gb = (l * 3 + 1) * 8
                    for i, (c0, W) in enumerate(subs):
                        rs = rms_rstd([(h[i], h[i].t[:, c, :W]) for c in range(8)], W, D, sqring, rsring)
                        for c in range(8):
                            S.op("dve", lambda e: e.scalar_tensor_tensor(out=xn[i].t[:, c, :W], in0=h[i].t[:, c, :W],
                                                                          scalar=gcols.t[:, gb + c:gb + c + 1], in1=rs.t[:, :W],
                                                                          op0=ALU.mult, op1=ALU.mult), [h[i], rs, gcols], [xn[i]])
                        for gq in range(4):
                            S.dma("pool", rda[i].t[gq * 32:(gq + 1) * 32, :, :W],
                                  ropeda_d.rearrange("(a r) t -> r a t", a=2)[:, :, c0:c0 + W], [Bconst], [rda[i]], rda[i])
                        for base in (0, 64):
                            S.dma("pool", rml[i].t[base:base + 32, :, :W],
                                  ropeml_d.rearrange("(a r) t -> r a t", a=2)[:, :, c0:c0 + W], [Bconst], [rml[i]], rml[i])
                    Bi = Bw["in%d" % l]
                    S.dma("sp", wvt.t[:].rearrange("p c f -> p (c f)"), wv_b[l * 128:(l + 1) * 128, :], [Bw["misc"]], [wvt], wvt)
                    S.dma("sp", wuqt.t[:], wuq_b[l * 1024:(l + 1) * 1024, :].rearrange("(j p) (c f) -> p j c f", p=128, c=2),
                          [Bw["misc"]], [wuqt], wuqt)
                    S.dma("sp", wukvt.t[:], wukv_b[l * 128:(l + 1) * 128, :], [Bw["misc"]], [wukvt], wukvt)

                    def load_chunk(j):
                        w = winring.next()
                        r0 = (l * NCH_IN + j) * 128
                        S.dma("sp", w.t[:].rearrange("p c f -> p (c f)"), win_b[r0:r0 + 128, :], [Bi], [w], w)
                        return w

                    def mm_chunk(w, i, W, M):
                        p = S.psum()
                        for c in range(8):
                            S.op("pe", lambda e: e.matmul(p.t[:M, :W], w.t[:, c, :M], xn[i].t[:, c, :W], start=(c == 0), stop=(c == 7)),
                                 [w, xn[i]], [p], sig=(c == 7))
                        return p

                    def store_rows(dst, row0, M, uo, c0, W):
                        S.dma("pool", dst[row0:row0 + M, c0:c0 + W], uo.t[:M, :W], [uo], [Bqk], uo)

                    for j in range(6):
                        w = load_chunk(j)
                        for i, (c0, W) in enumerate(subs):
                            p = mm_chunk(w, i, W, 128)
                            uo = uoring.next()
                            evac(uo.t[:, :W], p.t[:, :W], [p], [uo])
                            store_rows(naq_d if j < 3 else nak_d, (j % 3) * 128, 128, uo, c0, W)
                    for grp, dst in ((6, daq_d), (12, dak_d)):
                        for jj in range(3):
                            wm = load_chunk(grp + jj)
                            wp = load_chunk(grp + 3 + jj)
                            for i, (c0, W) in enumerate(subs):
                                pm = mm_chunk(wm, i, W, 128)
                                pp = mm_chunk(wp, i, W, 128)
                                t1 = t1ring.next()
                                t2 = t2ring.next()
                                S.op("dve", lambda e: e.tensor_tensor(t1.t[:, :W], pm.t[:, :W], rda[i].t[:, 0, :W], ALU.mult), [pm, rda[i]], [t1])
                                S.op("dve", lambda e: e.tensor_tensor(t2.t[:, :W], pp.t[:, :W], rda[i].t[:, 1, :W], ALU.mult), [pp, rda[i]], [t2])
                                uo = uoring.next()
                                S.op("pool", lambda e: e.tensor_tensor(uo.t[:, :W], t1.t[:, :W], t2.t[:, :W], ALU.add), [t1, t2], [uo])
                                store_rows(dst, jj * 128, 128, uo, c0, W)
                    wm = load_chunk(21)
                    wp = load_chunk(22)
                    for i, (c0, W) in enumerate(subs):
                        pm = mm_chunk(wm, i, W, 32)
                        pp = mm_chunk(wp, i, W, 32)
                        t1 = t1ring.next()
                        t2 = t2ring.next()
                        S.op("dve", lambda e: e.tensor_tensor(t1.t[:32, :W], pm.t[:32, :W], rml[i].t[0:32, 0, :W], ALU.mult), [pm, rml[i]], [t1])
                        S.op("dve", lambda e: e.tensor_tensor(t2.t[:32, :W], pp.t[:32, :W], rml[i].t[0:32, 1, :W], ALU.mult), [pp, rml[i]], [t2])
                        uo = uoring.next()
                        S.op("pool", lambda e: e.tensor_tensor(uo.t[:32, :W], t1.t[:32, :W], t2.t[:32, :W], ALU.add), [t1, t2], [uo])
                        store_rows(mlr_d, 0, 32, uo, c0, W)
                    for i, (c0, W) in enumerate(subs):
                        for jj in range(3):
                            w = load_chunk(18 + jj)
                            p = mm_chunk(w, i, W, 128)
                            evac(cq[i].t[:, jj, :W], p.t[:, :W], [p], [cq[i]])
                        rs = rms_rstd([(cq[i], cq[i].t[:, c, :W]) for c in range(2)], W, 256, sqring, rsring)
                        for c in range(2):
                            S.op("dve", lambda e: e.scalar_tensor_tensor(out=cqn[i].t[:, c, :W], in0=cq[i].t[:, c, :W],
                                                                          scalar=gcols.t[:, 56 + 2 * l + c:57 + 2 * l + c], in1=rs.t[:, :W],
                                                                          op0=ALU.mult, op1=ALU.mult), [cq[i], rs, gcols], [cqn[i]])
                        rs = rms_rstd([(cq[i], cq[i].t[:, 2, :W])], W, 128, sqring, rsring)
                        S.op("dve", lambda e: e.scalar_tensor_tensor(out=cqn[i].t[:, 2, :W], in0=cq[i].t[:, 2, :W],
                                                                      scalar=gcols.t[:, 60 + l:61 + l], in1=rs.t[:, :W],
                                                                      op0=ALU.mult, op1=ALU.mult), [cq[i], rs, gcols], [cqn[i]])
                        for hh in range(4):
                            pm = S.psum()
                            pp = S.psum()
                            for c in range(2):
                                S.op("pe", lambda e: e.matmul(pm.t[:96, :W], wuqt.t[:, 2 * hh, c, 0:96], cqn[i].t[:, c, :W], start=(c == 0), stop=(c == 1)),
                                     [wuqt, cqn[i]], [pm], sig=(c == 1))
                            for c in range(2):
                                S.op("pe", lambda e: e.matmul(pp.t[:96, :W], wuqt.t[:, 2 * hh + 1, c, 0:96], cqn[i].t[:, c, :W], start=(c == 0), stop=(c == 1)),
                                     [wuqt, cqn[i]], [pp], sig=(c == 1))
                            uo = uoring.next()
                            t1 = t1ring.next()
                            t2 = t2ring.next()
                            S.op("act", lambda e: e.activation(out=uo.t[0:64, :W], in_=pm.t[0:64, :W], func=AF.Copy), [pm], [uo])
                            S.op("dve", lambda e: e.tensor_tensor(t1.t[64:96, :W], pm.t[64:96, :W], rml[i].t[64:96, 0, :W], ALU.mult), [pm, rml[i]], [t1])
                            S.op("dve", lambda e: e.tensor_tensor(t2.t[64:96, :W], pp.t[64:96, :W], rml[i].t[64:96, 1, :W], ALU.mult), [pp, rml[i]], [t2])
                            S.op("pool", lambda e: e.tensor_tensor(uo.t[64:96, :W], t1.t[64:96, :W], t2.t[64:96, :W], ALU.add), [t1, t2, uo], [uo])
                            store_rows(mlq_d, hh * 96, 96, uo, c0, W)
                        for kk in range(2):
                            p = S.psum()
                            S.op("pe", lambda e: e.matmul(p.t[:, :W], wukvt.t[:, kk * 128:(kk + 1) * 128], cqn[i].t[:, 2, :W], start=True, stop=True),
                                 [wukvt, cqn[i]], [p])
                            uo = uoring.next()
                            evac(uo.t[:, :W], p.t[:, :W], [p], [uo])
                            store_rows(mlk_d, kk * 128, 128, uo, c0, W)
                        ng = (W + 127) // 128
                        for g in range(ng):
                            gw = min(128, W - g * 128)
                            vt = vtring.next()
                            for half in range(2):
                                p = S.psum()
                                for c in range(8):
                                    S.op("pe", lambda e: e.matmul(p.t[:gw, :384], xn[i].t[:, c, g * 128:g * 128 + gw], wvt.t[:, c, half * 384:(half + 1) * 384],
                                                                   start=(c == 0), stop=(c == 7)), [xn[i], wvt], [p], sig=(c == 7))
                                evac(vt.t[:gw, half * 384:(half + 1) * 384], p.t[:gw, :384], [p], [vt])
                            p = S.psum()
                            S.op("pe", lambda e: e.matmul(p.t[:gw, :256], cqn[i].t[:, 2, g * 128:g * 128 + gw], wukvt.t[:, 256:512], start=True, stop=True),
                                 [cqn[i], wukvt], [p])
                            evac(vt.t[:gw, 768:1024], p.t[:gw, :256], [p], [vt])
                            S.dma("pool", v_d[c0 + g * 128:c0 + g * 128 + gw, :], vt.t[:gw, :], [vt], [Bv], vt)

                for bi, subs in enumerate(blocks):
                    is_meta = subs[0][0] >= R
                    for i, (c0, W) in enumerate(subs):
                        key = c0
                        if key not in BhT:
                            BhT[key] = Buf("hT%d" % key, dram=True)
                            BoT[key] = Buf("oT%d" % key, dram=True)
                        if stage == 0:
                            xs = xsring.next()
                            if is_meta:
                                for s in range(3):
                                    S.dma("sp", xs.t[s * NMETA:(s + 1) * NMETA, 0, :], meta_d[:, :], [Bconst], [xs], xs)
                            else:
                                S.dma("sp", xs.t[:, :, :], x_d[c0:c0 + W, :].rearrange("(g p) d -> p g d", p=128), [Bx], [xs], xs)
                            ng = (W + 127) // 128
                            for c in range(8):
                                p = S.psum()
                                for g in range(ng):
                                    gw = min(128, W - g * 128)
                                    S.op("pe", lambda e: e.transpose(p.t[:, g * 128:g * 128 + gw], xs.t[:gw, g, c * 128:(c + 1) * 128], ident.t[:gw, :gw]),
                                         [xs, ident], [p], sig=(g == ng - 1))
                                evac(h[i].t[:, c, :W], p.t[:, :W], [p], [h[i]])
                        else:
                            S.dma("sp", h[i].t[:, :, :W], hT_v[:, :, c0:c0 + W], [BhT[key]], [h[i]], h[i])
                    if stage >= 1:
                        wout(stage - 1, subs)
                        ffn((stage - 1) * 2 + 1, subs, ((stage - 1) * 3 + 2) * 8)
                    if stage <= 1:
                        if SUB >= 1.2:
                            ffn(stage * 2, subs, (stage * 3) * 8)
                        if SUB >= 3:
                            proj_in(stage, subs)
                        for i, (c0, W) in enumerate(subs):
                            S.dma("pool", hT_v[:, :, c0:c0 + W], h[i].t[:, :, :W], [h[i]], [BhT[c0]], h[i])
                    else:
                        for i, (c0, W) in enumerate(subs):
                            rs = rms_rstd([(h[i], h[i].t[:, c, :W]) for c in range(8)], W, D, sqring, rsring)
                            for c in range(8):
                                S.op("dve", lambda e: e.scalar_tensor_tensor(out=hn[i].t[:, c, :W], in0=h[i].t[:, c, :W],
                                                                              scalar=gcols.t[:, 48 + c:49 + c], in1=rs.t[:, :W],
                                                                              op0=ALU.mult, op1=ALU.mult), [h[i], rs, gcols], [hn[i]])
                            for g in range(W // 128):
                                yt = yring.next()
                                for half in range(2):
                                    p = S.psum()
                                    for cc in range(4):
                                        c = half * 4 + cc
                                        S.op("pe", lambda e: e.transpose(p.t[:, cc * 128:(cc + 1) * 128], hn[i].t[:, c, g * 128:(g + 1) * 128], ident.t[:, :]),
                                             [hn[i], ident], [p], sig=(cc == 3))
                                    evac(yt.t[:, half * 512:(half + 1) * 512], p.t[:, :], [p], [yt])
                                S.dma("pool", y_d[c0 + g * 128:c0 + (g + 1) * 128, :], yt.t[:, :], [yt], [By], yt)
                S.barrier()
                S.release(phase_bufs)
                del phase_bufs[:]

        def att_phase(l, last):
            with ExitStack() as ph:
                NT = cfg.NMAX // 128
                ktring = Ring([sb(ph, "kt%d" % i, [96, cfg.NMAX], BF16) for i in range(2)])
                kmring = Ring([sb(ph, "km%d" % i, [96, NMETA], BF16) for i in range(2)])
                vring = Ring([sb(ph, "vv%d" % i, [128, NT, 128], BF16) for i in range(2)])
                vmring = Ring([sb(ph, "vm%d" % i, [NMETA, 128], BF16) for i in range(2)])
                qring = Ring([sb(ph, "qq%d" % i, [96, 512], BF16) for i in range(3)])
                ptring = Ring([sb(ph, "pt%d" % i, [128, 512], BF16) for i in range(6)])
                tmring = Ring([sb(ph, "tm%d" % i, [128, 512], F32) for i in range(3)])
                mb = sb(ph, "mb", [128, 3, 8, 512], F32)
                osring = Ring([sb(ph, "os%d" % i, [65, 512], F32) for i in range(4)])
                rdring = Ring([sb(ph, "rd%d" % i, [64, 512], F32) for i in range(6)])
                aring = Ring([sb(ph, "aa%d" % i, [64, 512], F32) for i in range(4)])
                ooring = Ring([sb(ph, "oo%d" % i, [64, 512], BF16) for i in range(3)])
                spool = Ring(S.ps[0:4])
                opool = Ring(S.ps[4:8])
                for b in vring.bufs:
                    S.op("pool", lambda e: e.memset(b.t[:, :, 64:128], 1.0), [], [b])
                for b in vmring.bufs:
                    S.op("pool", lambda e: e.memset(b.t[:, 64:128], 1.0), [], [b])

                def run_qtile(streams, Q, Nq, ktl, scale):
                    O = [opool.next() for _ in streams]
                    n = len(ktl)

                    def qk(i):
                        res = []
                        Kb, kfn, Vb, vap, nk, mbap = ktl[i]
                        for (r0, r1) in streams:
                            ps = spool.next()
                            S.op("pe", lambda e: e.matmul(ps.t[:nk, :Nq], kfn(r0, r1), Q.t[r0:r1, :Nq], start=True, stop=True), [Kb, Q], [ps])
                            pt = ptring.next()
                            if mbap is not None:
                                tm = tmring.next()
                                S.op("dve", lambda e: e.scalar_tensor_tensor(out=tm.t[:nk, :Nq], in0=ps.t[:nk, :Nq], scalar=scale, in1=mbap[:nk, :Nq],
                                                                              op0=ALU.mult, op1=ALU.add), [ps, mb], [tm])
                                S.op("act", lambda e: e.activation(out=pt.t[:nk, :Nq], in_=tm.t[:nk, :Nq], func=AF.Exp), [tm], [pt])
                            else:
                                S.op("act", lambda e: e.activation(out=pt.t[:nk, :Nq], in_=ps.t[:nk, :Nq], func=AF.Exp, scale=scale), [ps], [pt])
                            res.append(pt)
                        return res

                    cur = qk(0)
                    for i in range(n):
                        nxt = qk(i + 1) if i + 1 < n else None
                        Kb, kfn, Vb, vap, nk, mbap = ktl[i]
                        for si in range(len(streams)):
                            S.op("pe", lambda e: e.matmul(O[si].t[:128, :Nq], vap, cur[si].t[:nk, :Nq], start=(i == 0), stop=(i == n - 1)),
                                 [Vb, cur[si]], [O[si]], sig=True)
                        cur = nxt
                    return O

                def normalize(Ob, Nq):
                    osb = osring.next()
                    evac(osb.t[:65, :Nq], Ob.t[:65, :Nq], [Ob], [osb])
                    dps = spool.next()
                    S.op("pe", lambda e: e.matmul(dps.t[:64, :Nq], sel65.t[:65, :64], osb.t[:65, :Nq], start=True, stop=True), [sel65, osb], [dps])
                    rd = rdring.next()
                    S.op("dve", lambda e: e.reciprocal(rd.t[:64, :Nq], dps.t[:64, :Nq]), [dps], [rd])
                    return osb, rd

                heads = [("na", hh) for hh in range(6)] + [("da", hh) for hh in range(6)] + [("ml", hh) for hh in range(4)]
                for kind, hh in heads:
                    if kind == "na":
                        scale = 64 ** -0.5
                        S.op("pool", lambda e: e.memset(mb.t[:].rearrange("p a b c -> p (a b c)"), NEG), [], [mb])
                        for v in range(3):
                            delta = (0, 4, 8)[v]
                            for kr in range(16):
                                qs = []
                                for qr in range(8):
                                    if v == 0:
                                        lo = max(qr - 4, 0)
                                    elif v == 1:
                                        lo = qr
                                    else:
                                        lo = 8 + min(qr - 4, 0)
                                    if lo <= kr < lo + 8:
                                        qs.append(qr)
                                if not qs:
                                    continue
                                qa, qb = qs[0], qs[-1] + 1
                                dra = 7 - kr + qa + delta
                                row0 = ((l * 6 + hh) * 15 + dra) * 64
                                src = rbx_d[row0:row0 + (qb - qa) * 64, :].rearrange("(q k) c -> k q c", k=64)
                                krb = kr % 2
                                S.dma("sp", mb.t[krb * 64:(krb + 1) * 64, v, kr // 2, qa * 64:qb * 64].rearrange("p (q c) -> p q c", c=64),
                                      src, [Bconst], [mb], mb)
                        qsrc, ksrc, row0, nrows, voff = naq_d, nak_d, hh * 64, 64, hh * 64
                        streams = [(0, 64)]
                    elif kind == "da":
                        scale = 32 ** -0.5
                        qsrc, ksrc, row0, nrows, voff = daq_d, dak_d, hh * 64, 64, 384 + hh * 64
                        streams = [(0, 32), (32, 64)]
                    else:
                        scale = 96 ** -0.5
                        qsrc, ksrc, row0, nrows, voff = mlq_d, mlk_d, hh * 96, 96, 768 + hh * 64
                        streams = [(0, 96)]
                    orow = voff
                    for s in range(3):
                        n = cfg.seqn[s]
                        st = cfg.start[s]
                        mcol = R + NMETA * s
                        nt = n // 128
                        Kt = ktring.next()
                        Km = kmring.next()
                        Vv = vring.next()
                        Vm = vmring.next()
                        if kind == "ml":
                            S.dma("sp", Kt.t[0:64, :n], mlk_d[hh * 64:(hh + 1) * 64, st:st + n], [Bqk], [Kt], Kt)
                            S.dma("sp", Kt.t[64:96, :n], mlr_d[0:32, st:st + n], [Bqk], [Kt], Kt)
                            S.dma("sp", Km.t[0:64, :], mlk_d[hh * 64:(hh + 1) * 64, mcol:mcol + NMETA], [Bqk], [Km], Km)
                            S.dma("sp", Km.t[64:96, :], mlr_d[0:32, mcol:mcol + NMETA], [Bqk], [Km], Km)
                        else:
                            S.dma("sp", Kt.t[0:64, :n], ksrc[row0:row0 + 64, st:st + n], [Bqk], [Kt], Kt)
                            S.dma("sp", Km.t[0:64, :], ksrc[row0:row0 + 64, mcol:mcol + NMETA], [Bqk], [Km], Km)
                        for t0 in range(0, nt, 16):
                            t1_ = min(nt, t0 + 16)
                            S.dma("sp", Vv.t[:, t0:t1_, 0:64],
                                  v_d[st + t0 * 128:st + t1_ * 128, voff:voff + 64].rearrange("(t p) e -> p t e", p=128), [Bv], [Vv], Vv)
                        S.dma("sp", Vm.t[:, 0:64], v_d[mcol:mcol + NMETA, voff:voff + 64], [Bv], [Vm], Vm)

                        qtiles = [(st + q0, 512, q0 // 512) for q0 in range(0, n, 512)]
                        if not last:
                            qtiles.append((mcol, NMETA, -1))
                        nblk = n // 512
                        rows = n // 64
                        for (qc0, Nq, qb_) in qtiles:
                            Q = qring.next()
                            S.dma("sp", Q.t[:nrows, :Nq], qsrc[row0:row0 + nrows, qc0:qc0 + Nq], [Bqk], [Q], Q)
                            meta_kt = (Km, (lambda r0, r1, Km=Km: Km.t[r0:r1, 0:NMETA]), Vm, Vm.t[:NMETA, 0:128], NMETA, None)
                            if kind == "na":
                                if qb_ < 0:
                                    ktl = [meta_kt]
                                else:
                                    v = 0 if qb_ == 0 else (2 if qb_ == nblk - 1 else 1)
                                    w0 = min(max(8 * qb_ - 4, 0), rows - 16)
                                    kt0 = w0 // 2
                                    ktl = []
                                    for j in range(8):
                                        kt = kt0 + j
                                        ktl.append((Kt, (lambda r0, r1, kt=kt, Kt=Kt: Kt.t[r0:r1, kt * 128:(kt + 1) * 128]), Vv, Vv.t[:, kt, 0:128], 128,
                                                    mb.t[:, v, j, :]))
                                    ktl.append(meta_kt)
                            else:
                                ktl = [(Kt, (lambda r0, r1, kt=kt, Kt=Kt: Kt.t[r0:r1, kt * 128:(kt + 1) * 128]), Vv, Vv.t[:, kt, 0:128], 128, None)
                                       for kt in range(nt)]
                                ktl.append(meta_kt)
                            O = run_qtile(streams, Q, Nq, ktl, scale)
                            oo = ooring.next()
                            if kind == "da":
                                os0, rd0 = normalize(O[0], Nq)
                                os1, rd1 = normalize(O[1], Nq)
                                a = aring.next()
                                b = aring.next()
                                S.op("dve", lambda e: e.tensor_tensor(a.t[:64, :Nq], os0.t[:64, :Nq], rd0.t[:64, :Nq], ALU.mult), [os0, rd0], [a])
                                S.op("pool", lambda e: e.tensor_tensor(b.t[:64, :Nq], os1.t[:64, :Nq], rd1.t[:64, :Nq], ALU.mult), [os1, rd1], [b])
                                S.op("dve", lambda e: e.scalar_tensor_tensor(out=a.t[:64, :Nq], in0=b.t[:64, :Nq], scalar=neglam.t[:64, l:l + 1], in1=a.t[:64, :Nq],
                                                                              op0=ALU.mult, op1=ALU.add), [a, b, neglam], [a])
                                sq = aring.next()
                                S.op("act", lambda e: e.activation(out=sq.t[:64, :Nq], in_=a.t[:64, :Nq], func=AF.Square), [a], [sq])
                                sp_ = spool.next()
                                S.op("pe", lambda e: e.matmul(sp_.t[:64, :Nq], ones64.t[:64, :64], sq.t[:64, :Nq], start=True, stop=True), [ones64, sq], [sp_])
                                rs = rdring.next()
                                S.op("act", lambda e: e.activation(out=rs.t[:64, :Nq], in_=sp_.t[:64, :Nq], func=AF.Ln, bias=epscol.t[:64, 0:1], scale=1.0 / 64), [sp_, epscol], [rs])
                                rs_in = rs
                                rs = rdring.next()
                                S.op("act", lambda e: e.activation(out=rs.t[:64, :Nq], in_=rs_in.t[:64, :Nq], func=AF.Exp, scale=-0.5), [rs_in], [rs])
                                S.op("dve", lambda e: e.scalar_tensor_tensor(out=oo.t[:64, :Nq], in0=a.t[:64, :Nq], scalar=gsub.t[:64, l:l + 1], in1=rs.t[:64, :Nq],
                                                                              op0=ALU.mult, op1=ALU.mult), [a, rs, gsub], [oo])
                            else:
                                os0, rd0 = normalize(O[0], Nq)
                                S.op("dve", lambda e: e.tensor_tensor(oo.t[:64, :Nq], os0.t[:64, :Nq], rd0.t[:64, :Nq], ALU.mult), [os0, rd0], [oo])
                            key = qc0 if qc0 < R else R
                            S.dma("pool", oT_d[orow:orow + 64, qc0:qc0 + Nq], oo.t[:64, :Nq], [oo], [BoT[key]], oo)
                S.barrier()
                S.release(phase_bufs)
                del phase_bufs[:]

        if STOP >= 1:
            tok_phase(0)
        if STOP >= 2:
            att_phase(0, False)
        if STOP >= 3:
            tok_phase(1)
        if STOP >= 4:
            att_phase(1, True)
        if STOP >= 5:
            tok_phase(2)
        S.barrier()
    return nc


_CACHE = {}


def run(cfg, inp):
    sh = _host_shared(cfg, inp)
    if "nc" not in _CACHE or _CACHE.get("cfg") != (cfg.NP, cfg.NS, cfg.DFF):
        _CACHE["nc"] = build_program(cfg)
        _CACHE["cfg"] = (cfg.NP, cfg.NS, cfg.DFF)
    nc = _CACHE["nc"]
    xp, xs = inp["x_prompt"], inp["x_sample"]
    in_maps = []
    for c in range(NCORES):
        m = dict(sh)
        m["xtok"] = np.ascontiguousarray(np.concatenate([xp[c], xs[2 * c], xs[2 * c + 1]], axis=0).astype(np.float32))
        in_maps.append(m)
    res = run_bass_kernel_spmd(nc, in_maps, core_ids=list(range(NCORES)))
    _CACHE["res"] = res
    yp = np.empty(xp.shape, np.float32)
    ys = np.empty(xs.shape, np.float32)
    for c in range(NCORES):
        y = res.results[c]["y"]
        yp[c] = y[:cfg.NP]
        ys[2 * c] = y[cfg.NP:cfg.NP + cfg.NS]
        ys[2 * c + 1] = y[cfg.NP + cfg.NS:]
    return yp, ys


def kernel(**inputs):
    inp = {k: np.asarray(v) for k, v in inputs.items()}
    cfg = Cfg(inp["x_prompt"].shape[1], inp["x_sample"].shape[1], inp["ffn_w_gate"].shape[-1])
    return run(cfg, inp)
```

```python
import math
from contextlib import ExitStack
import numpy as np
import concourse.bass as bass
import concourse.mybir as mybir
from concourse.bass_utils import run_bass_kernel_spmd

F32 = mybir.dt.float32
BF16 = mybir.dt.bfloat16
AF = mybir.ActivationFunctionType
ALU = mybir.AluOpType

D = 1024
NMETA = 16
EPS = 1e-6
NEG = -30000.0
NCORES = 8
ROPE_THETA = 500000.0
DEBUG = False
STOP = 5
SUB = 9


class Buf:
    def __init__(self, name, t=None, dram=False):
        self.name = name
        self.t = t
        self.dram = dram
        self.w = {}
        self.rd = {}
        self.sem = None
        self.cnt = 0

    def add_rd(self, tok):
        sid = id(tok[0])
        if self.rd.get(sid, (None, 0))[1] < tok[1]:
            self.rd[sid] = tok

    def set_w(self, tok):
        sid = id(tok[0])
        if self.w.get(sid, (None, 0))[1] < tok[1]:
            self.w[sid] = tok


class Ring:
    def __init__(self, bufs):
        self.bufs = bufs
        self.i = 0

    def next(self):
        b = self.bufs[self.i % len(self.bufs)]
        self.i += 1
        return b


class Sched:
    def __init__(self, nc, stack):
        self.nc = nc
        self.stack = stack
        self.E = {"pe": nc.tensor, "act": nc.scalar, "dve": nc.vector, "pool": nc.gpsimd, "sp": nc.sync}
        self.esem = {}
        for e in ("pe", "act", "dve", "pool"):
            self.esem[e] = stack.enter_context(nc.semaphore("es_" + e))
        self.cnt = {e: 0 for e in self.E}
        self.seen = {e: {} for e in self.E}
        self.semcnt = {}
        self.free_sems = []
        self.nsem = 4
        self.ps = None
        self.psi = 0

    def _waits(self, eng, reads, writes):
        toks = {}
        own = self.esem.get(eng)

        def add(d, raw):
            for sid, (sem, val) in d.items():
                if sem is own and (eng == "pe" or (not raw and eng != "pool")):
                    continue
                if toks.get(sid, (None, 0))[1] < val:
                    toks[sid] = (sem, val)

        for b in reads:
            add(b.w, True)
        for b in writes:
            add(b.w, False)
            add(b.rd, False)
        e = self.E[eng]
        seen = self.seen[eng]
        for sid, (sem, val) in toks.items():
            if seen.get(sid, 0) >= val:
                continue
            seen[sid] = val
            e.wait_ge(sem, val)

    def op(self, eng, fn, reads=(), writes=(), sig=True):
        self._waits(eng, reads, writes)
        ins = fn(self.E[eng])
        if sig:
            self.cnt[eng] += 1
            ins.then_inc(self.esem[eng], 1)
            idx = self.cnt[eng]
        else:
            idx = self.cnt[eng] + 1
        tok = (self.esem[eng], idx)
        for b in reads:
            b.add_rd(tok)
        for b in writes:
            b.set_w(tok)

    def dma(self, eng, out, in_, reads, writes, sb):
        self._waits(eng, reads, writes)
        if sb.sem is None:
            if self.free_sems:
                sb.sem, sb.cnt = self.free_sems.pop()
            else:
                sb.sem = self.stack.enter_context(self.nc.semaphore("ds_%d" % self.nsem))
                self.nsem += 1
        self.E[eng].dma_start(out=out, in_=in_).then_inc(sb.sem, 16)
        sb.cnt += 16
        self.semcnt[id(sb.sem)] = (sb.sem, sb.cnt)
        tok = (sb.sem, sb.cnt)
        for b in reads:
            b.add_rd(tok)
        for b in writes:
            b.set_w(tok)

    def barrier(self):
        for eng, e in self.E.items():
            seen = self.seen[eng]
            for o, sem in self.esem.items():
                v = self.cnt[o]
                if v > 0 and seen.get(id(sem), 0) < v and o != eng:
                    seen[id(sem)] = v
                    e.wait_ge(sem, v)
            for sid, (sem, v) in self.semcnt.items():
                if seen.get(sid, 0) < v:
                    seen[sid] = v
                    e.wait_ge(sem, v)

    def release(self, bufs):
        for b in bufs:
            if b.sem is not None:
                self.free_sems.append((b.sem, b.cnt))
                b.sem = None

    def psum(self):
        b = self.ps[self.psi % 8]
        self.psi += 1
        return b


class Cfg:
    def __init__(self, NP, NS, DFF):
        self.NP, self.NS, self.DFF = NP, NS, DFF
        self.FC = DFF // 128
        self.seqn = [NP, NS, NS]
        self.start = [0, NP, NP + NS]
        self.R = NP + 2 * NS
        self.TT = self.R + 3 * NMETA
        self.NMAX = max(self.seqn)


NA_Q, NA_K, NA_V, DA_Q, DA_K, DA_V, M_CQ, M_CKV, M_KR = 0, 384, 768, 1152, 1536, 1920, 2304, 2560, 2688
NCH_IN = 23


def _win_cols():
    idx = -np.ones((NCH_IN, 128), np.int64)
    r = np.arange(128)
    for i in range(3):
        idx[i] = NA_Q + 128 * i + r
        idx[3 + i] = NA_K + 128 * i + r
        dd = r % 32
        pr = np.where(dd < 4, r + 4, np.where(dd < 8, r - 4, r))
        idx[6 + i] = DA_Q + 128 * i + r
        idx[9 + i] = DA_Q + 128 * i + pr
        idx[12 + i] = DA_K + 128 * i + r
        idx[15 + i] = DA_K + 128 * i + pr
    idx[18] = M_CQ + r
    idx[19] = M_CQ + 128 + r
    idx[20] = M_CKV + r
    idx[21, :32] = M_KR + np.arange(32)
    idx[22, :16] = M_KR + np.arange(16) + 16
    idx[22, 16:32] = M_KR + np.arange(16)
    return idx.reshape(-1)


def _gather_cols(w, idx):
    out = w[:, np.maximum(idx, 0)]
    out = np.where(idx[None, :] >= 0, out, np.float32(0.0))
    return np.ascontiguousarray(out.astype(np.float32))


def _host_shared(cfg, inp):
    FC = cfg.FC
    sh = {}
    g, u, dn = inp["ffn_w_gate"], inp["ffn_w_up"], inp["ffn_w_down"]
    wgu = np.empty((4, 2, FC, 128, 8 * 128), np.float32)
    wd = np.empty((4, 8, 128, FC * 128), np.float32)
    for l in range(2):
        for j in range(2):
            lj = l * 2 + j
            wgu[lj, 0] = g[l, j].reshape(8, 128, FC, 128).transpose(2, 1, 0, 3).reshape(FC, 128, 1024)
            wgu[lj, 1] = u[l, j].reshape(8, 128, FC, 128).transpose(2, 1, 0, 3).reshape(FC, 128, 1024)
            wd[lj] = dn[l, j].reshape(FC, 128, 8, 128).transpose(2, 1, 0, 3).reshape(8, 128, FC * 128)
    sh["wgu"] = wgu.reshape(4 * 2 * FC * 128, 1024)
    sh["wd"] = wd.reshape(4 * 8 * 128, FC * 128)
    idx = _win_cols()
    win = np.empty((2, NCH_IN, 128, 1024), np.float32)
    wv = np.empty((2, 128, 8 * 768), np.float32)
    wuq = np.empty((2, 8, 128, 256), np.float32)
    wukv = np.empty((2, 128, 512), np.float32)
    wo = np.empty((2, 8, 128, 1024), np.float32)
    f = np.arange(128)
    for l in range(2):
        w = inp["w_in"][l]
        we = _gather_cols(w, idx)
        win[l] = we.reshape(8, 128, NCH_IN, 128).transpose(2, 1, 0, 3).reshape(NCH_IN, 128, 1024)
        vcols = np.concatenate([NA_V + np.arange(384), DA_V + np.arange(384)])
        wv[l] = w[:, vcols].reshape(8, 128, 768).transpose(1, 0, 2).reshape(128, 8 * 768)
        uq = inp["mla_w_uq"][l]
        for h in range(4):
            im = np.where(f < 96, h * 96 + f, -1)
            ip = np.where((f >= 64) & (f < 80), h * 96 + f + 16, np.where((f >= 80) & (f < 96), h * 96 + f - 16, -1))
            for k, ii in ((0, im), (1, ip)):
                m = _gather_cols(uq, ii)
                wuq[l, h * 2 + k] = m.reshape(2, 128, 128).transpose(1, 0, 2).reshape(128, 256)
        ukv = inp["mla_w_ukv"][l]
        kc = np.concatenate([h * 128 + np.arange(64) for h in range(4)])
        vc = np.concatenate([h * 128 + 64 + np.arange(64) for h in range(4)])
        wukv[l] = np.concatenate([ukv[:, kc], ukv[:, vc]], axis=1)
        wo[l] = inp["w_out"][l].reshape(8, 128, 8, 128).transpose(2, 1, 0, 3).reshape(8, 128, 1024)
    sh["win"] = win.reshape(2 * NCH_IN * 128, 1024)
    sh["wv"] = wv.reshape(2 * 128, 8 * 768)
    sh["wuq"] = wuq.reshape(2 * 8 * 128, 256)
    sh["wukv"] = wukv.reshape(2 * 128, 512)
    sh["wo"] = wo.reshape(2 * 8 * 128, 1024)
    cols = []
    for l in range(2):
        for i in range(3):
            cols.append(inp["norm_g"][l, i].reshape(8, 128).T)
    cols.append(inp["final_norm_g"].reshape(8, 128).T)
    for l in range(2):
        cols.append(inp["mla_q_norm_g"][l].reshape(2, 128).T)
    for l in range(2):
        cols.append(inp["mla_kv_norm_g"][l].reshape(1, 128).T)
    for l in range(2):
        c = np.zeros((128, 1), np.float32)
        c[:64, 0] = inp["da_subln_g"][l]
        cols.append(c)
    sh["gcols"] = np.ascontiguousarray(np.concatenate(cols, axis=1).astype(np.float32))
    lam = np.empty((2, 128), np.float32)
    for l in range(2):
        lp = inp["da_lambda"][l]
        lam[l] = np.concatenate([lp[0], lp[2], lp[1], lp[3]])
    sh["dalam"] = lam
    kc = np.arange(64)[:, None]
    qc = np.arange(64)[None, :]
    cs = np.clip(qc - 8, 0, 48)
    valid = (kc >= cs) & (kc < cs + 16)
    ci = np.clip(kc - qc + 15, 0, 30)
    rb = inp["na_rel_bias"]
    rbx = rb[:, :, ::-1, :][:, :, :, ci]
    rbx = np.where(valid[None, None, None], rbx, np.float32(NEG)).astype(np.float32)
    sh["rbx"] = np.ascontiguousarray(rbx.reshape(2 * 6 * 15 * 64, 64))
    pos = np.empty(cfg.TT, np.float32)
    for s in range(3):
        pos[cfg.start[s]:cfg.start[s] + cfg.seqn[s]] = NMETA + np.arange(cfg.seqn[s], dtype=np.float32)
        pos[cfg.R + NMETA * s:cfg.R + NMETA * (s + 1)] = np.arange(NMETA, dtype=np.float32)

    def tables(dim):
        inv = (np.float32(ROPE_THETA) ** (-(np.arange(0, dim, 2, dtype=np.float32) / np.float32(dim)))).astype(np.float32)
        ang = pos[:, None] * inv[None, :]
        return np.cos(ang).astype(np.float32).T, np.sin(ang).astype(np.float32).T

    c8, s8 = tables(8)
    t = np.zeros((2, 32, cfg.TT), np.float32)
    t[0, :] = 1.0
    t[0, 0:4] = c8
    t[0, 4:8] = c8
    t[1, 0:4] = -s8
    t[1, 4:8] = s8
    sh["ropeda"] = np.ascontiguousarray(t.reshape(64, cfg.TT))
    c32, s32 = tables(32)
    t = np.zeros((2, 32, cfg.TT), np.float32)
    t[0, 0:16] = c32
    t[0, 16:32] = c32
    t[1, 0:16] = -s32
    t[1, 16:32] = s32
    sh["ropeml"] = np.ascontiguousarray(t.reshape(64, cfg.TT))
    sh["ident"] = np.eye(128, dtype=np.float32)
    sh["metatok"] = np.ascontiguousarray(inp["meta_tokens"].astype(np.float32))
    return sh


def build_program(cfg):
    FC, R, TT = cfg.FC, cfg.R, cfg.TT
    nc = bass.Bass("TRN2", target_bir_lowering=False)
    stack = ExitStack()
    with stack:
        def din(name, shape):
            return nc.dram_tensor(name, list(shape), F32, kind="ExternalInput").ap()

        x_d = din("xtok", [R, D])
        meta_d = din("metatok", [NMETA, D])
        wgu_f = din("wgu", [4 * 2 * FC * 128, 1024])
        wd_f = din("wd", [4 * 8 * 128, FC * 128])
        win_f = din("win", [2 * NCH_IN * 128, 1024])
        wv_f = din("wv", [2 * 128, 8 * 768])
        wuq_f = din("wuq", [2 * 8 * 128, 256])
        wukv_f = din("wukv", [2 * 128, 512])
        wo_f = din("wo", [2 * 8 * 128, 1024])
        gcols_d = din("gcols", [128, 64])
        dalam_d = din("dalam", [2, 128])
        rbx_d = din("rbx", [2 * 6 * 15 * 64, 64])
        ropeda_d = din("ropeda", [64, TT])
        ropeml_d = din("ropeml", [64, TT])
        ident_d = din("ident", [128, 128])
        y_d = nc.dram_tensor("y", [R, D], F32, kind="ExternalOutput").ap()

        def dscr(name, shape, dt=BF16):
            if DEBUG and name.endswith("_s"):
                return nc.dram_tensor(name, list(shape), dt, kind="ExternalOutput").ap()
            return nc.dram_tensor(name, list(shape), dt).ap()

        wgu_b = dscr("wgu_b", [4 * 2 * FC * 128, 1024])
        wd_b = dscr("wd_b", [4 * 8 * 128, FC * 128])
        win_b = dscr("win_b", [2 * NCH_IN * 128, 1024])
        wv_b = dscr("wv_b", [2 * 128, 8 * 768])
        wuq_b = dscr("wuq_b", [2 * 8 * 128, 256])
        wukv_b = dscr("wukv_b", [2 * 128, 512])
        wo_b = dscr("wo_b", [2 * 8 * 128, 1024])
        hT_d = dscr("hT_s", [D, TT], F32)
        oT_d = dscr("oT_s", [D, TT])
        naq_d = dscr("naq_s", [384, TT])
        nak_d = dscr("nak_s", [384, TT])
        daq_d = dscr("daq_s", [384, TT])
        dak_d = dscr("dak_s", [384, TT])
        mlq_d = dscr("mlq_s", [384, TT])
        mlk_d = dscr("mlk_s", [256, TT])
        mlr_d = dscr("mlr_s", [32, TT])
        v_d = dscr("v_s", [TT, 1024])

        S = Sched(nc, stack)
        S.ps = [Buf("ps%d" % i, stack.enter_context(nc.psum_tensor("ps%d" % i, [128, 512], F32))) for i in range(8)]

        phase_bufs = []

        uniq = [0]

        def sb(st, name, shape, dt):
            uniq[0] += 1
            name = "s%d_%s" % (uniq[0], name)
            b = Buf(name, st.enter_context(nc.sbuf_tensor(name, list(shape), dt)))
            if st is not stack:
                phase_bufs.append(b)
            return b

        Bx = Buf("x", dram=True)
        Bconst = Buf("const", dram=True)
        Bw = {k: Buf("w_" + k, dram=True) for k in ("gu0", "gu1", "gu2", "gu3", "d0", "d1", "d2", "d3", "in0", "in1", "misc")}
        BhT = {}
        BoT = {}
        Bqk = Buf("qk", dram=True)
        Bv = Buf("v", dram=True)
        By = Buf("y", dram=True)

        ident = sb(stack, "ident", [128, 128], F32)
        onesb = sb(stack, "onesb", [128, 128], BF16)
        ones64 = sb(stack, "ones64", [64, 64], F32)
        sel65 = sb(stack, "sel65", [65, 64], F32)
        onesrow = sb(stack, "onesrow", [1, 64], F32)
        gcols = sb(stack, "gcols", [128, 64], F32)
        neglam = sb(stack, "neglam", [64, 2], F32)
        gsub = sb(stack, "gsub", [64, 2], F32)
        lamrow = sb(stack, "lamrow", [1, 256], F32)
        lamtmp = sb(stack, "lamtmp", [1, 128], F32)
        epscol = sb(stack, "epscol", [128, 1], F32)
        S.op("pool", lambda e: e.memset(epscol.t[:], EPS), [], [epscol])
        S.dma("sp", ident.t[:], ident_d[:, :], [Bconst], [ident], ident)
        S.dma("sp", gcols.t[:], gcols_d[:, :], [Bconst], [gcols], gcols)
        S.dma("sp", lamrow.t[0:1, :], dalam_d.rearrange("l f -> (l f)").rearrange("(o f) -> o f", o=1), [Bconst], [lamrow], lamrow)
        S.op("pool", lambda e: e.memset(onesb.t[:], 1.0), [], [onesb])
        S.op("pool", lambda e: e.memset(ones64.t[:], 1.0), [], [ones64])
        S.op("pool", lambda e: e.memset(sel65.t[:], 0.0), [], [sel65])
        S.op("pool", lambda e: e.memset(sel65.t[64:65, :], 1.0), [], [sel65])
        S.op("pool", lambda e: e.memset(onesrow.t[:], 1.0), [], [onesrow])

        def convert(dst, src, rows, key, r0=0):
            r = r0
            while r < r0 + rows:
                n = min(128, r0 + rows - r)
                S.dma("pool", dst[r:r + n, :], src[r:r + n, :], [Bconst], [Bw[key]], Bw[key])
                r += n

        def conv_ffn(lj):
            convert(wgu_b, wgu_f, 2 * FC * 128, "gu%d" % lj, lj * 2 * FC * 128)
            convert(wd_b, wd_f, 8 * 128, "d%d" % lj, lj * 8 * 128)

        conv_ffn(0)
        convert(win_b, win_f, NCH_IN * 128, "in0", 0)
        convert(wv_b, wv_f, 2 * 128, "misc")
        convert(wuq_b, wuq_f, 2 * 8 * 128, "misc")
        convert(wukv_b, wukv_f, 2 * 128, "misc")
        convert(wo_b, wo_f, 2 * 8 * 128, "misc")
        conv_ffn(1)
        conv_ffn(2)
        convert(win_b, win_f, NCH_IN * 128, "in1", NCH_IN * 128)
        conv_ffn(3)

        for l in range(2):
            lam_init = 0.8 - 0.6 * math.exp(-0.3 * l)
            A = lamrow.t[0:1, l * 128:l * 128 + 64]
            Bm = lamrow.t[0:1, l * 128 + 64:l * 128 + 128]
            S.op("dve", lambda e: e.tensor_tensor(lamtmp.t[0:1, 0:64], A, Bm, ALU.mult), [lamrow], [lamtmp])
            S.op("dve", lambda e: e.tensor_reduce(lamtmp.t[0:1, 64:66], lamtmp.t[0:1, 0:64].rearrange("o (g f) -> o g f", g=2),
                                                   mybir.AxisListType.X, ALU.add), [lamtmp], [lamtmp])
            S.op("act", lambda e: e.activation(out=lamtmp.t[0:1, 66:68], in_=lamtmp.t[0:1, 64:66], func=AF.Exp), [lamtmp], [lamtmp])
            S.op("dve", lambda e: e.tensor_tensor(lamtmp.t[0:1, 68:69], lamtmp.t[0:1, 67:68], lamtmp.t[0:1, 66:67], ALU.subtract), [lamtmp], [lamtmp])
            S.op("dve", lambda e: e.tensor_scalar(lamtmp.t[0:1, 69:70], lamtmp.t[0:1, 68:69], -lam_init, None, ALU.add), [lamtmp], [lamtmp])
            p = S.psum()
            S.op("pe", lambda e: e.matmul(p.t[0:64, 0:1], onesrow.t[0:1, 0:64], lamtmp.t[0:1, 69:70], start=True, stop=True), [onesrow, lamtmp], [p])
            S.op("dve", lambda e: e.tensor_copy(neglam.t[:, l:l + 1], p.t[0:64, 0:1]), [p], [neglam])
            S.op("dve", lambda e: e.tensor_scalar(gsub.t[:, l:l + 1], gcols.t[0:64, 62 + l:63 + l], 1.0 - lam_init, None, ALU.mult), [gcols], [gsub])

        cp_flip = [0]

        def evac(out_ap, in_ap, reads, writes):
            cp_flip[0] ^= 1
            if cp_flip[0]:
                S.op("act", lambda e: e.activation(out=out_ap, in_=in_ap, func=AF.Copy), reads, writes)
            else:
                S.op("dve", lambda e: e.tensor_copy(out_ap, in_ap), reads, writes)

        def rms_rstd(srcs, W, Dn, sqring, rsring):
            ps = S.psum()
            n = len(srcs)
            for c, (b, ap) in enumerate(srcs):
                sq = sqring.next()
                S.op("act", lambda e: e.activation(out=sq.t[:, :W], in_=ap, func=AF.Square), [b], [sq])
                S.op("pe", lambda e: e.matmul(ps.t[:, :W], onesb.t[:, :], sq.t[:, :W], start=(c == 0), stop=(c == n - 1)),
                     [sq, onesb], [ps], sig=True)
            rs = rsring.next()
            S.op("act", lambda e: e.activation(out=rs.t[:, :W], in_=ps.t[:, :W], func=AF.Ln, bias=epscol.t[:, 0:1], scale=1.0 / Dn), [ps, epscol], [rs])
            rs2 = rsring.next()
            S.op("act", lambda e: e.activation(out=rs2.t[:, :W], in_=rs.t[:, :W], func=AF.Exp, scale=-0.5), [rs], [rs2])
            return rs2

        def tok_phase(stage):
            with ExitStack() as ph:
                NS_ = 2
                h = [sb(ph, "h%d" % i, [128, 8, 512], F32) for i in range(NS_)]
                xn = [sb(ph, "xn%d" % i, [128, 8, 512], BF16) for i in range(NS_)]
                hid = [sb(ph, "hid%d" % i, [128, FC, 512], BF16) for i in range(NS_)]
                sqring = Ring([sb(ph, "sq%d" % i, [128, 512], BF16) for i in range(2)])
                rsring = Ring([sb(ph, "rs%d" % i, [128, 512], F32) for i in range(3)])
                sgring = Ring([sb(ph, "sg%d" % i, [128, 512], F32) for i in range(4)])
                guring = Ring([sb(ph, "gu%d" % i, [128, 2, 8, 128], BF16) for i in range(3)])
                wdring = Ring([sb(ph, "wdr%d" % i, [128, FC, 128], BF16) for i in range(2)])
                if stage >= 1:
                    o1 = sb(ph, "o1", [128, 8, 512], BF16)
                    woring = Ring([sb(ph, "wor%d" % i, [128, 8, 128], BF16) for i in range(2)])
                if stage == 0:
                    xsring = Ring([sb(ph, "xs%d" % i, [128, 4, 1024], F32) for i in range(1)])
                if stage <= 1:
                    winring = Ring([sb(ph, "winr%d" % i, [128, 8, 128], BF16) for i in range(2)])
                    wvt = sb(ph, "wvt", [128, 8, 768], BF16)
                    wuqt = sb(ph, "wuqt", [128, 8, 2, 128], BF16)
                    wukvt = sb(ph, "wukvt", [128, 512], BF16)
                    rda = [sb(ph, "rda%d" % i, [128, 2, 512], BF16) for i in range(NS_)]
                    rml = [sb(ph, "rml%d" % i, [96, 2, 512], BF16) for i in range(NS_)]
                    uoring = Ring([sb(ph, "uo%d" % i, [128, 512], BF16) for i in range(3)])
                    t1ring = sgring
                    t2ring = sgring
                    cq1 = sb(ph, "cq1", [128, 3, 512], F32)
                    cqn1 = sb(ph, "cqn1", [128, 3, 512], BF16)
                    cq = [cq1, cq1]
                    cqn = [cqn1, cqn1]
                    vtring = Ring([sb(ph, "vt%d" % i, [128, 1024], BF16) for i in range(2)])
                if stage == 2:
                    hn = [sb(ph, "hn%d" % i, [128, 8, 512], F32) for i in range(NS_)]
                    yring = Ring([sb(ph, "yt%d" % i, [128, 1024], F32) for i in range(2)])

                blocks = []
                for c0 in range(0, R, 1024):
                    blocks.append([(c0, 512), (c0 + 512, 512)])
                if stage <= 1:
                    blocks.append([(R, 3 * NMETA)])

                hT_v = hT_d.rearrange("(c p) t -> p c t", p=128)
                oT_v = oT_d.rearrange("(c p) t -> p c t", p=128)

                def ffn(lj, subs, gbase):
                    for i, (c0, W) in enumerate(subs):
                        rs = rms_rstd([(h[i], h[i].t[:, c, :W]) for c in range(8)], W, D, sqring, rsring)
                        if SUB < 1.4:
                            continue
                        for c in range(8):
                            S.op("dve", lambda e: e.scalar_tensor_tensor(out=xn[i].t[:, c, :W], in0=h[i].t[:, c, :W],
                                                                          scalar=gcols.t[:, gbase + c:gbase + c + 1], in1=rs.t[:, :W],
                                                                          op0=ALU.mult, op1=ALU.mult), [h[i], rs, gcols], [xn[i]])
                    if SUB < 1.6:
                        return
                    Bg = Bw["gu%d" % lj]
                    Bd = Bw["d%d" % lj]
                    for fc in range(FC):
                        w = guring.next()
                        for k in range(2):
                            r0 = ((lj * 2 + k) * FC + fc) * 128
                            S.dma("sp", w.t[:, k].rearrange("p c f -> p (c f)"), wgu_b[r0:r0 + 128, :], [Bg], [w], w)
                        for i, (c0, W) in enumerate(subs):
                            pg = S.psum()
                            pu = S.psum()
                            for c in range(8):
                                S.op("pe", lambda e: e.matmul(pg.t[:, :W], w.t[:, 0, c, :], xn[i].t[:, c, :W], start=(c == 0), stop=(c == 7)),
                                     [w, xn[i]], [pg], sig=(c == 7))
                            for c in range(8):
                                S.op("pe", lambda e: e.matmul(pu.t[:, :W], w.t[:, 1, c, :], xn[i].t[:, c, :W], start=(c == 0), stop=(c == 7)),
                                     [w, xn[i]], [pu], sig=(c == 7))
                            sg = sgring.next()
                            S.op("act", lambda e: e.activation(out=sg.t[:, :W], in_=pg.t[:, :W], func=AF.Silu), [pg], [sg])
                            S.op("dve", lambda e: e.tensor_tensor(hid[i].t[:, fc, :W], pu.t[:, :W], sg.t[:, :W], ALU.mult), [pu, sg], [hid[i]])
                    if SUB < 1.8:
                        return
                    for dc in range(8):
                        w = wdring.next()
                        r0 = (lj * 8 + dc) * 128
                        S.dma("sp", w.t[:].rearrange("p c f -> p (c f)"), wd_b[r0:r0 + 128, :], [Bd], [w], w)
                        for i, (c0, W) in enumerate(subs):
                            py = S.psum()
                            for fc in range(FC):
                                S.op("pe", lambda e: e.matmul(py.t[:, :W], w.t[:, fc, :], hid[i].t[:, fc, :W], start=(fc == 0), stop=(fc == FC - 1)),
                                     [w, hid[i]], [py], sig=(fc == FC - 1))
                            S.op("dve", lambda e: e.scalar_tensor_tensor(out=h[i].t[:, dc, :W], in0=py.t[:, :W], scalar=0.5, in1=h[i].t[:, dc, :W],
                                                                          op0=ALU.mult, op1=ALU.add), [py, h[i]], [h[i]])

                def wout(l, subs):
                    for i, (c0, W) in enumerate(subs):
                        S.dma("sp", o1.t[:, :, :W], oT_v[:, :, c0:c0 + W], [BoT[c0]], [o1], o1)
                        for dc in range(8):
                            w = woring.next()
                            r0 = (l * 8 + dc) * 128
                            S.dma("sp", w.t[:].rearrange("p c f -> p (c f)"), wo_b[r0:r0 + 128, :], [Bw["misc"]], [w], w)
                            py = S.psum()
                            for fc in range(8):
                                S.op("pe", lambda e: e.matmul(py.t[:, :W], w.t[:, fc, :], o1.t[:, fc, :W], start=(fc == 0), stop=(fc == 7)),
                                     [w, o1], [py], sig=(fc == 7))
                            S.op("dve", lambda e: e.tensor_tensor(h[i].t[:, dc, :W], py.t[:, :W], h[i].t[:, dc, :W], ALU.add), [py, h[i]], [h[i]])

                def proj_in(l, subs):
                    gb = (l * 3 + 1) * 8
                    for i, (c0, W) in enumerate(subs):
                        rs = rms_rstd([(h[i], h[i].t[:, c, :W]) for c in range(8)], W, D, sqring, rsring)
                        for c in range(8):
                            S.op("dve", lambda e: e.scalar_tensor_tensor(out=xn[i].t[:, c, :W], in0=h[i].t[:, c, :W],
                                                                          scalar=gcols.t[:, gb + c:gb + c + 1], in1=rs.t[:, :W],
                                                                          op0=ALU.mult, op1=ALU.mult), [h[i], rs, gcols], [xn[i]])
                        for gq in range(4):
                            S.dma("pool", rda[i].t[gq * 32:(gq + 1) * 32, :, :W],
                                  ropeda_d.rearrange("(a r) t -> r a t", a=2)[:, :, c0:c0 + W], [Bconst], [rda[i]], rda[i])
                        for base in (0, 64):
                            S.dma("pool", rml[i].t[base:base + 32, :, :W],
                                  ropeml_d.rearrange("(a r) t -> r a t", a=2)[:, :, c0:c0 + W], [Bconst], [rml[i]], rml[i])
                    Bi = Bw["in%d" % l]
                    S.dma("sp", wvt.t[:].rearrange("p c f -> p (c f)"), wv_b[l * 128:(l + 1) * 128, :], [Bw["misc"]], [wvt], wvt)
                    S.dma("sp", wuqt.t[:], wuq_b[l * 1024:(l + 1) * 1024, :].rearrange("(j p) (c f) -> p j c f", p=128, c=2),
                          [Bw["misc"]], [wuqt], wuqt)
                    S.dma("sp", wukvt.t[:], wukv_b[l * 128:(l + 1) * 128, :], [Bw["misc"]], [wukvt], wukvt)

                    def load_chunk(j):
                        w = winring.next()
                        r0 = (l * NCH_IN + j) * 128
                        S.dma("sp", w.t[:].rearrange("p c f -> p (c f)"), win_b[r0:r0 + 128, :], [Bi], [w], w)
                        return w

                    def mm_chunk(w, i, W, M):
                        p = S.psum()
                        for c in range(8):
                            S.op("pe", lambda e: e.matmul(p.t[:M, :W], w.t[:, c, :M], xn[i].t[:, c, :W], start=(c == 0), stop=(c == 7)),
                                 [w, xn[i]], [p], sig=(c == 7))
                        return p

                    def store_rows(dst, row0, M, uo, c0, W):
                        S.dma("pool", dst[row0:row0 + M, c0:c0 + W], uo.t[:M, :W], [uo], [Bqk], uo)

                    for j in range(6):
                        w = load_chunk(j)
                        for i, (c0, W) in enumerate(subs):
                            p = mm_chunk(w, i, W, 128)
                            uo = uoring.next()
                            evac(uo.t[:, :W], p.t[:, :W], [p], [uo])
                            store_rows(naq_d if j < 3 else nak_d, (j % 3) * 128, 128, uo, c0, W)
                    for grp, dst in ((6, daq_d), (12, dak_d)):
                        for jj in range(3):
                            wm = load_chunk(grp + jj)
                            wp = load_chunk(grp + 3 + jj)
                            for i, (c0, W) in enumerate(subs):
                                pm = mm_chunk(wm, i, W, 128)
                                pp = mm_chunk(wp, i, W, 128)
                                t1 = t1ring.next()
                                t2 = t2ring.next()
                                S.op("dve", lambda e: e.tensor_tensor(t1.t[:, :W], pm.t[:, :W], rda[i].t[:, 0, :W], ALU.mult), [pm, rda[i]], [t1])
                                S.op("dve", lambda e: e.tensor_tensor(t2.t[:, :W], pp.t[:, :W], rda[i].t[:, 1, :W], ALU.mult), [pp, rda[i]], [t2])
                                uo = uoring.next()
                                S.op("pool", lambda e: e.tensor_tensor(uo.t[:, :W], t1.t[:, :W], t2.t[:, :W], ALU.add), [t1, t2], [uo])
                                store_rows(dst, jj * 128, 128, uo, c0, W)
                    wm = load_chunk(21)
                    wp = load_chunk(22)
                    for i, (c0, W) in enumerate(subs):
                        pm = mm_chunk(wm, i, W, 32)
                        pp = mm_chunk(wp, i, W, 32)
                        t1 = t1ring.next()
                        t2 = t2ring.next()
                        S.op("dve", lambda e: e.tensor_tensor(t1.t[:32, :W], pm.t[:32, :W], rml[i].t[0:32, 0, :W], ALU.mult), [pm, rml[i]], [t1])
                        S.op("dve", lambda e: e.tensor_tensor(t2.t[:32, :W], pp.t[:32, :W], rml[i].t[0:32, 1, :W], ALU.mult), [pp, rml[i]], [t2])
                        uo = uoring.next()
                        S.op("pool", lambda e: e.tensor_tensor(uo.t[:32, :W], t1.t[:32, :W], t2.t[:32, :W], ALU.add), [t1, t2], [uo])
                        store_rows(mlr_d, 0, 32, uo, c0, W)
                    for i, (c0, W) in enumerate(subs):
                        for jj in range(3):
                            w = load_chunk(18 + jj)
                            p = mm_chunk(w, i, W, 128)
                            evac(cq[i].t[:, jj, :W], p.t[:, :W], [p], [cq[i]])
                        rs = rms_rstd([(cq[i], cq[i].t[:, c, :W]) for c in range(2)], W, 256, sqring, rsring)
                        for c in range(2):
                            S.op("dve", lambda e: e.scalar_tensor_tensor(out=cqn[i].t[:, c, :W], in0=cq[i].t[:, c, :W],
                                                                          scalar=gcols.t[:, 56 + 2 * l + c:57 + 2 * l + c], in1=rs.t[:, :W],
                                                                          op0=ALU.mult, op1=ALU.mult), [cq[i], rs, gcols], [cqn[i]])
                        rs = rms_rstd([(cq[i], cq[i].t[:, 2, :W])], W, 128, sqring, rsring)
                        S.op("dve", lambda e: e.scalar_tensor_tensor(out=cqn[i].t[:, 2, :W], in0=cq[i].t[:, 2, :W],
                                                                      scalar=gcols.t[:, 60 + l:61 + l], in1=rs.t[:, :W],
                                                                      op0=ALU.mult, op1=ALU.mult), [cq[i], rs, gcols], [cqn[i]])
                        for hh in range(4):
                            pm = S.psum()
                            pp = S.psum()
                            for c in range(2):
                                S.op("pe", lambda e: e.matmul(pm.t[:96, :W], wuqt.t[:, 2 * hh, c, 0:96], cqn[i].t[:, c, :W], start=(c == 0), stop=(c == 1)),
                                     [wuqt, cqn[i]], [pm], sig=(c == 1))
                            for c in range(2):
                                S.op("pe", lambda e: e.matmul(pp.t[:96, :W], wuqt.t[:, 2 * hh + 1, c, 0:96], cqn[i].t[:, c, :W], start=(c == 0), stop=(c == 1)),
                                     [wuqt, cqn[i]], [pp], sig=(c == 1))
                            uo = uoring.next()
                            t1 = t1ring.next()
                            t2 = t2ring.next()
                            S.op("act", lambda e: e.activation(out=uo.t[0:64, :W], in_=pm.t[0:64, :W], func=AF.Copy), [pm], [uo])
                            S.op("dve", lambda e: e.tensor_tensor(t1.t[64:96, :W], pm.t[64:96, :W], rml[i].t[64:96, 0, :W], ALU.mult), [pm, rml[i]], [t1])
                            S.op("dve", lambda e: e.tensor_tensor(t2.t[64:96, :W], pp.t[64:96, :W], rml[i].t[64:96, 1, :W], ALU.mult), [pp, rml[i]], [t2])
                            S.op("pool", lambda e: e.tensor_tensor(uo.t[64:96, :W], t1.t[64:96, :W], t2.t[64:96, :W], ALU.add), [t1, t2, uo], [uo])
                            store_rows(mlq_d, hh * 96, 96, uo, c0, W)
                        for kk in range(2):
                            p = S.psum()
                            S.op("pe", lambda e: e.matmul(p.t[:, :W], wukvt.t[:, kk * 128:(kk + 1) * 128], cqn[i].t[:, 2, :W], start=True, stop=True),
                                 [wukvt, cqn[i]], [p])
                            uo = uoring.next()
                            evac(uo.t[:, :W], p.t[:, :W], [p], [uo])
                            store_rows(mlk_d, kk * 128, 128, uo, c0, W)
                        ng = (W + 127) // 128
                        for g in range(ng):
                            gw = min(128, W - g * 128)
                            vt = vtring.next()
                            for half in range(2):
                                p = S.psum()
                                for c in range(8):
                                    S.op("pe", lambda e: e.matmul(p.t[:gw, :384], xn[i].t[:, c, g * 128:g * 128 + gw], wvt.t[:, c, half * 384:(half + 1) * 384],
                                                                   start=(c == 0), stop=(c == 7)), [xn[i], wvt], [p], sig=(c == 7))
                                evac(vt.t[:gw, half * 384:(half + 1) * 384], p.t[:gw, :384], [p], [vt])
                            p = S.psum()
                            S.op("pe", lambda e: e.matmul(p.t[:gw, :256], cqn[i].t[:, 2, g * 128:g * 128 + gw], wukvt.t[:, 256:512], start=True, stop=True),
                                 [cqn[i], wukvt], [p])
                            evac(vt.t[:gw, 768:1024], p.t[:gw, :256], [p], [vt])
                            S.dma("pool", v_d[c0 + g * 128:c0 + g * 128 + gw, :], vt.t[:gw, :], [vt], [Bv], vt)

                for bi, subs in enumerate(blocks):
                    is_meta = subs[0][0] >= R
                    for i, (c0, W) in enumerate(subs):
                        key = c0
                        if key not in BhT:
                            BhT[key] = Buf("hT%d" % key, dram=True)
                            BoT[key] = Buf("oT%d" % key, dram=True)
                        if stage == 0:
                            xs = xsring.next()
                            if is_meta:
                                for s in range(3):
                                    S.dma("sp", xs.t[s * NMETA:(s + 1) * NMETA, 0, :], meta_d[:, :], [Bconst], [xs], xs)
                            else:
                                S.dma("sp", xs.t[:, :, :], x_d[c0:c0 + W, :].rearrange("(g p) d -> p g d", p=128), [Bx], [xs], xs)
                            ng = (W + 127) // 128
                            for c in range(8):
                                p = S.psum()
                                for g in range(ng):
                                    gw = min(128, W - g * 128)
                                    S.op("pe", lambda e: e.transpose(p.t[:, g * 128:g * 128 + gw], xs.t[:gw, g, c * 128:(c + 1) * 128], ident.t[:gw, :gw]),
                                         [xs, ident], [p], sig=(g == ng - 1))
                                evac(h[i].t[:, c, :W], p.t[:, :W], [p], [h[i]])
                        else:
                            S.dma("sp", h[i].t[:, :, :W], hT_v[:, :, c0:c0 + W], [BhT[key]], [h[i]], h[i])
                    if stage >= 1:
                        wout(stage - 1, subs)
                        ffn((stage - 1) * 2 + 1, subs, ((stage - 1) * 3 + 2) * 8)
                    if stage <= 1:
                        if SUB >= 1.2:
                            ffn(stage * 2, subs, (stage * 3) * 8)
                        if SUB >= 3:
                            proj_in(stage, subs)
                        for i, (c0, W) in enumerate(subs):
                            S.dma("pool", hT_v[:, :, c0:c0 + W], h[i].t[:, :, :W], [h[i]], [BhT[c0]], h[i])
                    else:
                        for i, (c0, W) in enumerate(subs):
                            rs = rms_rstd([(h[i], h[i].t[:, c, :W]) for c in range(8)], W, D, sqring, rsring)
                            for c in range(8):
                                S.op("dve", lambda e: e.scalar_tensor_tensor(out=hn[i].t[:, c, :W], in0=h[i].t[:, c, :W],
                                                                              scalar=gcols.t[:, 48 + c:49 + c], in1=rs.t[:, :W],
                                                                              op0=ALU.mult, op1=ALU.mult), [h[i], rs, gcols], [hn[i]])
                            for g in range(W // 128):
                                yt = yring.next()
                                for half in range(2):
                                    p = S.psum()
                                    for cc in range(4):
                                        c = half * 4 + cc
                                        S.op("pe", lambda e: e.transpose(p.t[:, cc * 128:(cc + 1) * 128], hn[i].t[:, c, g * 128:(g + 1) * 128], ident.t[:, :]),
                                             [hn[i], ident], [p], sig=(cc == 3))
                                    evac(yt.t[:, half * 512:(half + 1) * 512], p.t[:, :], [p], [yt])
                                S.dma("pool", y_d[c0 + g * 128:c0 + (g + 1) * 128, :], yt.t[:, :], [yt], [By], yt)
                S.barrier()
                S.release(phase_bufs)
                del phase_bufs[:]

        def att_phase(l, last):
            with ExitStack() as ph:
                NT = cfg.NMAX // 128
                ktring = Ring([sb(ph, "kt%d" % i, [128, cfg.NMAX], BF16) for i in range(2)])
                kmring = Ring([sb(ph, "km%d" % i, [128, NMETA], BF16) for i in range(2)])
                vring = Ring([sb(ph, "vv%d" % i, [128, NT, 128], BF16) for i in range(2)])
                vmring = Ring([sb(ph, "vm%d" % i, [NMETA, 128], BF16) for i in range(2)])
                qaring = Ring([sb(ph, "qa%d" % i, [128, 512], BF16) for i in range(3)])
                q0ring = Ring([sb(ph, "q0_%d" % i, [128, 512], BF16) for i in range(2)])
                q1ring = Ring([sb(ph, "q1_%d" % i, [128, 512], BF16) for i in range(2)])
                ptring = Ring([sb(ph, "pt%d" % i, [128, 512], BF16) for i in range(6)])
                tmring = Ring([sb(ph, "tm%d" % i, [128, 512], F32) for i in range(3)])
                mb = sb(ph, "mb", [128, 3, 8, 512], F32)
                osring = Ring([sb(ph, "os%d" % i, [65, 512], F32) for i in range(4)])
                rdring = Ring([sb(ph, "rd%d" % i, [64, 512], F32) for i in range(8)])
                aring = Ring([sb(ph, "aa%d" % i, [64, 512], F32) for i in range(6)])
                ooring = Ring([sb(ph, "oo%d" % i, [64, 512], BF16) for i in range(4)])
                spool = Ring(S.ps[0:4])
                opool = Ring(S.ps[4:8])
                for b in ktring.bufs + kmring.bufs + qaring.bufs + q0ring.bufs + q1ring.bufs:
                    S.op("pool", lambda e: e.memset(b.t[:, :], 0.0), [], [b])
                for b in vring.bufs:
                    S.op("pool", lambda e: e.memset(b.t[:, :, 64:128], 1.0), [], [b])
                for b in vmring.bufs:
                    S.op("pool", lambda e: e.memset(b.t[:, 64:128], 1.0), [], [b])

                pending = []

                def drain(i):
                    while pending and pending[0][0] <= i:
                        pending.pop(0)[1]()

                def run_qtile(streams, Nq, ktl, scale):
                    O = [opool.next() for _ in streams]
                    n = len(ktl)

                    def qk(i):
                        res = []
                        Kb, kfn, Vb, vap, nk, mbap = ktl[i]
                        for Q in streams:
                            ps = spool.next()
                            S.op("pe", lambda e: e.matmul(ps.t[:nk, :Nq], kfn(), Q.t[:, :Nq], start=True, stop=True), [Kb, Q], [ps])
                            pt = ptring.next()
                            if mbap is not None:
                                tm = tmring.next()
                                S.op("dve", lambda e: e.scalar_tensor_tensor(out=tm.t[:nk, :Nq], in0=ps.t[:nk, :Nq], scalar=scale, in1=mbap[:nk, :Nq],
                                                                              op0=ALU.mult, op1=ALU.add), [ps, mb], [tm])
                                S.op("act", lambda e: e.activation(out=pt.t[:nk, :Nq], in_=tm.t[:nk, :Nq], func=AF.Exp), [tm], [pt])
                            else:
                                S.op("act", lambda e: e.activation(out=pt.t[:nk, :Nq], in_=ps.t[:nk, :Nq], func=AF.Exp, scale=scale), [ps], [pt])
                            res.append(pt)
                        return res

                    cur = qk(0)
                    for i in range(n):
                        nxt = qk(i + 1) if i + 1 < n else None
                        Kb, kfn, Vb, vap, nk, mbap = ktl[i]
                        for si in range(len(streams)):
                            S.op("pe", lambda e: e.matmul(O[si].t[:128, :Nq], vap, cur[si].t[:nk, :Nq], start=(i == 0), stop=(i == n - 1)),
                                 [Vb, cur[si]], [O[si]], sig=True)
                        cur = nxt
                        drain(i)
                    drain(10 ** 9)
                    return O

                def post_stages(kind, O, Nq, oo, orow, qc0, key):
                    st = []
                    osb = [osring.next() for _ in O]
                    rd = [rdring.next() for _ in O]

                    def s_evac():
                        for si, Ob in enumerate(O):
                            S.op("dve", lambda e: e.tensor_copy(osb[si].t[:65, :Nq], Ob.t[:65, :Nq]), [Ob], [osb[si]])

                    def s_den(si):
                        def f():
                            Ob = O[si]
                            S.op("pe", lambda e: e.matmul(Ob.t[:64, :Nq], sel65.t[:65, :64], osb[si].t[:65, :Nq], start=True, stop=True), [sel65, osb[si]], [Ob])
                            S.op("dve", lambda e: e.reciprocal(rd[si].t[:64, :Nq], Ob.t[:64, :Nq]), [Ob], [rd[si]])
                        return f

                    def s_store():
                        S.dma("pool", oT_d[orow:orow + 64, qc0:qc0 + Nq], oo.t[:64, :Nq], [oo], [BoT[key]], oo)

                    st.append((1, s_evac))
                    st.append((3, s_den(0)))
                    if kind != "da":
                        def s_fin():
                            S.op("dve", lambda e: e.tensor_tensor(oo.t[:64, :Nq], osb[0].t[:64, :Nq], rd[0].t[:64, :Nq], ALU.mult), [osb[0], rd[0]], [oo])
                            s_store()
                        st.append((8, s_fin))
                        return st
                    st.append((6, s_den(1)))
                    a = aring.next()
                    b2 = aring.next()
                    sq = aring.next()
                    rs = rdring.next()
                    rs2 = rdring.next()

                    def s_comb():
                        S.op("dve", lambda e: e.tensor_tensor(a.t[:64, :Nq], osb[0].t[:64, :Nq], rd[0].t[:64, :Nq], ALU.mult), [osb[0], rd[0]], [a])
                        S.op("pool", lambda e: e.tensor_tensor(b2.t[:64, :Nq], osb[1].t[:64, :Nq], rd[1].t[:64, :Nq], ALU.mult), [osb[1], rd[1]], [b2])
                        S.op("dve", lambda e: e.scalar_tensor_tensor(out=a.t[:64, :Nq], in0=b2.t[:64, :Nq], scalar=neglam.t[:64, l:l + 1], in1=a.t[:64, :Nq],
                                                                      op0=ALU.mult, op1=ALU.add), [a, b2, neglam], [a])
                        S.op("pool", lambda e: e.tensor_tensor(sq.t[:64, :Nq], a.t[:64, :Nq], a.t[:64, :Nq], ALU.mult), [a], [sq])

                    def s_ss():
                        Ob = O[0]
                        S.op("pe", lambda e: e.matmul(Ob.t[:64, :Nq], ones64.t[:64, :64], sq.t[:64, :Nq], start=True, stop=True), [ones64, sq], [Ob])
                        S.op("act", lambda e: e.activation(out=rs.t[:64, :Nq], in_=Ob.t[:64, :Nq], func=AF.Ln, bias=epscol.t[:64, 0:1], scale=1.0 / 64), [Ob, epscol], [rs])
                        S.op("act", lambda e: e.activation(out=rs2.t[:64, :Nq], in_=rs.t[:64, :Nq], func=AF.Exp, scale=-0.5), [rs], [rs2])
                        S.op("dve", lambda e: e.scalar_tensor_tensor(out=oo.t[:64, :Nq], in0=a.t[:64, :Nq], scalar=gsub.t[:64, l:l + 1], in1=rs2.t[:64, :Nq],
                                                                      op0=ALU.mult, op1=ALU.mult), [a, rs2, gsub], [oo])
                        s_store()

                    st.append((11, s_comb))
                    st.append((14, s_ss))
                    return st

                heads = [("na", hh) for hh in range(6)] + [("da", hh) for hh in range(6)] + [("ml", hh) for hh in range(4)]
                for kind, hh in heads:
                    if kind == "na":
                        scale = 64 ** -0.5
                        S.op("pool", lambda e: e.memset(mb.t[:].rearrange("p a b c -> p (a b c)"), NEG), [], [mb])
                        for v in range(3):
                            delta = (0, 4, 8)[v]
                            for kr in range(16):
                                qs = []
                                for qr in range(8):
                                    if v == 0:
                                        lo = max(qr - 4, 0)
                                    elif v == 1:
                                        lo = qr
                                    else:
                                        lo = 8 + min(qr - 4, 0)
                                    if lo <= kr < lo + 8:
                                        qs.append(qr)
                                if not qs:
                                    continue
                                qa, qb = qs[0], qs[-1] + 1
                                dra = 7 - kr + qa + delta
                                row0 = ((l * 6 + hh) * 15 + dra) * 64
                                src = rbx_d[row0:row0 + (qb - qa) * 64, :].rearrange("(q k) c -> k q c", k=64)
                                krb = kr % 2
                                S.dma("sp", mb.t[krb * 64:(krb + 1) * 64, v, kr // 2, qa * 64:qb * 64].rearrange("p (q c) -> p q c", c=64),
                                      src, [Bconst], [mb], mb)
                        qsrc, ksrc, row0, nrows, voff = naq_d, nak_d, hh * 64, 64, hh * 64
                    elif kind == "da":
                        scale = 32 ** -0.5
                        qsrc, ksrc, row0, nrows, voff = daq_d, dak_d, hh * 64, 64, 384 + hh * 64
                    else:
                        scale = 96 ** -0.5
                        qsrc, ksrc, row0, nrows, voff = mlq_d, mlk_d, hh * 96, 96, 768 + hh * 64
                    orow = voff
                    for s in range(3):
                        n = cfg.seqn[s]
                        st = cfg.start[s]
                        mcol = R + NMETA * s
                        nt = n // 128
                        Kt = ktring.next()
                        Km = kmring.next()
                        Vv = vring.next()
                        Vm = vmring.next()
                        if kind == "ml":
                            S.dma("sp", Kt.t[0:64, :n], mlk_d[hh * 64:(hh + 1) * 64, st:st + n], [Bqk], [Kt], Kt)
                            S.dma("sp", Kt.t[64:96, :n], mlr_d[0:32, st:st + n], [Bqk], [Kt], Kt)
                            S.dma("sp", Km.t[0:64, :], mlk_d[hh * 64:(hh + 1) * 64, mcol:mcol + NMETA], [Bqk], [Km], Km)
                            S.dma("sp", Km.t[64:96, :], mlr_d[0:32, mcol:mcol + NMETA], [Bqk], [Km], Km)
                        else:
                            S.dma("sp", Kt.t[0:64, :n], ksrc[row0:row0 + 64, st:st + n], [Bqk], [Kt], Kt)
                            S.dma("sp", Km.t[0:64, :], ksrc[row0:row0 + 64, mcol:mcol + NMETA], [Bqk], [Km], Km)
                        for t0 in range(0, nt, 16):
                            t1_ = min(nt, t0 + 16)
                            S.dma("sp", Vv.t[:, t0:t1_, 0:64],
                                  v_d[st + t0 * 128:st + t1_ * 128, voff:voff + 64].rearrange("(t p) e -> p t e", p=128), [Bv], [Vv], Vv)
                        S.dma("sp", Vm.t[:, 0:64], v_d[mcol:mcol + NMETA, voff:voff + 64], [Bv], [Vm], Vm)

                        qtiles = [(st + q0, 512, q0 // 512) for q0 in range(0, n, 512)]
                        if not last:
                            qtiles.append((mcol, NMETA, -1))
                        nblk = n // 512
                        rows = n // 64
                        for (qc0, Nq, qb_) in qtiles:
                            if kind == "da":
                                Q0 = q0ring.next()
                                Q1 = q1ring.next()
                                S.dma("sp", Q0.t[0:32, :Nq], qsrc[row0:row0 + 32, qc0:qc0 + Nq], [Bqk], [Q0], Q0)
                                S.dma("sp", Q1.t[32:64, :Nq], qsrc[row0 + 32:row0 + 64, qc0:qc0 + Nq], [Bqk], [Q1], Q1)
                                streams = [Q0, Q1]
                            else:
                                Q = qaring.next()
                                S.dma("sp", Q.t[:nrows, :Nq], qsrc[row0:row0 + nrows, qc0:qc0 + Nq], [Bqk], [Q], Q)
                                streams = [Q]
                            meta_kt = (Km, (lambda Km=Km: Km.t[:, 0:NMETA]), Vm, Vm.t[:NMETA, 0:128], NMETA, None)
                            if kind == "na":
                                if qb_ < 0:
                                    ktl = [meta_kt]
                                else:
                                    v = 0 if qb_ == 0 else (2 if qb_ == nblk - 1 else 1)
                                    w0 = min(max(8 * qb_ - 4, 0), rows - 16)
                                    kt0 = w0 // 2
                                    ktl = []
                                    for j in range(8):
                                        kt = kt0 + j
                                        ktl.append((Kt, (lambda kt=kt, Kt=Kt: Kt.t[:, kt * 128:(kt + 1) * 128]), Vv, Vv.t[:, kt, 0:128], 128,
                                                    mb.t[:, v, j, :]))
                                    ktl.append(meta_kt)
                            else:
                                ktl = [(Kt, (lambda kt=kt, Kt=Kt: Kt.t[:, kt * 128:(kt + 1) * 128]), Vv, Vv.t[:, kt, 0:128], 128, None)
                                       for kt in range(nt)]
                                ktl.append(meta_kt)
                            O = run_qtile(streams, Nq, ktl, scale)
                            oo = ooring.next()
                            key = qc0 if qc0 < R else R
                            pending.extend(post_stages(kind, O, Nq, oo, orow, qc0, key))
                drain(10 ** 9)
                S.barrier()
                S.release(phase_bufs)
                del phase_bufs[:]

        if STOP >= 1:
            tok_phase(0)
        if STOP >= 2:
            att_phase(0, False)
        if STOP >= 3:
            tok_phase(1)
        if STOP >= 4:
            att_phase(1, True)
        if STOP >= 5:
            tok_phase(2)
        S.barrier()
    return nc


_CACHE = {}


def run(cfg, inp):
    sh = _host_shared(cfg, inp)
    if "nc" not in _CACHE or _CACHE.get("cfg") != (cfg.NP, cfg.NS, cfg.DFF):
        _CACHE["nc"] = build_program(cfg)
        _CACHE["cfg"] = (cfg.NP, cfg.NS, cfg.DFF)
    nc = _CACHE["nc"]
    xp, xs = inp["x_prompt"], inp["x_sample"]
    in_maps = []
    for c in range(NCORES):
        m = dict(sh)
        m["xtok"] = np.ascontiguousarray(np.concatenate([xp[c], xs[2 * c], xs[2 * c + 1]], axis=0).astype(np.float32))
        in_maps.append(m)
    res = run_bass_kernel_spmd(nc, in_maps, core_ids=list(range(NCORES)))
    _CACHE["res"] = res
    yp = np.empty(xp.shape, np.float32)
    ys = np.empty(xs.shape, np.float32)
    for c in range(NCORES):
        y = res.results[c]["y"]
        yp[c] = y[:cfg.NP]
        ys[2 * c] = y[cfg.NP:cfg.NP + cfg.NS]
        ys[2 * c + 1] = y[cfg.NP + cfg.NS:]
    return yp, ys


def kernel(**inputs):
    inp = {k: np.asarray(v) for k, v in inputs.items()}
    cfg = Cfg(inp["x_prompt"].shape[1], inp["x_sample"].shape[1], inp["ffn_w_gate"].shape[-1])
    return run(cfg, inp)
```

```python
import math
from contextlib import ExitStack
import numpy as np
import concourse.bass as bass
import concourse.mybir as mybir
from concourse.bass_utils import run_bass_kernel_spmd

F32 = mybir.dt.float32
BF16 = mybir.dt.bfloat16
AF = mybir.ActivationFunctionType
ALU = mybir.AluOpType

D = 1024
NMETA = 16
EPS = 1e-6
NEG = -30000.0
NCORES = 8
ROPE_THETA = 500000.0
DEBUG = False
STOP = 5
SUB = 9


class Buf:
    def __init__(self, name, t=None, dram=False):
        self.name = name
        self.t = t
        self.dram = dram
        self.w = {}
        self.rd = {}
        self.sem = None
        self.cnt = 0

    def add_rd(self, tok):
        sid = id(tok[0])
        if self.rd.get(sid, (None, 0))[1] < tok[1]:
            self.rd[sid] = tok

    def set_w(self, tok):
        sid = id(tok[0])
        if self.w.get(sid, (None, 0))[1] < tok[1]:
            self.w[sid] = tok


class Ring:
    def __init__(self, bufs):
        self.bufs = bufs
        self.i = 0

    def next(self):
        b = self.bufs[self.i % len(self.bufs)]
        self.i += 1
        return b


class Sched:
    def __init__(self, nc, stack):
        self.nc = nc
        self.stack = stack
        self.E = {"pe": nc.tensor, "act": nc.scalar, "dve": nc.vector, "pool": nc.gpsimd, "sp": nc.sync}
        self.esem = {}
        for e in ("pe", "act", "dve", "pool"):
            self.esem[e] = stack.enter_context(nc.semaphore("es_" + e))
        self.cnt = {e: 0 for e in self.E}
        self.seen = {e: {} for e in self.E}
        self.semcnt = {}
        self.free_sems = []
        self.nsem = 4
        self.ps = None
        self.psi = 0

    def _waits(self, eng, reads, writes):
        toks = {}
        own = self.esem.get(eng)

        def add(d, raw):
            for sid, (sem, val) in d.items():
                if sem is own and (eng == "pe" or (not raw and eng != "pool")):
                    continue
                if toks.get(sid, (None, 0))[1] < val:
                    toks[sid] = (sem, val)

        for b in reads:
            add(b.w, True)
        for b in writes:
            add(b.w, False)
            add(b.rd, False)
        e = self.E[eng]
        seen = self.seen[eng]
        for sid, (sem, val) in toks.items():
            if seen.get(sid, 0) >= val:
                continue
            seen[sid] = val
            e.wait_ge(sem, val)

    def op(self, eng, fn, reads=(), writes=(), sig=True):
        self._waits(eng, reads, writes)
        ins = fn(self.E[eng])
        if sig:
            self.cnt[eng] += 1
            ins.then_inc(self.esem[eng], 1)
            idx = self.cnt[eng]
        else:
            idx = self.cnt[eng] + 1
        tok = (self.esem[eng], idx)
        for b in reads:
            b.add_rd(tok)
        for b in writes:
            b.set_w(tok)

    def dma(self, eng, out, in_, reads, writes, sb):
        self._waits(eng, reads, writes)
        if sb.sem is None:
            if self.free_sems:
                sb.sem, sb.cnt = self.free_sems.pop()
            else:
                sb.sem = self.stack.enter_context(self.nc.semaphore("ds_%d" % self.nsem))
                self.nsem += 1
        self.E[eng].dma_start(out=out, in_=in_).then_inc(sb.sem, 16)
        sb.cnt += 16
        self.semcnt[id(sb.sem)] = (sb.sem, sb.cnt)
        tok = (sb.sem, sb.cnt)
        for b in reads:
            b.add_rd(tok)
        for b in writes:
            b.set_w(tok)

    def barrier(self):
        for eng, e in self.E.items():
            seen = self.seen[eng]
            for o, sem in self.esem.items():
                v = self.cnt[o]
                if v > 0 and seen.get(id(sem), 0) < v and o != eng:
                    seen[id(sem)] = v
                    e.wait_ge(sem, v)
            for sid, (sem, v) in self.semcnt.items():
                if seen.get(sid, 0) < v:
                    seen[sid] = v
                    e.wait_ge(sem, v)

    def release(self, bufs):
        for b in bufs:
            if b.sem is not None:
                self.free_sems.append((b.sem, b.cnt))
                b.sem = None

    def psum(self):
        b = self.ps[self.psi % 8]
        self.psi += 1
        return b


class Cfg:
    def __init__(self, NP, NS, DFF):
        self.NP, self.NS, self.DFF = NP, NS, DFF
        self.FC = DFF // 128
        self.seqn = [NP, NS, NS]
        self.start = [0, NP, NP + NS]
        self.R = NP + 2 * NS
        self.TT = self.R + 3 * NMETA
        self.NMAX = max(self.seqn)


NA_Q, NA_K, NA_V, DA_Q, DA_K, DA_V, M_CQ, M_CKV, M_KR = 0, 384, 768, 1152, 1536, 1920, 2304, 2560, 2688
NCH_IN = 23


def _win_cols():
    idx = -np.ones((NCH_IN, 128), np.int64)
    r = np.arange(128)
    for i in range(3):
        idx[i] = NA_Q + 128 * i + r
        idx[3 + i] = NA_K + 128 * i + r
        dd = r % 32
        pr = np.where(dd < 4, r + 4, np.where(dd < 8, r - 4, r))
        idx[6 + i] = DA_Q + 128 * i + r
        idx[9 + i] = DA_Q + 128 * i + pr
        idx[12 + i] = DA_K + 128 * i + r
        idx[15 + i] = DA_K + 128 * i + pr
    idx[18] = M_CQ + r
    idx[19] = M_CQ + 128 + r
    idx[20] = M_CKV + r
    idx[21, :32] = M_KR + np.arange(32)
    idx[22, :16] = M_KR + np.arange(16) + 16
    idx[22, 16:32] = M_KR + np.arange(16)
    return idx.reshape(-1)


def _gather_cols(w, idx):
    out = w[:, np.maximum(idx, 0)]
    out = np.where(idx[None, :] >= 0, out, np.float32(0.0))
    return np.ascontiguousarray(out.astype(np.float32))


def _host_shared(cfg, inp):
    FC = cfg.FC
    sh = {}
    g, u, dn = inp["ffn_w_gate"], inp["ffn_w_up"], inp["ffn_w_down"]
    wgu = np.empty((4, 2, FC, 128, 8 * 128), np.float32)
    wd = np.empty((4, 8, 128, FC * 128), np.float32)
    for l in range(2):
        for j in range(2):
            lj = l * 2 + j
            wgu[lj, 0] = g[l, j].reshape(8, 128, FC, 128).transpose(2, 1, 0, 3).reshape(FC, 128, 1024)
            wgu[lj, 1] = u[l, j].reshape(8, 128, FC, 128).transpose(2, 1, 0, 3).reshape(FC, 128, 1024)
            wd[lj] = dn[l, j].reshape(FC, 128, 8, 128).transpose(2, 1, 0, 3).reshape(8, 128, FC * 128)
    sh["wgu"] = wgu.reshape(4 * 2 * FC * 128, 1024)
    sh["wd"] = wd.reshape(4 * 8 * 128, FC * 128)
    idx = _win_cols()
    win = np.empty((2, NCH_IN, 128, 1024), np.float32)
    wv = np.empty((2, 128, 8 * 768), np.float32)
    wuq = np.empty((2, 8, 128, 256), np.float32)
    wukv = np.empty((2, 128, 512), np.float32)
    wo = np.empty((2, 8, 128, 1024), np.float32)
    f = np.arange(128)
    for l in range(2):
        w = inp["w_in"][l]
        we = _gather_cols(w, idx)
        win[l] = we.reshape(8, 128, NCH_IN, 128).transpose(2, 1, 0, 3).reshape(NCH_IN, 128, 1024)
        vcols = np.concatenate([NA_V + np.arange(384), DA_V + np.arange(384)])
        wv[l] = w[:, vcols].reshape(8, 128, 768).transpose(1, 0, 2).reshape(128, 8 * 768)
        uq = inp["mla_w_uq"][l]
        for h in range(4):
            im = np.where(f < 96, h * 96 + f, -1)
            ip = np.where((f >= 64) & (f < 80), h * 96 + f + 16, np.where((f >= 80) & (f < 96), h * 96 + f - 16, -1))
            for k, ii in ((0, im), (1, ip)):
                m = _gather_cols(uq, ii)
                wuq[l, h * 2 + k] = m.reshape(2, 128, 128).transpose(1, 0, 2).reshape(128, 256)
        ukv = inp["mla_w_ukv"][l]
        kc = np.concatenate([h * 128 + np.arange(64) for h in range(4)])
        vc = np.concatenate([h * 128 + 64 + np.arange(64) for h in range(4)])
        wukv[l] = np.concatenate([ukv[:, kc], ukv[:, vc]], axis=1)
        wo[l] = inp["w_out"][l].reshape(8, 128, 8, 128).transpose(2, 1, 0, 3).reshape(8, 128, 1024)
    sh["win"] = win.reshape(2 * NCH_IN * 128, 1024)
    sh["wv"] = wv.reshape(2 * 128, 8 * 768)
    sh["wuq"] = wuq.reshape(2 * 8 * 128, 256)
    sh["wukv"] = wukv.reshape(2 * 128, 512)
    sh["wo"] = wo.reshape(2 * 8 * 128, 1024)
    cols = []
    for l in range(2):
        for i in range(3):
            cols.append(inp["norm_g"][l, i].reshape(8, 128).T)
    cols.append(inp["final_norm_g"].reshape(8, 128).T)
    for l in range(2):
        cols.append(inp["mla_q_norm_g"][l].reshape(2, 128).T)
    for l in range(2):
        cols.append(inp["mla_kv_norm_g"][l].reshape(1, 128).T)
    for l in range(2):
        c = np.zeros((128, 1), np.float32)
        c[:64, 0] = inp["da_subln_g"][l]
        cols.append(c)
    sh["gcols"] = np.ascontiguousarray(np.concatenate(cols, axis=1).astype(np.float32))
    lam = np.empty((2, 128), np.float32)
    for l in range(2):
        lp = inp["da_lambda"][l]
        lam[l] = np.concatenate([lp[0], lp[2], lp[1], lp[3]])
    sh["dalam"] = lam
    kc = np.arange(64)[:, None]
    qc = np.arange(64)[None, :]
    cs = np.clip(qc - 8, 0, 48)
    valid = (kc >= cs) & (kc < cs + 16)
    ci = np.clip(kc - qc + 15, 0, 30)
    rb = inp["na_rel_bias"]
    rbx = rb[:, :, ::-1, :][:, :, :, ci]
    rbx = np.where(valid[None, None, None], rbx, np.float32(NEG)).astype(np.float32)
    sh["rbx"] = np.ascontiguousarray(rbx.reshape(2 * 6 * 15 * 64, 64))
    pos = np.empty(cfg.TT, np.float32)
    for s in range(3):
        pos[cfg.start[s]:cfg.start[s] + cfg.seqn[s]] = NMETA + np.arange(cfg.seqn[s], dtype=np.float32)
        pos[cfg.R + NMETA * s:cfg.R + NMETA * (s + 1)] = np.arange(NMETA, dtype=np.float32)

    def tables(dim):
        inv = (np.float32(ROPE_THETA) ** (-(np.arange(0, dim, 2, dtype=np.float32) / np.float32(dim)))).astype(np.float32)
        ang = pos[:, None] * inv[None, :]
        return np.cos(ang).astype(np.float32).T, np.sin(ang).astype(np.float32).T

    c8, s8 = tables(8)
    t = np.zeros((2, 32, cfg.TT), np.float32)
    t[0, :] = 1.0
    t[0, 0:4] = c8
    t[0, 4:8] = c8
    t[1, 0:4] = -s8
    t[1, 4:8] = s8
    sh["ropeda"] = np.ascontiguousarray(t.reshape(64, cfg.TT))
    c32, s32 = tables(32)
    t = np.zeros((2, 32, cfg.TT), np.float32)
    t[0, 0:16] = c32
    t[0, 16:32] = c32
    t[1, 0:16] = -s32
    t[1, 16:32] = s32
    sh["ropeml"] = np.ascontiguousarray(t.reshape(64, cfg.TT))
    sh["ident"] = np.eye(128, dtype=np.float32)
    sh["metatok"] = np.ascontiguousarray(inp["meta_tokens"].astype(np.float32))
    return sh


def build_program(cfg):
    FC, R, TT = cfg.FC, cfg.R, cfg.TT
    nc = bass.Bass("TRN2", target_bir_lowering=False)
    stack = ExitStack()
    with stack:
        def din(name, shape):
            return nc.dram_tensor(name, list(shape), F32, kind="ExternalInput").ap()

        x_d = din("xtok", [R, D])
        meta_d = din("metatok", [NMETA, D])
        wgu_f = din("wgu", [4 * 2 * FC * 128, 1024])
        wd_f = din("wd", [4 * 8 * 128, FC * 128])
        win_f = din("win", [2 * NCH_IN * 128, 1024])
        wv_f = din("wv", [2 * 128, 8 * 768])
        wuq_f = din("wuq", [2 * 8 * 128, 256])
        wukv_f = din("wukv", [2 * 128, 512])
        wo_f = din("wo", [2 * 8 * 128, 1024])
        gcols_d = din("gcols", [128, 64])
        dalam_d = din("dalam", [2, 128])
        rbx_d = din("rbx", [2 * 6 * 15 * 64, 64])
        ropeda_d = din("ropeda", [64, TT])
        ropeml_d = din("ropeml", [64, TT])
        ident_d = din("ident", [128, 128])
        y_d = nc.dram_tensor("y", [R, D], F32, kind="ExternalOutput").ap()

        def dscr(name, shape, dt=BF16):
            if DEBUG and name.endswith("_s"):
                return nc.dram_tensor(name, list(shape), dt, kind="ExternalOutput").ap()
            return nc.dram_tensor(name, list(shape), dt).ap()

        wgu_b = dscr("wgu_b", [4 * 2 * FC * 128, 1024])
        wd_b = dscr("wd_b", [4 * 8 * 128, FC * 128])
        win_b = dscr("win_b", [2 * NCH_IN * 128, 1024])
        wv_b = dscr("wv_b", [2 * 128, 8 * 768])
        wuq_b = dscr("wuq_b", [2 * 8 * 128, 256])
        wukv_b = dscr("wukv_b", [2 * 128, 512])
        wo_b = dscr("wo_b", [2 * 8 * 128, 1024])
        hT_d = dscr("hT_s", [D, TT], F32)
        oT_d = dscr("oT_s", [D, TT])
        naq_d = dscr("naq_s", [384, TT])
        nak_d = dscr("nak_s", [384, TT])
        daq_d = dscr("daq_s", [384, TT])
        dak_d = dscr("dak_s", [384, TT])
        mlq_d = dscr("mlq_s", [384, TT])
        mlk_d = dscr("mlk_s", [256, TT])
        mlr_d = dscr("mlr_s", [32, TT])
        v_d = dscr("v_s", [TT, 1024])

        S = Sched(nc, stack)
        S.ps = [Buf("ps%d" % i, stack.enter_context(nc.psum_tensor("ps%d" % i, [128, 512], F32))) for i in range(8)]

        phase_bufs = []

        uniq = [0]

        def sb(st, name, shape, dt):
            uniq[0] += 1
            name = "s%d_%s" % (uniq[0], name)
            b = Buf(name, st.enter_context(nc.sbuf_tensor(name, list(shape), dt)))
            if st is not stack:
                phase_bufs.append(b)
            return b

        Bx = Buf("x", dram=True)
        Bconst = Buf("const", dram=True)
        Bw = {k: Buf("w_" + k, dram=True) for k in ("gu0", "gu1", "gu2", "gu3", "d0", "d1", "d2", "d3", "in0", "in1", "misc")}
        BhT = {}
        BoT = {}
        Bqk = Buf("qk", dram=True)
        Bv = Buf("v", dram=True)
        By = Buf("y", dram=True)

        ident = sb(stack, "ident", [128, 128], F32)
        onesb = sb(stack, "onesb", [128, 128], BF16)
        ones64 = sb(stack, "ones64", [64, 64], F32)
        sel65 = sb(stack, "sel65", [65, 64], F32)
        onesrow = sb(stack, "onesrow", [1, 64], F32)
        gcols = sb(stack, "gcols", [128, 64], F32)
        neglam = sb(stack, "neglam", [64, 2], F32)
        gsub = sb(stack, "gsub", [64, 2], F32)
        lamrow = sb(stack, "lamrow", [1, 256], F32)
        lamtmp = sb(stack, "lamtmp", [1, 128], F32)
        epscol = sb(stack, "epscol", [128, 1], F32)
        S.op("pool", lambda e: e.memset(epscol.t[:], EPS), [], [epscol])
        S.dma("sp", ident.t[:], ident_d[:, :], [Bconst], [ident], ident)
        S.dma("sp", gcols.t[:], gcols_d[:, :], [Bconst], [gcols], gcols)
        S.dma("sp", lamrow.t[0:1, :], dalam_d.rearrange("l f -> (l f)").rearrange("(o f) -> o f", o=1), [Bconst], [lamrow], lamrow)
        S.op("pool", lambda e: e.memset(onesb.t[:], 1.0), [], [onesb])
        S.op("pool", lambda e: e.memset(ones64.t[:], 1.0), [], [ones64])
        S.op("pool", lambda e: e.memset(sel65.t[:], 0.0), [], [sel65])
        S.op("pool", lambda e: e.memset(sel65.t[64:65, :], 1.0), [], [sel65])
        S.op("pool", lambda e: e.memset(onesrow.t[:], 1.0), [], [onesrow])

        def convert(dst, src, rows, key, r0=0):
            r = r0
            while r < r0 + rows:
                n = min(128, r0 + rows - r)
                S.dma("pool", dst[r:r + n, :], src[r:r + n, :], [Bconst], [Bw[key]], Bw[key])
                r += n

        def conv_ffn(lj):
            convert(wgu_b, wgu_f, 2 * FC * 128, "gu%d" % lj, lj * 2 * FC * 128)
            convert(wd_b, wd_f, 8 * 128, "d%d" % lj, lj * 8 * 128)

        conv_ffn(0)
        convert(win_b, win_f, NCH_IN * 128, "in0", 0)
        convert(wv_b, wv_f, 2 * 128, "misc")
        convert(wuq_b, wuq_f, 2 * 8 * 128, "misc")
        convert(wukv_b, wukv_f, 2 * 128, "misc")
        convert(wo_b, wo_f, 2 * 8 * 128, "misc")
        conv_ffn(1)
        conv_ffn(2)
        convert(win_b, win_f, NCH_IN * 128, "in1", NCH_IN * 128)
        conv_ffn(3)

        for l in range(2):
            lam_init = 0.8 - 0.6 * math.exp(-0.3 * l)
            A = lamrow.t[0:1, l * 128:l * 128 + 64]
            Bm = lamrow.t[0:1, l * 128 + 64:l * 128 + 128]
            S.op("dve", lambda e: e.tensor_tensor(lamtmp.t[0:1, 0:64], A, Bm, ALU.mult), [lamrow], [lamtmp])
            S.op("dve", lambda e: e.tensor_reduce(lamtmp.t[0:1, 64:66], lamtmp.t[0:1, 0:64].rearrange("o (g f) -> o g f", g=2),
                                                   mybir.AxisListType.X, ALU.add), [lamtmp], [lamtmp])
            S.op("act", lambda e: e.activation(out=lamtmp.t[0:1, 66:68], in_=lamtmp.t[0:1, 64:66], func=AF.Exp), [lamtmp], [lamtmp])
            S.op("dve", lambda e: e.tensor_tensor(lamtmp.t[0:1, 68:69], lamtmp.t[0:1, 67:68], lamtmp.t[0:1, 66:67], ALU.subtract), [lamtmp], [lamtmp])
            S.op("dve", lambda e: e.tensor_scalar(lamtmp.t[0:1, 69:70], lamtmp.t[0:1, 68:69], -lam_init, None, ALU.add), [lamtmp], [lamtmp])
            p = S.psum()
            S.op("pe", lambda e: e.matmul(p.t[0:64, 0:1], onesrow.t[0:1, 0:64], lamtmp.t[0:1, 69:70], start=True, stop=True), [onesrow, lamtmp], [p])
            S.op("dve", lambda e: e.tensor_copy(neglam.t[:, l:l + 1], p.t[0:64, 0:1]), [p], [neglam])
            S.op("dve", lambda e: e.tensor_scalar(gsub.t[:, l:l + 1], gcols.t[0:64, 62 + l:63 + l], 1.0 - lam_init, None, ALU.mult), [gcols], [gsub])

        cp_flip = [0]

        def evac(out_ap, in_ap, reads, writes):
            cp_flip[0] ^= 1
            if cp_flip[0]:
                S.op("act", lambda e: e.activation(out=out_ap, in_=in_ap, func=AF.Copy), reads, writes)
            else:
                S.op("dve", lambda e: e.tensor_copy(out_ap, in_ap), reads, writes)

        def rms_rstd(srcs, W, Dn, sqring, rsring):
            ps = S.psum()
            n = len(srcs)
            for c, (b, ap) in enumerate(srcs):
                sq = sqring.next()
                if c % 3 == 2:
                    S.op("pool", lambda e: e.tensor_tensor(sq.t[:, :W], ap, ap, ALU.mult), [b], [sq])
                else:
                    S.op("act", lambda e: e.activation(out=sq.t[:, :W], in_=ap, func=AF.Square), [b], [sq])
                S.op("pe", lambda e: e.matmul(ps.t[:, :W], onesb.t[:, :], sq.t[:, :W], start=(c == 0), stop=(c == n - 1)),
                     [sq, onesb], [ps], sig=True)
            rs = rsring.next()
            S.op("act", lambda e: e.activation(out=rs.t[:, :W], in_=ps.t[:, :W], func=AF.Ln, bias=epscol.t[:, 0:1], scale=1.0 / Dn), [ps, epscol], [rs])
            rs2 = rsring.next()
            S.op("act", lambda e: e.activation(out=rs2.t[:, :W], in_=rs.t[:, :W], func=AF.Exp, scale=-0.5), [rs], [rs2])
            return rs2

        def tok_phase(stage):
            with ExitStack() as ph:
                NS_ = 2
                h = [sb(ph, "h%d" % i, [128, 8, 512], F32) for i in range(NS_)]
                xn = [sb(ph, "xn%d" % i, [128, 8, 512], BF16) for i in range(NS_)]
                hid = [sb(ph, "hid%d" % i, [128, FC, 512], BF16) for i in range(NS_)]
                sqring = Ring([sb(ph, "sq%d" % i, [128, 512], BF16) for i in range(4)])
                rsring = Ring([sb(ph, "rs%d" % i, [128, 512], F32) for i in range(3)])
                sgring = Ring([sb(ph, "sg%d" % i, [128, 512], F32) for i in range(4)])
                guring = Ring([sb(ph, "gu%d" % i, [128, 2, 8, 128], BF16) for i in range(3)])
                wdring = Ring([sb(ph, "wdr%d" % i, [128, FC, 128], BF16) for i in range(2)])
                if stage >= 1:
                    o1 = sb(ph, "o1", [128, 8, 512], BF16)
                    woring = Ring([sb(ph, "wor%d" % i, [128, 8, 128], BF16) for i in range(2)])
                if stage == 0:
                    xsring = Ring([sb(ph, "xs%d" % i, [128, 4, 1024], F32) for i in range(1)])
                if stage <= 1:
                    winring = Ring([sb(ph, "winr%d" % i, [128, 8, 128], BF16) for i in range(2)])
                    wvt = sb(ph, "wvt", [128, 8, 768], BF16)
                    wuqt = sb(ph, "wuqt", [128, 8, 2, 128], BF16)
                    wukvt = sb(ph, "wukvt", [128, 512], BF16)
                    rda = [sb(ph, "rda%d" % i, [128, 2, 512], BF16) for i in range(NS_)]
                    rml = [sb(ph, "rml%d" % i, [96, 2, 512], BF16) for i in range(NS_)]
                    uoring = Ring([sb(ph, "uo%d" % i, [128, 512], BF16) for i in range(3)])
                    t1ring = sgring
                    t2ring = sgring
                    cq1 = sb(ph, "cq1", [128, 3, 512], F32)
                    cqn1 = sb(ph, "cqn1", [128, 3, 512], BF16)
                    cq = [cq1, cq1]
                    cqn = [cqn1, cqn1]
                    vtring = Ring([sb(ph, "vt%d" % i, [128, 1024], BF16) for i in range(2)])
                if stage == 2:
                    hn = [sb(ph, "hn%d" % i, [128, 8, 512], F32) for i in range(NS_)]
                    yring = Ring([sb(ph, "yt%d" % i, [128, 1024], F32) for i in range(2)])

                blocks = []
                for c0 in range(0, R, 1024):
                    blocks.append([(c0, 512), (c0 + 512, 512)])
                if stage <= 1:
                    blocks.append([(R, 3 * NMETA)])

                hT_v = hT_d.rearrange("(c p) t -> p c t", p=128)
                oT_v = oT_d.rearrange("(c p) t -> p c t", p=128)

                def ffn(lj, subs, gbase):
                    for i, (c0, W) in enumerate(subs):
                        rs = rms_rstd([(h[i], h[i].t[:, c, :W]) for c in range(8)], W, D, sqring, rsring)
                        if SUB < 1.4:
                            continue
                        for c in range(8):
                            S.op("dve", lambda e: e.scalar_tensor_tensor(out=xn[i].t[:, c, :W], in0=h[i].t[:, c, :W],
                                                                          scalar=gcols.t[:, gbase + c:gbase + c + 1], in1=rs.t[:, :W],
                                                                          op0=ALU.mult, op1=ALU.mult), [h[i], rs, gcols], [xn[i]])
                    if SUB < 1.6:
                        return
                    Bg = Bw["gu%d" % lj]
                    Bd = Bw["d%d" % lj]
                    for fc in range(FC):
                        w = guring.next()
                        for k in range(2):
                            r0 = ((lj * 2 + k) * FC + fc) * 128
                            S.dma("sp", w.t[:, k].rearrange("p c f -> p (c f)"), wgu_b[r0:r0 + 128, :], [Bg], [w], w)
                        for i, (c0, W) in enumerate(subs):
                            pg = S.psum()
                            pu = S.psum()
                            for c in range(8):
                                S.op("pe", lambda e: e.matmul(pg.t[:, :W], w.t[:, 0, c, :], xn[i].t[:, c, :W], start=(c == 0), stop=(c == 7)),
                                     [w, xn[i]], [pg], sig=(c == 7))
                            for c in range(8):
                                S.op("pe", lambda e: e.matmul(pu.t[:, :W], w.t[:, 1, c, :], xn[i].t[:, c, :W], start=(c == 0), stop=(c == 7)),
                                     [w, xn[i]], [pu], sig=(c == 7))
                            sg = sgring.next()
                            S.op("act", lambda e: e.activation(out=sg.t[:, :W], in_=pg.t[:, :W], func=AF.Silu), [pg], [sg])
                            S.op("dve", lambda e: e.tensor_tensor(hid[i].t[:, fc, :W], pu.t[:, :W], sg.t[:, :W], ALU.mult), [pu, sg], [hid[i]])
                    if SUB < 1.8:
                        return
                    for dc in range(8):
                        w = wdring.next()
                        r0 = (lj * 8 + dc) * 128
                        S.dma("sp", w.t[:].rearrange("p c f -> p (c f)"), wd_b[r0:r0 + 128, :], [Bd], [w], w)
                        for i, (c0, W) in enumerate(subs):
                            py = S.psum()
                            for fc in range(FC):
                                S.op("pe", lambda e: e.matmul(py.t[:, :W], w.t[:, fc, :], hid[i].t[:, fc, :W], start=(fc == 0), stop=(fc == FC - 1)),
                                     [w, hid[i]], [py], sig=(fc == FC - 1))
                            S.op("dve", lambda e: e.scalar_tensor_tensor(out=h[i].t[:, dc, :W], in0=py.t[:, :W], scalar=0.5, in1=h[i].t[:, dc, :W],
                                                                          op0=ALU.mult, op1=ALU.add), [py, h[i]], [h[i]])

                def wout(l, subs):
                    for i, (c0, W) in enumerate(subs):
                        S.dma("sp", o1.t[:, :, :W], oT_v[:, :, c0:c0 + W], [BoT[c0]], [o1], o1)
                        for dc in range(8):
                            w = woring.next()
                            r0 = (l * 8 + dc) * 128
                            S.dma("sp", w.t[:].rearrange("p c f -> p (c f)"), wo_b[r0:r0 + 128, :], [Bw["misc"]], [w], w)
                            py = S.psum()
                            for fc in range(8):
                                S.op("pe", lambda e: e.matmul(py.t[:, :W], w.t[:, fc, :], o1.t[:, fc, :W], start=(fc == 0), stop=(fc == 7)),
                                     [w, o1], [py], sig=(fc == 7))
                            S.op("dve", lambda e: e.tensor_tensor(h[i].t[:, dc, :W], py.t[:, :W], h[i].t[:, dc, :W], ALU.add), [py, h[i]], [h[i]])

                def proj_in(l, subs):
                    gb = (l * 3 + 1) * 8
                    for i, (c0, W) in enumerate(subs):
                        rs = rms_rstd([(h[i], h[i].t[:, c, :W]) for c in range(8)], W, D, sqring, rsring)
                        for c in range(8):
                            S.op("dve", lambda e: e.scalar_tensor_tensor(out=xn[i].t[:, c, :W], in0=h[i].t[:, c, :W],
                                                                          scalar=gcols.t[:, gb + c:gb + c + 1], in1=rs.t[:, :W],
                                                                          op0=ALU.mult, op1=ALU.mult), [h[i], rs, gcols], [xn[i]])
                        for gq in range(4):
                            S.dma("pool", rda[i].t[gq * 32:(gq + 1) * 32, :, :W],
                                  ropeda_d.rearrange("(a r) t -> r a t", a=2)[:, :, c0:c0 + W], [Bconst], [rda[i]], rda[i])
                        for base in (0, 64):
                            S.dma("pool", rml[i].t[base:base + 32, :, :W],
                                  ropeml_d.rearrange("(a r) t -> r a t", a=2)[:, :, c0:c0 + W], [Bconst], [rml[i]], rml[i])
                    Bi = Bw["in%d" % l]
                    S.dma("sp", wvt.t[:].rearrange("p c f -> p (c f)"), wv_b[l * 128:(l + 1) * 128, :], [Bw["misc"]], [wvt], wvt)
                    S.dma("sp", wuqt.t[:], wuq_b[l * 1024:(l + 1) * 1024, :].rearrange("(j p) (c f) -> p j c f", p=128, c=2),
                          [Bw["misc"]], [wuqt], wuqt)
                    S.dma("sp", wukvt.t[:], wukv_b[l * 128:(l + 1) * 128, :], [Bw["misc"]], [wukvt], wukvt)

                    def load_chunk(j):
                        w = winring.next()
                        r0 = (l * NCH_IN + j) * 128
                        S.dma("sp", w.t[:].rearrange("p c f -> p (c f)"), win_b[r0:r0 + 128, :], [Bi], [w], w)
                        return w

                    def mm_chunk(w, i, W, M):
                        p = S.psum()
                        for c in range(8):
                            S.op("pe", lambda e: e.matmul(p.t[:M, :W], w.t[:, c, :M], xn[i].t[:, c, :W], start=(c == 0), stop=(c == 7)),
                                 [w, xn[i]], [p], sig=(c == 7))
                        return p

                    def store_rows(dst, row0, M, uo, c0, W):
                        S.dma("pool", dst[row0:row0 + M, c0:c0 + W], uo.t[:M, :W], [uo], [Bqk], uo)

                    for j in range(6):
                        w = load_chunk(j)
                        for i, (c0, W) in enumerate(subs):
                            p = mm_chunk(w, i, W, 128)
                            uo = uoring.next()
                            evac(uo.t[:, :W], p.t[:, :W], [p], [uo])
                            store_rows(naq_d if j < 3 else nak_d, (j % 3) * 128, 128, uo, c0, W)
                    for grp, dst in ((6, daq_d), (12, dak_d)):
                        for jj in range(3):
                            wm = load_chunk(grp + jj)
                            wp = load_chunk(grp + 3 + jj)
                            for i, (c0, W) in enumerate(subs):
                                pm = mm_chunk(wm, i, W, 128)
                                pp = mm_chunk(wp, i, W, 128)
                                t1 = t1ring.next()
                                t2 = t2ring.next()
                                S.op("dve", lambda e: e.tensor_tensor(t1.t[:, :W], pm.t[:, :W], rda[i].t[:, 0, :W], ALU.mult), [pm, rda[i]], [t1])
                                S.op("dve", lambda e: e.tensor_tensor(t2.t[:, :W], pp.t[:, :W], rda[i].t[:, 1, :W], ALU.mult), [pp, rda[i]], [t2])
                                uo = uoring.next()
                                S.op("pool", lambda e: e.tensor_tensor(uo.t[:, :W], t1.t[:, :W], t2.t[:, :W], ALU.add), [t1, t2], [uo])
                                store_rows(dst, jj * 128, 128, uo, c0, W)
                    wm = load_chunk(21)
                    wp = load_chunk(22)
                    for i, (c0, W) in enumerate(subs):
                        pm = mm_chunk(wm, i, W, 32)
                        pp = mm_chunk(wp, i, W, 32)
                        t1 = t1ring.next()
                        t2 = t2ring.next()
                        S.op("dve", lambda e: e.tensor_tensor(t1.t[:32, :W], pm.t[:32, :W], rml[i].t[0:32, 0, :W], ALU.mult), [pm, rml[i]], [t1])
                        S.op("dve", lambda e: e.tensor_tensor(t2.t[:32, :W], pp.t[:32, :W], rml[i].t[0:32, 1, :W], ALU.mult), [pp, rml[i]], [t2])
                        uo = uoring.next()
                        S.op("pool", lambda e: e.tensor_tensor(uo.t[:32, :W], t1.t[:32, :W], t2.t[:32, :W], ALU.add), [t1, t2], [uo])
                        store_rows(mlr_d, 0, 32, uo, c0, W)
                    for i, (c0, W) in enumerate(subs):
                        for jj in range(3):
                            w = load_chunk(18 + jj)
                            p = mm_chunk(w, i, W, 128)
                            evac(cq[i].t[:, jj, :W], p.t[:, :W], [p], [cq[i]])
                        rs = rms_rstd([(cq[i], cq[i].t[:, c, :W]) for c in range(2)], W, 256, sqring, rsring)
                        for c in range(2):
                            S.op("dve", lambda e: e.scalar_tensor_tensor(out=cqn[i].t[:, c, :W], in0=cq[i].t[:, c, :W],
                                                                          scalar=gcols.t[:, 56 + 2 * l + c:57 + 2 * l + c], in1=rs.t[:, :W],
                                                                          op0=ALU.mult, op1=ALU.mult), [cq[i], rs, gcols], [cqn[i]])
                        rs = rms_rstd([(cq[i], cq[i].t[:, 2, :W])], W, 128, sqring, rsring)
                        S.op("dve", lambda e: e.scalar_tensor_tensor(out=cqn[i].t[:, 2, :W], in0=cq[i].t[:, 2, :W],
                                                                      scalar=gcols.t[:, 60 + l:61 + l], in1=rs.t[:, :W],
                                                                      op0=ALU.mult, op1=ALU.mult), [cq[i], rs, gcols], [cqn[i]])
                        for hh in range(4):
                            pm = S.psum()
                            pp = S.psum()
                            for c in range(2):
                                S.op("pe", lambda e: e.matmul(pm.t[:96, :W], wuqt.t[:, 2 * hh, c, 0:96], cqn[i].t[:, c, :W], start=(c == 0), stop=(c == 1)),
                                     [wuqt, cqn[i]], [pm], sig=(c == 1))
                            for c in range(2):
                                S.op("pe", lambda e: e.matmul(pp.t[:96, :W], wuqt.t[:, 2 * hh + 1, c, 0:96], cqn[i].t[:, c, :W], start=(c == 0), stop=(c == 1)),
                                     [wuqt, cqn[i]], [pp], sig=(c == 1))
                            uo = uoring.next()
                            t1 = t1ring.next()
                            t2 = t2ring.next()
                            S.op("act", lambda e: e.activation(out=uo.t[0:64, :W], in_=pm.t[0:64, :W], func=AF.Copy), [pm], [uo])
                            S.op("dve", lambda e: e.tensor_tensor(t1.t[64:96, :W], pm.t[64:96, :W], rml[i].t[64:96, 0, :W], ALU.mult), [pm, rml[i]], [t1])
                            S.op("dve", lambda e: e.tensor_tensor(t2.t[64:96, :W], pp.t[64:96, :W], rml[i].t[64:96, 1, :W], ALU.mult), [pp, rml[i]], [t2])
                            S.op("pool", lambda e: e.tensor_tensor(uo.t[64:96, :W], t1.t[64:96, :W], t2.t[64:96, :W], ALU.add), [t1, t2, uo], [uo])
                            store_rows(mlq_d, hh * 96, 96, uo, c0, W)
                        for kk in range(2):
                            p = S.psum()
                            S.op("pe", lambda e: e.matmul(p.t[:, :W], wukvt.t[:, kk * 128:(kk + 1) * 128], cqn[i].t[:, 2, :W], start=True, stop=True),
                                 [wukvt, cqn[i]], [p])
                            uo = uoring.next()
                            evac(uo.t[:, :W], p.t[:, :W], [p], [uo])
                            store_rows(mlk_d, kk * 128, 128, uo, c0, W)
                        ng = (W + 127) // 128
                        for g in range(ng):
                            gw = min(128, W - g * 128)
                            vt = vtring.next()
                            for half in range(2):
                                p = S.psum()
                                for c in range(8):
                                    S.op("pe", lambda e: e.matmul(p.t[:gw, :384], xn[i].t[:, c, g * 128:g * 128 + gw], wvt.t[:, c, half * 384:(half + 1) * 384],
                                                                   start=(c == 0), stop=(c == 7)), [xn[i], wvt], [p], sig=(c == 7))
                                evac(vt.t[:gw, half * 384:(half + 1) * 384], p.t[:gw, :384], [p], [vt])
                            p = S.psum()
                            S.op("pe", lambda e: e.matmul(p.t[:gw, :256], cqn[i].t[:, 2, g * 128:g * 128 + gw], wukvt.t[:, 256:512], start=True, stop=True),
                                 [cqn[i], wukvt], [p])
                            evac(vt.t[:gw, 768:1024], p.t[:gw, :256], [p], [vt])
                            S.dma("pool", v_d[c0 + g * 128:c0 + g * 128 + gw, :], vt.t[:gw, :], [vt], [Bv], vt)

                for bi, subs in enumerate(blocks):
                    is_meta = subs[0][0] >= R
                    for i, (c0, W) in enumerate(subs):
                        key = c0
                        if key not in BhT:
                            BhT[key] = Buf("hT%d" % key, dram=True)
                            BoT[key] = Buf("oT%d" % key, dram=True)
                        if stage == 0:
                            xs = xsring.next()
                            if is_meta:
                                for s in range(3):
                                    S.dma("sp", xs.t[s * NMETA:(s + 1) * NMETA, 0, :], meta_d[:, :], [Bconst], [xs], xs)
                            else:
                                S.dma("sp", xs.t[:, :, :], x_d[c0:c0 + W, :].rearrange("(g p) d -> p g d", p=128), [Bx], [xs], xs)
                            ng = (W + 127) // 128
                            for c in range(8):
                                p = S.psum()
                                for g in range(ng):
                                    gw = min(128, W - g * 128)
                                    S.op("pe", lambda e: e.transpose(p.t[:, g * 128:g * 128 + gw], xs.t[:gw, g, c * 128:(c + 1) * 128], ident.t[:gw, :gw]),
                                         [xs, ident], [p], sig=(g == ng - 1))
                                evac(h[i].t[:, c, :W], p.t[:, :W], [p], [h[i]])
                        else:
                            S.dma("sp", h[i].t[:, :, :W], hT_v[:, :, c0:c0 + W], [BhT[key]], [h[i]], h[i])
                    if stage >= 1:
                        wout(stage - 1, subs)
                        ffn((stage - 1) * 2 + 1, subs, ((stage - 1) * 3 + 2) * 8)
                    if stage <= 1:
                        if SUB >= 1.2:
                            ffn(stage * 2, subs, (stage * 3) * 8)
                        if SUB >= 3:
                            proj_in(stage, subs)
                        for i, (c0, W) in enumerate(subs):
                            S.dma("pool", hT_v[:, :, c0:c0 + W], h[i].t[:, :, :W], [h[i]], [BhT[c0]], h[i])
                    else:
                        for i, (c0, W) in enumerate(subs):
                            rs = rms_rstd([(h[i], h[i].t[:, c, :W]) for c in range(8)], W, D, sqring, rsring)
                            for c in range(8):
                                S.op("dve", lambda e: e.scalar_tensor_tensor(out=hn[i].t[:, c, :W], in0=h[i].t[:, c, :W],
                                                                              scalar=gcols.t[:, 48 + c:49 + c], in1=rs.t[:, :W],
                                                                              op0=ALU.mult, op1=ALU.mult), [h[i], rs, gcols], [hn[i]])
                            for g in range(W // 128):
                                yt = yring.next()
                                for half in range(2):
                                    p = S.psum()
                                    for cc in range(4):
                                        c = half * 4 + cc
                                        S.op("pe", lambda e: e.transpose(p.t[:, cc * 128:(cc + 1) * 128], hn[i].t[:, c, g * 128:(g + 1) * 128], ident.t[:, :]),
                                             [hn[i], ident], [p], sig=(cc == 3))
                                    evac(yt.t[:, half * 512:(half + 1) * 512], p.t[:, :], [p], [yt])
                                S.dma("pool", y_d[c0 + g * 128:c0 + (g + 1) * 128, :], yt.t[:, :], [yt], [By], yt)
                S.barrier()
                S.release(phase_bufs)
                del phase_bufs[:]

        def att_phase(l, last):
            with ExitStack() as ph:
                NT = cfg.NMAX // 128
                ktring = Ring([sb(ph, "kt%d" % i, [128, cfg.NMAX], BF16) for i in range(2)])
                kmring = Ring([sb(ph, "km%d" % i, [128, NMETA], BF16) for i in range(2)])
                vring = Ring([sb(ph, "vv%d" % i, [128, NT, 128], BF16) for i in range(2)])
                vmring = Ring([sb(ph, "vm%d" % i, [NMETA, 128], BF16) for i in range(2)])
                qaring = Ring([sb(ph, "qa%d" % i, [128, 512], BF16) for i in range(3)])
                q0ring = Ring([sb(ph, "q0_%d" % i, [128, 512], BF16) for i in range(2)])
                q1ring = Ring([sb(ph, "q1_%d" % i, [128, 512], BF16) for i in range(2)])
                ptring = Ring([sb(ph, "pt%d" % i, [128, 512], BF16) for i in range(8)])
                tmring = Ring([sb(ph, "tm%d" % i, [128, 512], F32) for i in range(5)])
                mbring = Ring([sb(ph, "mb%d" % i, [128, 3, 8, 512], BF16) for i in range(2)])
                osring = Ring([sb(ph, "os%d" % i, [65, 512], F32) for i in range(4)])
                rdring = Ring([sb(ph, "rd%d" % i, [64, 512], F32) for i in range(8)])
                aring = Ring([sb(ph, "aa%d" % i, [64, 512], F32) for i in range(6)])
                ooring = Ring([sb(ph, "oo%d" % i, [64, 512], BF16) for i in range(4)])
                spool = Ring(S.ps[0:4])
                opool = Ring(S.ps[4:8])
                for b in ktring.bufs + kmring.bufs + qaring.bufs + q0ring.bufs + q1ring.bufs:
                    S.op("pool", lambda e: e.memset(b.t[:, :], 0.0), [], [b])
                for b in vring.bufs:
                    S.op("pool", lambda e: e.memset(b.t[:, :, 64:128], 1.0), [], [b])
                for b in vmring.bufs:
                    S.op("pool", lambda e: e.memset(b.t[:, 64:128], 1.0), [], [b])

                pending = []

                def drain(i):
                    while pending and pending[0][0] <= i:
                        pending.pop(0)[1]()

                def run_qtile(streams, Nq, ktl, scale):
                    O = [opool.next() for _ in streams]
                    n = len(ktl)

                    def qk(i):
                        res = []
                        Kb, kfn, Vb, vap, nk, mbap = ktl[i]
                        for Q in streams:
                            ps = spool.next()
                            S.op("pe", lambda e: e.matmul(ps.t[:nk, :Nq], kfn(), Q.t[:, :Nq], start=True, stop=True), [Kb, Q], [ps])
                            pt = ptring.next()
                            if mbap is not None:
                                tm = tmring.next()
                                S.op("dve", lambda e: e.scalar_tensor_tensor(out=tm.t[:nk, :Nq], in0=ps.t[:nk, :Nq], scalar=scale, in1=mbap[1][:nk, :Nq],
                                                                              op0=ALU.mult, op1=ALU.add), [ps, mbap[0]], [tm])
                                S.op("act", lambda e: e.activation(out=pt.t[:nk, :Nq], in_=tm.t[:nk, :Nq], func=AF.Exp), [tm], [pt])
                            else:
                                S.op("act", lambda e: e.activation(out=pt.t[:nk, :Nq], in_=ps.t[:nk, :Nq], func=AF.Exp, scale=scale), [ps], [pt])
                            res.append(pt)
                        return res

                    ahead = max(1, 4 // len(streams) - 1)
                    fifo = [qk(j) for j in range(min(ahead, n))]
                    for i in range(n):
                        if i + ahead < n:
                            fifo.append(qk(i + ahead))
                        cur = fifo.pop(0)
                        Kb, kfn, Vb, vap, nk, mbap = ktl[i]
                        for si in range(len(streams)):
                            S.op("pe", lambda e: e.matmul(O[si].t[:128, :Nq], vap, cur[si].t[:nk, :Nq], start=(i == 0), stop=(i == n - 1)),
                                 [Vb, cur[si]], [O[si]], sig=True)
                        drain(i)
                    drain(10 ** 9)
                    return O

                def post_stages(kind, O, Nq, oo, orow, qc0, key):
                    st = []
                    osb = [osring.next() for _ in O]
                    rd = [rdring.next() for _ in O]

                    def s_evac():
                        for si, Ob in enumerate(O):
                            S.op("dve", lambda e: e.tensor_copy(osb[si].t[:65, :Nq], Ob.t[:65, :Nq]), [Ob], [osb[si]])

                    def s_den(si):
                        def f():
                            Ob = O[si]
                            S.op("pe", lambda e: e.matmul(Ob.t[:64, :Nq], sel65.t[:65, :64], osb[si].t[:65, :Nq], start=True, stop=True), [sel65, osb[si]], [Ob])
                            S.op("dve", lambda e: e.reciprocal(rd[si].t[:64, :Nq], Ob.t[:64, :Nq]), [Ob], [rd[si]])
                        return f

                    def s_store():
                        S.dma("pool", oT_d[orow:orow + 64, qc0:qc0 + Nq], oo.t[:64, :Nq], [oo], [BoT[key]], oo)

                    st.append((1, s_evac))
                    st.append((3, s_den(0)))
                    if kind != "da":
                        def s_fin():
                            S.op("dve", lambda e: e.tensor_tensor(oo.t[:64, :Nq], osb[0].t[:64, :Nq], rd[0].t[:64, :Nq], ALU.mult), [osb[0], rd[0]], [oo])
                            s_store()
                        st.append((8, s_fin))
                        return st
                    st.append((6, s_den(1)))
                    a = aring.next()
                    b2 = aring.next()
                    sq = aring.next()
                    rs = rdring.next()
                    rs2 = rdring.next()

                    def s_comb():
                        S.op("dve", lambda e: e.tensor_tensor(a.t[:64, :Nq], osb[0].t[:64, :Nq], rd[0].t[:64, :Nq], ALU.mult), [osb[0], rd[0]], [a])
                        S.op("pool", lambda e: e.tensor_tensor(b2.t[:64, :Nq], osb[1].t[:64, :Nq], rd[1].t[:64, :Nq], ALU.mult), [osb[1], rd[1]], [b2])
                        S.op("dve", lambda e: e.scalar_tensor_tensor(out=a.t[:64, :Nq], in0=b2.t[:64, :Nq], scalar=neglam.t[:64, l:l + 1], in1=a.t[:64, :Nq],
                                                                      op0=ALU.mult, op1=ALU.add), [a, b2, neglam], [a])
                        S.op("pool", lambda e: e.tensor_tensor(sq.t[:64, :Nq], a.t[:64, :Nq], a.t[:64, :Nq], ALU.mult), [a], [sq])

                    def s_ss():
                        Ob = O[0]
                        S.op("pe", lambda e: e.matmul(Ob.t[:64, :Nq], ones64.t[:64, :64], sq.t[:64, :Nq], start=True, stop=True), [ones64, sq], [Ob])
                        S.op("act", lambda e: e.activation(out=rs.t[:64, :Nq], in_=Ob.t[:64, :Nq], func=AF.Ln, bias=epscol.t[:64, 0:1], scale=1.0 / 64), [Ob, epscol], [rs])
                        S.op("act", lambda e: e.activation(out=rs2.t[:64, :Nq], in_=rs.t[:64, :Nq], func=AF.Exp, scale=-0.5), [rs], [rs2])
                        S.op("dve", lambda e: e.scalar_tensor_tensor(out=oo.t[:64, :Nq], in0=a.t[:64, :Nq], scalar=gsub.t[:64, l:l + 1], in1=rs2.t[:64, :Nq],
                                                                      op0=ALU.mult, op1=ALU.mult), [a, rs2, gsub], [oo])
                        s_store()

                    st.append((11, s_comb))
                    st.append((14, s_ss))
                    return st

                heads = [("na", hh) for hh in range(6)] + [("da", hh) for hh in range(6)] + [("ml", hh) for hh in range(4)]
                for kind, hh in heads:
                    if kind == "na":
                        scale = 64 ** -0.5
                        mb = mbring.next()
                        S.op("pool", lambda e: e.memset(mb.t[:].rearrange("p a b c -> p (a b c)"), NEG), [], [mb])
                        for v in range(3):
                            delta = (0, 4, 8)[v]
                            for kr in range(16):
                                qs = []
                                for qr in range(8):
                                    if v == 0:
                                        lo = max(qr - 4, 0)
                                    elif v == 1:
                                        lo = qr
                                    else:
                                        lo = 8 + min(qr - 4, 0)
                                    if lo <= kr < lo + 8:
                                        qs.append(qr)
                                if not qs:
                                    continue
                                qa, qb = qs[0], qs[-1] + 1
                                dra = 7 - kr + qa + delta
                                row0 = ((l * 6 + hh) * 15 + dra) * 64
                                src = rbx_d[row0:row0 + (qb - qa) * 64, :].rearrange("(q k) c -> k q c", k=64)
                                krb = kr % 2
                                S.dma("pool", mb.t[krb * 64:(krb + 1) * 64, v, kr // 2, qa * 64:qb * 64].rearrange("p (q c) -> p q c", c=64),
                                      src, [Bconst], [mb], mb)
                        qsrc, ksrc, row0, nrows, voff = naq_d, nak_d, hh * 64, 64, hh * 64
                    elif kind == "da":
                        scale = 32 ** -0.5
                        qsrc, ksrc, row0, nrows, voff = daq_d, dak_d, hh * 64, 64, 384 + hh * 64
                    else:
                        scale = 96 ** -0.5
                        qsrc, ksrc, row0, nrows, voff = mlq_d, mlk_d, hh * 96, 96, 768 + hh * 64
                    orow = voff
                    for s in range(3):
                        n = cfg.seqn[s]
                        st = cfg.start[s]
                        mcol = R + NMETA * s
                        nt = n // 128
                        Kt = ktring.next()
                        Km = kmring.next()
                        Vv = vring.next()
                        Vm = vmring.next()
                        if kind == "ml":
                            S.dma("sp", Kt.t[0:64, :n], mlk_d[hh * 64:(hh + 1) * 64, st:st + n], [Bqk], [Kt], Kt)
                            S.dma("sp", Kt.t[64:96, :n], mlr_d[0:32, st:st + n], [Bqk], [Kt], Kt)
                            S.dma("sp", Km.t[0:64, :], mlk_d[hh * 64:(hh + 1) * 64, mcol:mcol + NMETA], [Bqk], [Km], Km)
                            S.dma("sp", Km.t[64:96, :], mlr_d[0:32, mcol:mcol + NMETA], [Bqk], [Km], Km)
                        else:
                            S.dma("sp", Kt.t[0:64, :n], ksrc[row0:row0 + 64, st:st + n], [Bqk], [Kt], Kt)
                            S.dma("sp", Km.t[0:64, :], ksrc[row0:row0 + 64, mcol:mcol + NMETA], [Bqk], [Km], Km)
                        for t0 in range(0, nt, 16):
                            t1_ = min(nt, t0 + 16)
                            S.dma("sp", Vv.t[:, t0:t1_, 0:64],
                                  v_d[st + t0 * 128:st + t1_ * 128, voff:voff + 64].rearrange("(t p) e -> p t e", p=128), [Bv], [Vv], Vv)
                        S.dma("sp", Vm.t[:, 0:64], v_d[mcol:mcol + NMETA, voff:voff + 64], [Bv], [Vm], Vm)

                        qtiles = [(st + q0, 512, q0 // 512) for q0 in range(0, n, 512)]
                        if not last:
                            qtiles.append((mcol, NMETA, -1))
                        nblk = n // 512
                        rows = n // 64
                        for (qc0, Nq, qb_) in qtiles:
                            if kind == "da":
                                Q0 = q0ring.next()
                                Q1 = q1ring.next()
                                S.dma("sp", Q0.t[0:32, :Nq], qsrc[row0:row0 + 32, qc0:qc0 + Nq], [Bqk], [Q0], Q0)
                                S.dma("sp", Q1.t[32:64, :Nq], qsrc[row0 + 32:row0 + 64, qc0:qc0 + Nq], [Bqk], [Q1], Q1)
                                streams = [Q0, Q1]
                            else:
                                Q = qaring.next()
                                S.dma("sp", Q.t[:nrows, :Nq], qsrc[row0:row0 + nrows, qc0:qc0 + Nq], [Bqk], [Q], Q)
                                streams = [Q]
                            meta_kt = (Km, (lambda Km=Km: Km.t[:, 0:NMETA]), Vm, Vm.t[:NMETA, 0:128], NMETA, None)
                            if kind == "na":
                                if qb_ < 0:
                                    ktl = [meta_kt]
                                else:
                                    v = 0 if qb_ == 0 else (2 if qb_ == nblk - 1 else 1)
                                    w0 = min(max(8 * qb_ - 4, 0), rows - 16)
                                    kt0 = w0 // 2
                                    ktl = []
                                    for j in range(8):
                                        kt = kt0 + j
                                        ktl.append((Kt, (lambda kt=kt, Kt=Kt: Kt.t[:, kt * 128:(kt + 1) * 128]), Vv, Vv.t[:, kt, 0:128], 128,
                                                    (mb, mb.t[:, v, j, :])))
                                    ktl.append(meta_kt)
                            else:
                                ktl = [(Kt, (lambda kt=kt, Kt=Kt: Kt.t[:, kt * 128:(kt + 1) * 128]), Vv, Vv.t[:, kt, 0:128], 128, None)
                                       for kt in range(nt)]
                                ktl.append(meta_kt)
                            O = run_qtile(streams, Nq, ktl, scale)
                            oo = ooring.next()
                            key = qc0 if qc0 < R else R
                            pending.extend(post_stages(kind, O, Nq, oo, orow, qc0, key))
                drain(10 ** 9)
                S.barrier()
                S.release(phase_bufs)
                del phase_bufs[:]

        if STOP >= 1:
            tok_phase(0)
        if STOP >= 2:
            att_phase(0, False)
        if STOP >= 3:
            tok_phase(1)
        if STOP >= 4:
            att_phase(1, True)
        if STOP >= 5:
            tok_phase(2)
        S.barrier()
    return nc


_CACHE = {}


def run(cfg, inp):
    sh = _host_shared(cfg, inp)
    if "nc" not in _CACHE or _CACHE.get("cfg") != (cfg.NP, cfg.NS, cfg.DFF):
        _CACHE["nc"] = build_program(cfg)
        _CACHE["cfg"] = (cfg.NP, cfg.NS, cfg.DFF)
    nc = _CACHE["nc"]
    xp, xs = inp["x_prompt"], inp["x_sample"]
    in_maps = []
    for c in range(NCORES):
        m = dict(sh)
        m["xtok"] = np.ascontiguousarray(np.concatenate([xp[c], xs[2 * c], xs[2 * c + 1]], axis=0).astype(np.float32))
        in_maps.append(m)
    res = run_bass_kernel_spmd(nc, in_maps, core_ids=list(range(NCORES)))
    _CACHE["res"] = res
    yp = np.empty(xp.shape, np.float32)
    ys = np.empty(xs.shape, np.float32)
    for c in range(NCORES):
        y = res.results[c]["y"]
        yp[c] = y[:cfg.NP]
        ys[2 * c] = y[cfg.NP:cfg.NP + cfg.NS]
        ys[2 * c + 1] = y[cfg.NP + cfg.NS:]
    return yp, ys


def kernel(**inputs):
    inp = {k: np.asarray(v) for k, v in inputs.items()}
    cfg = Cfg(inp["x_prompt"].shape[1], inp["x_sample"].shape[1], inp["ffn_w_gate"].shape[-1])
    return run(cfg, inp)
```

```python
import math
from contextlib import ExitStack
import numpy as np
import concourse.bass as bass
import concourse.mybir as mybir
from concourse.bass_utils import run_bass_kernel_spmd

F32 = mybir.dt.float32
BF16 = mybir.dt.bfloat16
AF = mybir.ActivationFunctionType
ALU = mybir.AluOpType

D = 1024
NMETA = 16
EPS = 1e-6
NEG = -30000.0
NCORES = 8
ROPE_THETA = 500000.0
DEBUG = False
STOP = 5
SUB = 9


class Buf:
    def __init__(self, name, t=None, dram=False):
        self.name = name
        self.t = t
        self.dram = dram
        self.w = {}
        self.rd = {}
        self.sem = None
        self.cnt = 0

    def add_rd(self, tok):
        sid = id(tok[0])
        if self.rd.get(sid, (None, 0))[1] < tok[1]:
            self.rd[sid] = tok

    def set_w(self, tok):
        sid = id(tok[0])
        if self.w.get(sid, (None, 0))[1] < tok[1]:
            self.w[sid] = tok


class Ring:
    def __init__(self, bufs):
        self.bufs = bufs
        self.i = 0

    def next(self):
        b = self.bufs[self.i % len(self.bufs)]
        self.i += 1
        return b


class Sched:
    def __init__(self, nc, stack):
        self.nc = nc
        self.stack = stack
        self.E = {"pe": nc.tensor, "act": nc.scalar, "dve": nc.vector, "pool": nc.gpsimd, "sp": nc.sync}
        self.esem = {}
        for e in ("pe", "act", "dve", "pool"):
            self.esem[e] = stack.enter_context(nc.semaphore("es_" + e))
        self.cnt = {e: 0 for e in self.E}
        self.seen = {e: {} for e in self.E}
        self.semcnt = {}
        self.free_sems = []
        self.nsem = 4
        self.ps = None
        self.psi = 0

    def _waits(self, eng, reads, writes):
        toks = {}
        own = self.esem.get(eng)

        def add(d, raw):
            for sid, (sem, val) in d.items():
                if sem is own and (eng == "pe" or (not raw and eng != "pool")):
                    continue
                if toks.get(sid, (None, 0))[1] < val:
                    toks[sid] = (sem, val)

        for b in reads:
            add(b.w, True)
        for b in writes:
            add(b.w, False)
            add(b.rd, False)
        e = self.E[eng]
        seen = self.seen[eng]
        for sid, (sem, val) in toks.items():
            if seen.get(sid, 0) >= val:
                continue
            seen[sid] = val
            e.wait_ge(sem, val)

    def op(self, eng, fn, reads=(), writes=(), sig=True):
        self._waits(eng, reads, writes)
        ins = fn(self.E[eng])
        if sig:
            self.cnt[eng] += 1
            ins.then_inc(self.esem[eng], 1)
            idx = self.cnt[eng]
        else:
            idx = self.cnt[eng] + 1
        tok = (self.esem[eng], idx)
        for b in reads:
            b.add_rd(tok)
        for b in writes:
            b.set_w(tok)

    def dma(self, eng, out, in_, reads, writes, sb):
        self._waits(eng, reads, writes)
        if sb.sem is None:
            if self.free_sems:
                sb.sem, sb.cnt = self.free_sems.pop()
            else:
                sb.sem = self.stack.enter_context(self.nc.semaphore("ds_%d" % self.nsem))
                self.nsem += 1
        self.E[eng].dma_start(out=out, in_=in_).then_inc(sb.sem, 16)
        sb.cnt += 16
        self.semcnt[id(sb.sem)] = (sb.sem, sb.cnt)
        tok = (sb.sem, sb.cnt)
        for b in reads:
            b.add_rd(tok)
        for b in writes:
            b.set_w(tok)

    def barrier(self):
        for eng, e in self.E.items():
            seen = self.seen[eng]
            for o, sem in self.esem.items():
                v = self.cnt[o]
                if v > 0 and seen.get(id(sem), 0) < v and o != eng:
                    seen[id(sem)] = v
                    e.wait_ge(sem, v)
            for sid, (sem, v) in self.semcnt.items():
                if seen.get(sid, 0) < v:
                    seen[sid] = v
                    e.wait_ge(sem, v)

    def release(self, bufs):
        for b in bufs:
            if b.sem is not None:
                self.free_sems.append((b.sem, b.cnt))
                b.sem = None

    def psum(self):
        b = self.ps[self.psi % 8]
        self.psi += 1
        return b


class Cfg:
    def __init__(self, NP, NS, DFF):
        self.NP, self.NS, self.DFF = NP, NS, DFF
        self.FC = DFF // 128
        self.seqn = [NP, NS, NS]
        self.start = [0, NP, NP + NS]
        self.R = NP + 2 * NS
        self.TT = self.R + 3 * NMETA
        self.NMAX = max(self.seqn)


NA_Q, NA_K, NA_V, DA_Q, DA_K, DA_V, M_CQ, M_CKV, M_KR = 0, 384, 768, 1152, 1536, 1920, 2304, 2560, 2688
NCH_IN = 23


def _win_cols():
    idx = -np.ones((NCH_IN, 128), np.int64)
    r = np.arange(128)
    for i in range(3):
        idx[i] = NA_Q + 128 * i + r
        idx[3 + i] = NA_K + 128 * i + r
        dd = r % 32
        pr = np.where(dd < 4, r + 4, np.where(dd < 8, r - 4, r))
        idx[6 + i] = DA_Q + 128 * i + r
        idx[9 + i] = DA_Q + 128 * i + pr
        idx[12 + i] = DA_K + 128 * i + r
        idx[15 + i] = DA_K + 128 * i + pr
    idx[18] = M_CQ + r
    idx[19] = M_CQ + 128 + r
    idx[20] = M_CKV + r
    idx[21, :32] = M_KR + np.arange(32)
    idx[22, :16] = M_KR + np.arange(16) + 16
    idx[22, 16:32] = M_KR + np.arange(16)
    return idx.reshape(-1)


def _gather_cols(w, idx):
    out = w[:, np.maximum(idx, 0)]
    out = np.where(idx[None, :] >= 0, out, np.float32(0.0))
    return np.ascontiguousarray(out.astype(np.float32))


def _host_shared(cfg, inp):
    FC = cfg.FC
    sh = {}
    g, u, dn = inp["ffn_w_gate"], inp["ffn_w_up"], inp["ffn_w_down"]
    wgu = np.empty((4, 2, FC, 128, 8 * 128), np.float32)
    wd = np.empty((4, 8, 128, FC * 128), np.float32)
    for l in range(2):
        for j in range(2):
            lj = l * 2 + j
            wgu[lj, 0] = g[l, j].reshape(8, 128, FC, 128).transpose(2, 1, 0, 3).reshape(FC, 128, 1024)
            wgu[lj, 1] = u[l, j].reshape(8, 128, FC, 128).transpose(2, 1, 0, 3).reshape(FC, 128, 1024)
            wd[lj] = dn[l, j].reshape(FC, 128, 8, 128).transpose(2, 1, 0, 3).reshape(8, 128, FC * 128)
    sh["wgu"] = wgu.reshape(4 * 2 * FC * 128, 1024)
    sh["wd"] = wd.reshape(4 * 8 * 128, FC * 128)
    idx = _win_cols()
    win = np.empty((2, NCH_IN, 128, 1024), np.float32)
    wv = np.empty((2, 128, 8 * 768), np.float32)
    wuq = np.empty((2, 8, 128, 256), np.float32)
    wukv = np.empty((2, 128, 512), np.float32)
    wo = np.empty((2, 8, 128, 1024), np.float32)
    f = np.arange(128)
    for l in range(2):
        w = inp["w_in"][l]
        we = _gather_cols(w, idx)
        win[l] = we.reshape(8, 128, NCH_IN, 128).transpose(2, 1, 0, 3).reshape(NCH_IN, 128, 1024)
        vcols = np.concatenate([NA_V + np.arange(384), DA_V + np.arange(384)])
        wv[l] = w[:, vcols].reshape(8, 128, 768).transpose(1, 0, 2).reshape(128, 8 * 768)
        uq = inp["mla_w_uq"][l]
        for h in range(4):
            im = np.where(f < 96, h * 96 + f, -1)
            ip = np.where((f >= 64) & (f < 80), h * 96 + f + 16, np.where((f >= 80) & (f < 96), h * 96 + f - 16, -1))
            for k, ii in ((0, im), (1, ip)):
                m = _gather_cols(uq, ii)
                wuq[l, h * 2 + k] = m.reshape(2, 128, 128).transpose(1, 0, 2).reshape(128, 256)
        ukv = inp["mla_w_ukv"][l]
        kc = np.concatenate([h * 128 + np.arange(64) for h in range(4)])
        vc = np.concatenate([h * 128 + 64 + np.arange(64) for h in range(4)])
        wukv[l] = np.concatenate([ukv[:, kc], ukv[:, vc]], axis=1)
        wo[l] = inp["w_out"][l].reshape(8, 128, 8, 128).transpose(2, 1, 0, 3).reshape(8, 128, 1024)
    sh["win"] = win.reshape(2 * NCH_IN * 128, 1024)
    sh["wv"] = wv.reshape(2 * 128, 8 * 768)
    sh["wuq"] = wuq.reshape(2 * 8 * 128, 256)
    sh["wukv"] = wukv.reshape(2 * 128, 512)
    sh["wo"] = wo.reshape(2 * 8 * 128, 1024)
    cols = []
    for l in range(2):
        for i in range(3):
            cols.append(inp["norm_g"][l, i].reshape(8, 128).T)
    cols.append(inp["final_norm_g"].reshape(8, 128).T)
    for l in range(2):
        cols.append(inp["mla_q_norm_g"][l].reshape(2, 128).T)
    for l in range(2):
        cols.append(inp["mla_kv_norm_g"][l].reshape(1, 128).T)
    for l in range(2):
        c = np.zeros((128, 1), np.float32)
        c[:64, 0] = inp["da_subln_g"][l]
        cols.append(c)
    sh["gcols"] = np.ascontiguousarray(np.concatenate(cols, axis=1).astype(np.float32))
    lam = np.empty((2, 128), np.float32)
    for l in range(2):
        lp = inp["da_lambda"][l]
        lam[l] = np.concatenate([lp[0], lp[2], lp[1], lp[3]])
    sh["dalam"] = lam
    kc = np.arange(64)[:, None]
    qc = np.arange(64)[None, :]
    cs = np.clip(qc - 8, 0, 48)
    valid = (kc >= cs) & (kc < cs + 16)
    ci = np.clip(kc - qc + 15, 0, 30)
    rb = inp["na_rel_bias"]
    rbx = rb[:, :, ::-1, :][:, :, :, ci]
    rbx = np.where(valid[None, None, None], rbx, np.float32(NEG)).astype(np.float32)
    sh["rbx"] = np.ascontiguousarray(rbx.reshape(2 * 6 * 15 * 64, 64))
    pos = np.empty(cfg.TT, np.float32)
    for s in range(3):
        pos[cfg.start[s]:cfg.start[s] + cfg.seqn[s]] = NMETA + np.arange(cfg.seqn[s], dtype=np.float32)
        pos[cfg.R + NMETA * s:cfg.R + NMETA * (s + 1)] = np.arange(NMETA, dtype=np.float32)

    def tables(dim):
        inv = (np.float32(ROPE_THETA) ** (-(np.arange(0, dim, 2, dtype=np.float32) / np.float32(dim)))).astype(np.float32)
        ang = pos[:, None] * inv[None, :]
        return np.cos(ang).astype(np.float32).T, np.sin(ang).astype(np.float32).T

    c8, s8 = tables(8)
    t = np.zeros((2, 32, cfg.TT), np.float32)
    t[0, :] = 1.0
    t[0, 0:4] = c8
    t[0, 4:8] = c8
    t[1, 0:4] = -s8
    t[1, 4:8] = s8
    sh["ropeda"] = np.ascontiguousarray(t.reshape(64, cfg.TT))
    c32, s32 = tables(32)
    t = np.zeros((2, 32, cfg.TT), np.float32)
    t[0, 0:16] = c32
    t[0, 16:32] = c32
    t[1, 0:16] = -s32
    t[1, 16:32] = s32
    sh["ropeml"] = np.ascontiguousarray(t.reshape(64, cfg.TT))
    sh["ident"] = np.eye(128, dtype=np.float32)
    sh["metatok"] = np.ascontiguousarray(inp["meta_tokens"].astype(np.float32))
    return sh


def build_program(cfg):
    FC, R, TT = cfg.FC, cfg.R, cfg.TT
    nc = bass.Bass("TRN2", target_bir_lowering=False)
    stack = ExitStack()
    with stack:
        def din(name, shape):
            return nc.dram_tensor(name, list(shape), F32, kind="ExternalInput").ap()

        x_d = din("xtok", [R, D])
        meta_d = din("metatok", [NMETA, D])
        wgu_f = din("wgu", [4 * 2 * FC * 128, 1024])
        wd_f = din("wd", [4 * 8 * 128, FC * 128])
        win_f = din("win", [2 * NCH_IN * 128, 1024])
        wv_f = din("wv", [2 * 128, 8 * 768])
        wuq_f = din("wuq", [2 * 8 * 128, 256])
        wukv_f = din("wukv", [2 * 128, 512])
        wo_f = din("wo", [2 * 8 * 128, 1024])
        gcols_d = din("gcols", [128, 64])
        dalam_d = din("dalam", [2, 128])
        rbx_d = din("rbx", [2 * 6 * 15 * 64, 64])
        ropeda_d = din("ropeda", [64, TT])
        ropeml_d = din("ropeml", [64, TT])
        ident_d = din("ident", [128, 128])
        y_d = nc.dram_tensor("y", [R, D], F32, kind="ExternalOutput").ap()

        def dscr(name, shape, dt=BF16):
            if DEBUG and name.endswith("_s"):
                return nc.dram_tensor(name, list(shape), dt, kind="ExternalOutput").ap()
            return nc.dram_tensor(name, list(shape), dt).ap()

        wgu_b = dscr("wgu_b", [4 * 2 * FC * 128, 1024])
        wd_b = dscr("wd_b", [4 * 8 * 128, FC * 128])
        win_b = dscr("win_b", [2 * NCH_IN * 128, 1024])
        wv_b = dscr("wv_b", [2 * 128, 8 * 768])
        wuq_b = dscr("wuq_b", [2 * 8 * 128, 256])
        wukv_b = dscr("wukv_b", [2 * 128, 512])
        wo_b = dscr("wo_b", [2 * 8 * 128, 1024])
        hT_d = dscr("hT_s", [D, TT], F32)
        oT_d = dscr("oT_s", [D, TT])
        naq_d = dscr("naq_s", [384, TT])
        nak_d = dscr("nak_s", [384, TT])
        daq_d = dscr("daq_s", [384, TT])
        dak_d = dscr("dak_s", [384, TT])
        mlq_d = dscr("mlq_s", [384, TT])
        mlk_d = dscr("mlk_s", [256, TT])
        mlr_d = dscr("mlr_s", [32, TT])
        v_d = dscr("v_s", [TT, 1024])

        S = Sched(nc, stack)
        S.ps = [Buf("ps%d" % i, stack.enter_context(nc.psum_tensor("ps%d" % i, [128, 512], F32))) for i in range(8)]

        phase_bufs = []

        uniq = [0]

        def sb(st, name, shape, dt):
            uniq[0] += 1
            name = "s%d_%s" % (uniq[0], name)
            b = Buf(name, st.enter_context(nc.sbuf_tensor(name, list(shape), dt)))
            if st is not stack:
                phase_bufs.append(b)
            return b

        Bx = Buf("x", dram=True)
        Bconst = Buf("const", dram=True)
        Bw = {k: Buf("w_" + k, dram=True) for k in ("gu0", "gu1", "gu2", "gu3", "d0", "d1", "d2", "d3", "in0", "in1", "misc")}
        BhT = {}
        BoT = {}
        Bqk = Buf("qk", dram=True)
        Bv = Buf("v", dram=True)
        By = Buf("y", dram=True)

        ident = sb(stack, "ident", [128, 128], F32)
        onesb = sb(stack, "onesb", [128, 128], BF16)
        ones64 = sb(stack, "ones64", [64, 64], F32)
        sel65 = sb(stack, "sel65", [65, 64], F32)
        onesrow = sb(stack, "onesrow", [1, 64], F32)
        gcols = sb(stack, "gcols", [128, 64], F32)
        neglam = sb(stack, "neglam", [64, 2], F32)
        gsub = sb(stack, "gsub", [64, 2], F32)
        lamrow = sb(stack, "lamrow", [1, 256], F32)
        lamtmp = sb(stack, "lamtmp", [1, 128], F32)
        epscol = sb(stack, "epscol", [128, 1], F32)
        S.op("pool", lambda e: e.memset(epscol.t[:], EPS), [], [epscol])
        S.dma("sp", ident.t[:], ident_d[:, :], [Bconst], [ident], ident)
        S.dma("sp", gcols.t[:], gcols_d[:, :], [Bconst], [gcols], gcols)
        S.dma("sp", lamrow.t[0:1, :], dalam_d.rearrange("l f -> (l f)").rearrange("(o f) -> o f", o=1), [Bconst], [lamrow], lamrow)
        S.op("pool", lambda e: e.memset(onesb.t[:], 1.0), [], [onesb])
        S.op("pool", lambda e: e.memset(ones64.t[:], 1.0), [], [ones64])
        S.op("pool", lambda e: e.memset(sel65.t[:], 0.0), [], [sel65])
        S.op("pool", lambda e: e.memset(sel65.t[64:65, :], 1.0), [], [sel65])
        S.op("pool", lambda e: e.memset(onesrow.t[:], 1.0), [], [onesrow])

        late = []
        defer = [False]

        def convert(dst, src, rows, key, r0=0):
            r = r0
            while r < r0 + rows:
                n = min(128, r0 + rows - r)
                if defer[0]:
                    late.append((dst, src, r, n, key))
                else:
                    S.dma("pool", dst[r:r + n, :], src[r:r + n, :], [Bconst], [Bw[key]], Bw[key])
                r += n

        def emit_late(k):
            while late and k > 0:
                dst, src, r, n, key = late.pop(0)
                S.dma("pool", dst[r:r + n, :], src[r:r + n, :], [Bconst], [Bw[key]], Bw[key])
                k -= 1

        def conv_ffn(lj):
            convert(wgu_b, wgu_f, 2 * FC * 128, "gu%d" % lj, lj * 2 * FC * 128)
            convert(wd_b, wd_f, 8 * 128, "d%d" % lj, lj * 8 * 128)

        conv_ffn(0)
        convert(win_b, win_f, NCH_IN * 128, "in0", 0)
        convert(wv_b, wv_f, 2 * 128, "misc")
        convert(wuq_b, wuq_f, 2 * 8 * 128, "misc")
        convert(wukv_b, wukv_f, 2 * 128, "misc")
        convert(wo_b, wo_f, 2 * 8 * 128, "misc")
        defer[0] = True
        conv_ffn(1)
        conv_ffn(2)
        convert(win_b, win_f, NCH_IN * 128, "in1", NCH_IN * 128)
        conv_ffn(3)

        for l in range(2):
            lam_init = 0.8 - 0.6 * math.exp(-0.3 * l)
            A = lamrow.t[0:1, l * 128:l * 128 + 64]
            Bm = lamrow.t[0:1, l * 128 + 64:l * 128 + 128]
            S.op("dve", lambda e: e.tensor_tensor(lamtmp.t[0:1, 0:64], A, Bm, ALU.mult), [lamrow], [lamtmp])
            S.op("dve", lambda e: e.tensor_reduce(lamtmp.t[0:1, 64:66], lamtmp.t[0:1, 0:64].rearrange("o (g f) -> o g f", g=2),
                                                   mybir.AxisListType.X, ALU.add), [lamtmp], [lamtmp])
            S.op("act", lambda e: e.activation(out=lamtmp.t[0:1, 66:68], in_=lamtmp.t[0:1, 64:66], func=AF.Exp), [lamtmp], [lamtmp])
            S.op("dve", lambda e: e.tensor_tensor(lamtmp.t[0:1, 68:69], lamtmp.t[0:1, 67:68], lamtmp.t[0:1, 66:67], ALU.subtract), [lamtmp], [lamtmp])
            S.op("dve", lambda e: e.tensor_scalar(lamtmp.t[0:1, 69:70], lamtmp.t[0:1, 68:69], -lam_init, None, ALU.add), [lamtmp], [lamtmp])
            p = S.psum()
            S.op("pe", lambda e: e.matmul(p.t[0:64, 0:1], onesrow.t[0:1, 0:64], lamtmp.t[0:1, 69:70], start=True, stop=True), [onesrow, lamtmp], [p])
            S.op("dve", lambda e: e.tensor_copy(neglam.t[:, l:l + 1], p.t[0:64, 0:1]), [p], [neglam])
            S.op("dve", lambda e: e.tensor_scalar(gsub.t[:, l:l + 1], gcols.t[0:64, 62 + l:63 + l], 1.0 - lam_init, None, ALU.mult), [gcols], [gsub])

        cp_flip = [0]

        def evac(out_ap, in_ap, reads, writes):
            cp_flip[0] ^= 1
            if cp_flip[0]:
                S.op("act", lambda e: e.activation(out=out_ap, in_=in_ap, func=AF.Copy), reads, writes)
            else:
                S.op("dve", lambda e: e.tensor_copy(out_ap, in_ap), reads, writes)

        def rms_rstd(srcs, W, Dn, sqring, rsring):
            ps = S.psum()
            n = len(srcs)
            for c, (b, ap) in enumerate(srcs):
                sq = sqring.next()
                if c % 3 == 2:
                    S.op("pool", lambda e: e.tensor_tensor(sq.t[:, :W], ap, ap, ALU.mult), [b], [sq])
                else:
                    S.op("act", lambda e: e.activation(out=sq.t[:, :W], in_=ap, func=AF.Square), [b], [sq])
                S.op("pe", lambda e: e.matmul(ps.t[:, :W], onesb.t[:, :], sq.t[:, :W], start=(c == 0), stop=(c == n - 1)),
                     [sq, onesb], [ps], sig=True)
            rs = rsring.next()
            S.op("act", lambda e: e.activation(out=rs.t[:, :W], in_=ps.t[:, :W], func=AF.Ln, bias=epscol.t[:, 0:1], scale=1.0 / Dn), [ps, epscol], [rs])
            rs2 = rsring.next()
            S.op("act", lambda e: e.activation(out=rs2.t[:, :W], in_=rs.t[:, :W], func=AF.Exp, scale=-0.5), [rs], [rs2])
            return rs2

        def tok_phase(stage):
            with ExitStack() as ph:
                NS_ = 2
                h = [sb(ph, "h%d" % i, [128, 8, 512], F32) for i in range(NS_)]
                xn = [sb(ph, "xn%d" % i, [128, 8, 512], BF16) for i in range(NS_)]
                hid = [sb(ph, "hid%d" % i, [128, FC, 512], BF16) for i in range(NS_)]
                sqring = Ring([sb(ph, "sq%d" % i, [128, 512], BF16) for i in range(4)])
                rsring = Ring([sb(ph, "rs%d" % i, [128, 512], F32) for i in range(3)])
                sgring = Ring([sb(ph, "sg%d" % i, [128, 512], F32) for i in range(4)])
                guring = Ring([sb(ph, "gu%d" % i, [128, 2, 8, 128], BF16) for i in range(3)])
                wdring = Ring([sb(ph, "wdr%d" % i, [128, FC, 128], BF16) for i in range(2)])
                if stage >= 1:
                    o1 = sb(ph, "o1", [128, 8, 512], BF16)
                    woring = Ring([sb(ph, "wor%d" % i, [128, 8, 128], BF16) for i in range(2)])
                if stage == 0:
                    xsring = Ring([sb(ph, "xs%d" % i, [128, 4, 1024], F32) for i in range(1)])
                if stage <= 1:
                    winring = Ring([sb(ph, "winr%d" % i, [128, 8, 128], BF16) for i in range(2)])
                    wvt = sb(ph, "wvt", [128, 8, 768], BF16)
                    wuqt = sb(ph, "wuqt", [128, 8, 2, 128], BF16)
                    wukvt = sb(ph, "wukvt", [128, 512], BF16)
                    rda = [sb(ph, "rda%d" % i, [128, 2, 512], BF16) for i in range(NS_)]
                    rml = [sb(ph, "rml%d" % i, [96, 2, 512], BF16) for i in range(NS_)]
                    uoring = Ring([sb(ph, "uo%d" % i, [128, 512], BF16) for i in range(3)])
                    t1ring = sgring
                    t2ring = sgring
                    cq1 = sb(ph, "cq1", [128, 3, 512], F32)
                    cqn1 = sb(ph, "cqn1", [128, 3, 512], BF16)
                    cq = [cq1, cq1]
                    cqn = [cqn1, cqn1]
                    vtring = Ring([sb(ph, "vt%d" % i, [128, 1024], BF16) for i in range(2)])
                if stage == 2:
                    hn = [sb(ph, "hn%d" % i, [128, 8, 512], F32) for i in range(NS_)]
                    yring = Ring([sb(ph, "yt%d" % i, [128, 1024], F32) for i in range(2)])

                blocks = []
                for c0 in range(0, R, 1024):
                    blocks.append([(c0, 512), (c0 + 512, 512)])
                if stage <= 1:
                    blocks.append([(R, 3 * NMETA)])

                hT_v = hT_d.rearrange("(c p) t -> p c t", p=128)
                oT_v = oT_d.rearrange("(c p) t -> p c t", p=128)

                def ffn(lj, subs, gbase):
                    for i, (c0, W) in enumerate(subs):
                        rs = rms_rstd([(h[i], h[i].t[:, c, :W]) for c in range(8)], W, D, sqring, rsring)
                        if SUB < 1.4:
                            continue
                        for c in range(8):
                            S.op("dve", lambda e: e.scalar_tensor_tensor(out=xn[i].t[:, c, :W], in0=h[i].t[:, c, :W],
                                                                          scalar=gcols.t[:, gbase + c:gbase + c + 1], in1=rs.t[:, :W],
                                                                          op0=ALU.mult, op1=ALU.mult), [h[i], rs, gcols], [xn[i]])
                    if SUB < 1.6:
                        return
                    Bg = Bw["gu%d" % lj]
                    Bd = Bw["d%d" % lj]
                    for fc in range(FC):
                        w = guring.next()
                        for k in range(2):
                            r0 = ((lj * 2 + k) * FC + fc) * 128
                            S.dma("sp", w.t[:, k].rearrange("p c f -> p (c f)"), wgu_b[r0:r0 + 128, :], [Bg], [w], w)
                        for i, (c0, W) in enumerate(subs):
                            pg = S.psum()
                            pu = S.psum()
                            for c in range(8):
                                S.op("pe", lambda e: e.matmul(pg.t[:, :W], w.t[:, 0, c, :], xn[i].t[:, c, :W], start=(c == 0), stop=(c == 7)),
                                     [w, xn[i]], [pg], sig=(c == 7))
                            for c in range(8):
                                S.op("pe", lambda e: e.matmul(pu.t[:, :W], w.t[:, 1, c, :], xn[i].t[:, c, :W], start=(c == 0), stop=(c == 7)),
                                     [w, xn[i]], [pu], sig=(c == 7))
                            sg = sgring.next()
                            S.op("act", lambda e: e.activation(out=sg.t[:, :W], in_=pg.t[:, :W], func=AF.Silu), [pg], [sg])
                            S.op("dve", lambda e: e.tensor_tensor(hid[i].t[:, fc, :W], pu.t[:, :W], sg.t[:, :W], ALU.mult), [pu, sg], [hid[i]])
                    if SUB < 1.8:
                        return
                    for dc in range(8):
                        w = wdring.next()
                        r0 = (lj * 8 + dc) * 128
                        S.dma("sp", w.t[:].rearrange("p c f -> p (c f)"), wd_b[r0:r0 + 128, :], [Bd], [w], w)
                        for i, (c0, W) in enumerate(subs):
                            py = S.psum()
                            for fc in range(FC):
                                S.op("pe", lambda e: e.matmul(py.t[:, :W], w.t[:, fc, :], hid[i].t[:, fc, :W], start=(fc == 0), stop=(fc == FC - 1)),
                                     [w, hid[i]], [py], sig=(fc == FC - 1))
                            S.op("dve", lambda e: e.scalar_tensor_tensor(out=h[i].t[:, dc, :W], in0=py.t[:, :W], scalar=0.5, in1=h[i].t[:, dc, :W],
                                                                          op0=ALU.mult, op1=ALU.add), [py, h[i]], [h[i]])

                def wout(l, subs):
                    for i, (c0, W) in enumerate(subs):
                        S.dma("sp", o1.t[:, :, :W], oT_v[:, :, c0:c0 + W], [BoT[c0]], [o1], o1)
                        for dc in range(8):
                            w = woring.next()
                            r0 = (l * 8 + dc) * 128
                            S.dma("sp", w.t[:].rearrange("p c f -> p (c f)"), wo_b[r0:r0 + 128, :], [Bw["misc"]], [w], w)
                            py = S.psum()
                            for fc in range(8):
                                S.op("pe", lambda e: e.matmul(py.t[:, :W], w.t[:, fc, :], o1.t[:, fc, :W], start=(fc == 0), stop=(fc == 7)),
                                     [w, o1], [py], sig=(fc == 7))
                            S.op("dve", lambda e: e.tensor_tensor(h[i].t[:, dc, :W], py.t[:, :W], h[i].t[:, dc, :W], ALU.add), [py, h[i]], [h[i]])

                def proj_in(l, subs):
                    gb = (l * 3 + 1) * 8
                    for i, (c0, W) in enumerate(subs):
                        rs = rms_rstd([(h[i], h[i].t[:, c, :W]) for c in range(8)], W, D, sqring, rsring)
                        for c in range(8):
                            S.op("dve", lambda e: e.scalar_tensor_tensor(out=xn[i].t[:, c, :W], in0=h[i].t[:, c, :W],
                                                                          scalar=gcols.t[:, gb + c:gb + c + 1], in1=rs.t[:, :W],
                                                                          op0=ALU.mult, op1=ALU.mult), [h[i], rs, gcols], [xn[i]])
                        for gq in range(4):
                            S.dma("pool", rda[i].t[gq * 32:(gq + 1) * 32, :, :W],
                                  ropeda_d.rearrange("(a r) t -> r a t", a=2)[:, :, c0:c0 + W], [Bconst], [rda[i]], rda[i])
                        for base in (0, 64):
                            S.dma("pool", rml[i].t[base:base + 32, :, :W],
                                  ropeml_d.rearrange("(a r) t -> r a t", a=2)[:, :, c0:c0 + W], [Bconst], [rml[i]], rml[i])
                    Bi = Bw["in%d" % l]
                    S.dma("sp", wvt.t[:].rearrange("p c f -> p (c f)"), wv_b[l * 128:(l + 1) * 128, :], [Bw["misc"]], [wvt], wvt)
                    S.dma("sp", wuqt.t[:], wuq_b[l * 1024:(l + 1) * 1024, :].rearrange("(j p) (c f) -> p j c f", p=128, c=2),
                          [Bw["misc"]], [wuqt], wuqt)
                    S.dma("sp", wukvt.t[:], wukv_b[l * 128:(l + 1) * 128, :], [Bw["misc"]], [wukvt], wukvt)

                    def load_chunk(j):
                        w = winring.next()
                        r0 = (l * NCH_IN + j) * 128
                        S.dma("sp", w.t[:].rearrange("p c f -> p (c f)"), win_b[r0:r0 + 128, :], [Bi], [w], w)
                        return w

                    def mm_chunk(w, i, W, M):
                        p = S.psum()
                        for c in range(8):
                            S.op("pe", lambda e: e.matmul(p.t[:M, :W], w.t[:, c, :M], xn[i].t[:, c, :W], start=(c == 0), stop=(c == 7)),
                                 [w, xn[i]], [p], sig=(c == 7))
                        return p

                    def store_rows(dst, row0, M, uo, c0, W):
                        S.dma("pool", dst[row0:row0 + M, c0:c0 + W], uo.t[:M, :W], [uo], [Bqk], uo)

                    for j in range(6):
                        w = load_chunk(j)
                        for i, (c0, W) in enumerate(subs):
                            p = mm_chunk(w, i, W, 128)
                            uo = uoring.next()
                            evac(uo.t[:, :W], p.t[:, :W], [p], [uo])
                            store_rows(naq_d if j < 3 else nak_d, (j % 3) * 128, 128, uo, c0, W)
                    for grp, dst in ((6, daq_d), (12, dak_d)):
                        for jj in range(3):
                            wm = load_chunk(grp + jj)
                            wp = load_chunk(grp + 3 + jj)
                            for i, (c0, W) in enumerate(subs):
                                pm = mm_chunk(wm, i, W, 128)
                                pp = mm_chunk(wp, i, W, 128)
                                t1 = t1ring.next()
                                t2 = t2ring.next()
                                S.op("dve", lambda e: e.tensor_tensor(t1.t[:, :W], pm.t[:, :W], rda[i].t[:, 0, :W], ALU.mult), [pm, rda[i]], [t1])
                                S.op("dve", lambda e: e.tensor_tensor(t2.t[:, :W], pp.t[:, :W], rda[i].t[:, 1, :W], ALU.mult), [pp, rda[i]], [t2])
                                uo = uoring.next()
                                S.op("pool", lambda e: e.tensor_tensor(uo.t[:, :W], t1.t[:, :W], t2.t[:, :W], ALU.add), [t1, t2], [uo])
                                store_rows(dst, jj * 128, 128, uo, c0, W)
                    wm = load_chunk(21)
                    wp = load_chunk(22)
                    for i, (c0, W) in enumerate(subs):
                        pm = mm_chunk(wm, i, W, 32)
                        pp = mm_chunk(wp, i, W, 32)
                        t1 = t1ring.next()
                        t2 = t2ring.next()
                        S.op("dve", lambda e: e.tensor_tensor(t1.t[:32, :W], pm.t[:32, :W], rml[i].t[0:32, 0, :W], ALU.mult), [pm, rml[i]], [t1])
                        S.op("dve", lambda e: e.tensor_tensor(t2.t[:32, :W], pp.t[:32, :W], rml[i].t[0:32, 1, :W], ALU.mult), [pp, rml[i]], [t2])
                        uo = uoring.next()
                        S.op("pool", lambda e: e.tensor_tensor(uo.t[:32, :W], t1.t[:32, :W], t2.t[:32, :W], ALU.add), [t1, t2], [uo])
                        store_rows(mlr_d, 0, 32, uo, c0, W)
                    for i, (c0, W) in enumerate(subs):
                        for jj in range(3):
                            w = load_chunk(18 + jj)
                            p = mm_chunk(w, i, W, 128)
                            evac(cq[i].t[:, jj, :W], p.t[:, :W], [p], [cq[i]])
                        rs = rms_rstd([(cq[i], cq[i].t[:, c, :W]) for c in range(2)], W, 256, sqring, rsring)
                        for c in range(2):
                            S.op("dve", lambda e: e.scalar_tensor_tensor(out=cqn[i].t[:, c, :W], in0=cq[i].t[:, c, :W],
                                                                          scalar=gcols.t[:, 56 + 2 * l + c:57 + 2 * l + c], in1=rs.t[:, :W],
                                                                          op0=ALU.mult, op1=ALU.mult), [cq[i], rs, gcols], [cqn[i]])
                        rs = rms_rstd([(cq[i], cq[i].t[:, 2, :W])], W, 128, sqring, rsring)
                        S.op("dve", lambda e: e.scalar_tensor_tensor(out=cqn[i].t[:, 2, :W], in0=cq[i].t[:, 2, :W],
                                                                      scalar=gcols.t[:, 60 + l:61 + l], in1=rs.t[:, :W],
                                                                      op0=ALU.mult, op1=ALU.mult), [cq[i], rs, gcols], [cqn[i]])
                        for hh in range(4):
                            pm = S.psum()
                            pp = S.psum()
                            for c in range(2):
                                S.op("pe", lambda e: e.matmul(pm.t[:96, :W], wuqt.t[:, 2 * hh, c, 0:96], cqn[i].t[:, c, :W], start=(c == 0), stop=(c == 1)),
                                     [wuqt, cqn[i]], [pm], sig=(c == 1))
                            for c in range(2):
                                S.op("pe", lambda e: e.matmul(pp.t[:96, :W], wuqt.t[:, 2 * hh + 1, c, 0:96], cqn[i].t[:, c, :W], start=(c == 0), stop=(c == 1)),
                                     [wuqt, cqn[i]], [pp], sig=(c == 1))
                            uo = uoring.next()
                            t1 = t1ring.next()
                            t2 = t2ring.next()
                            S.op("act", lambda e: e.activation(out=uo.t[0:64, :W], in_=pm.t[0:64, :W], func=AF.Copy), [pm], [uo])
                            S.op("dve", lambda e: e.tensor_tensor(t1.t[64:96, :W], pm.t[64:96, :W], rml[i].t[64:96, 0, :W], ALU.mult), [pm, rml[i]], [t1])
                            S.op("dve", lambda e: e.tensor_tensor(t2.t[64:96, :W], pp.t[64:96, :W], rml[i].t[64:96, 1, :W], ALU.mult), [pp, rml[i]], [t2])
                            S.op("pool", lambda e: e.tensor_tensor(uo.t[64:96, :W], t1.t[64:96, :W], t2.t[64:96, :W], ALU.add), [t1, t2, uo], [uo])
                            store_rows(mlq_d, hh * 96, 96, uo, c0, W)
                        for kk in range(2):
                            p = S.psum()
                            S.op("pe", lambda e: e.matmul(p.t[:, :W], wukvt.t[:, kk * 128:(kk + 1) * 128], cqn[i].t[:, 2, :W], start=True, stop=True),
                                 [wukvt, cqn[i]], [p])
                            uo = uoring.next()
                            evac(uo.t[:, :W], p.t[:, :W], [p], [uo])
                            store_rows(mlk_d, kk * 128, 128, uo, c0, W)
                        ng = (W + 127) // 128
                        for g in range(ng):
                            gw = min(128, W - g * 128)
                            vt = vtring.next()
                            for half in range(2):
                                p = S.psum()
                                for c in range(8):
                                    S.op("pe", lambda e: e.matmul(p.t[:gw, :384], xn[i].t[:, c, g * 128:g * 128 + gw], wvt.t[:, c, half * 384:(half + 1) * 384],
                                                                   start=(c == 0), stop=(c == 7)), [xn[i], wvt], [p], sig=(c == 7))
                                evac(vt.t[:gw, half * 384:(half + 1) * 384], p.t[:gw, :384], [p], [vt])
                            p = S.psum()
                            S.op("pe", lambda e: e.matmul(p.t[:gw, :256], cqn[i].t[:, 2, g * 128:g * 128 + gw], wukvt.t[:, 256:512], start=True, stop=True),
                                 [cqn[i], wukvt], [p])
                            evac(vt.t[:gw, 768:1024], p.t[:gw, :256], [p], [vt])
                            S.dma("pool", v_d[c0 + g * 128:c0 + g * 128 + gw, :], vt.t[:gw, :], [vt], [Bv], vt)

                for bi, subs in enumerate(blocks):
                    is_meta = subs[0][0] >= R
                    for i, (c0, W) in enumerate(subs):
                        key = c0
                        if key not in BhT:
                            BhT[key] = Buf("hT%d" % key, dram=True)
                            BoT[key] = Buf("oT%d" % key, dram=True)
                        if stage == 0:
                            xs = xsring.next()
                            if is_meta:
                                for s in range(3):
                                    S.dma("sp", xs.t[s * NMETA:(s + 1) * NMETA, 0, :], meta_d[:, :], [Bconst], [xs], xs)
                            else:
                                S.dma("sp", xs.t[:, :, :], x_d[c0:c0 + W, :].rearrange("(g p) d -> p g d", p=128), [Bx], [xs], xs)
                            ng = (W + 127) // 128
                            for c in range(8):
                                p = S.psum()
                                for g in range(ng):
                                    gw = min(128, W - g * 128)
                                    S.op("pe", lambda e: e.transpose(p.t[:, g * 128:g * 128 + gw], xs.t[:gw, g, c * 128:(c + 1) * 128], ident.t[:gw, :gw]),
                                         [xs, ident], [p], sig=(g == ng - 1))
                                evac(h[i].t[:, c, :W], p.t[:, :W], [p], [h[i]])
                        else:
                            S.dma("sp", h[i].t[:, :, :W], hT_v[:, :, c0:c0 + W], [BhT[key]], [h[i]], h[i])
                    if stage >= 1:
                        wout(stage - 1, subs)
                        ffn((stage - 1) * 2 + 1, subs, ((stage - 1) * 3 + 2) * 8)
                    if stage <= 1:
                        if SUB >= 1.2:
                            ffn(stage * 2, subs, (stage * 3) * 8)
                        if SUB >= 3:
                            proj_in(stage, subs)
                        for i, (c0, W) in enumerate(subs):
                            S.dma("pool", hT_v[:, :, c0:c0 + W], h[i].t[:, :, :W], [h[i]], [BhT[c0]], h[i])
                    else:
                        for i, (c0, W) in enumerate(subs):
                            rs = rms_rstd([(h[i], h[i].t[:, c, :W]) for c in range(8)], W, D, sqring, rsring)
                            for c in range(8):
                                S.op("dve", lambda e: e.scalar_tensor_tensor(out=hn[i].t[:, c, :W], in0=h[i].t[:, c, :W],
                                                                              scalar=gcols.t[:, 48 + c:49 + c], in1=rs.t[:, :W],
                                                                              op0=ALU.mult, op1=ALU.mult), [h[i], rs, gcols], [hn[i]])
                            for g in range(W // 128):
                                yt = yring.next()
                                for half in range(2):
                                    p = S.psum()
                                    for cc in range(4):
                                        c = half * 4 + cc
                                        S.op("pe", lambda e: e.transpose(p.t[:, cc * 128:(cc + 1) * 128], hn[i].t[:, c, g * 128:(g + 1) * 128], ident.t[:, :]),
                                             [hn[i], ident], [p], sig=(cc == 3))
                                    evac(yt.t[:, half * 512:(half + 1) * 512], p.t[:, :], [p], [yt])
                                S.dma("pool", y_d[c0 + g * 128:c0 + (g + 1) * 128, :], yt.t[:, :], [yt], [By], yt)
                S.barrier()
                S.release(phase_bufs)
                del phase_bufs[:]

        def att_phase(l, last):
            with ExitStack() as ph:
                NT = cfg.NMAX // 128
                ktring = Ring([sb(ph, "kt%d" % i, [128, cfg.NMAX], BF16) for i in range(2)])
                kmring = Ring([sb(ph, "km%d" % i, [128, NMETA], BF16) for i in range(2)])
                vring = Ring([sb(ph, "vv%d" % i, [128, NT, 128], BF16) for i in range(2)])
                vmring = Ring([sb(ph, "vm%d" % i, [NMETA, 128], BF16) for i in range(2)])
                qaring = Ring([sb(ph, "qa%d" % i, [128, 512], BF16) for i in range(3)])
                q0ring = Ring([sb(ph, "q0_%d" % i, [128, 512], BF16) for i in range(2)])
                q1ring = Ring([sb(ph, "q1_%d" % i, [128, 512], BF16) for i in range(2)])
                ptring = Ring([sb(ph, "pt%d" % i, [128, 512], BF16) for i in range(8)])
                tmring = Ring([sb(ph, "tm%d" % i, [128, 512], F32) for i in range(5)])
                mbring = Ring([sb(ph, "mb%d" % i, [128, 3, 8, 512], BF16) for i in range(2)])
                osring = Ring([sb(ph, "os%d" % i, [65, 512], F32) for i in range(4)])
                rdring = Ring([sb(ph, "rd%d" % i, [64, 512], F32) for i in range(8)])
                aring = Ring([sb(ph, "aa%d" % i, [64, 512], F32) for i in range(6)])
                ooring = Ring([sb(ph, "oo%d" % i, [64, 512], BF16) for i in range(4)])
                spool = Ring(S.ps[0:4])
                opool = Ring(S.ps[4:8])
                for b in ktring.bufs + kmring.bufs + qaring.bufs + q0ring.bufs + q1ring.bufs:
                    S.op("pool", lambda e: e.memset(b.t[:, :], 0.0), [], [b])
                for b in vring.bufs:
                    S.op("pool", lambda e: e.memset(b.t[:, :, 64:128], 1.0), [], [b])
                for b in vmring.bufs:
                    S.op("pool", lambda e: e.memset(b.t[:, 64:128], 1.0), [], [b])

                pending = []

                def drain(i):
                    while pending and pending[0][0] <= i:
                        pending.pop(0)[1]()

                def run_qtile(streams, Nq, ktl, scale):
                    O = [opool.next() for _ in streams]
                    n = len(ktl)

                    def qk(i):
                        res = []
                        Kb, kfn, Vb, vap, nk, mbap = ktl[i]
                        for Q in streams:
                            ps = spool.next()
                            S.op("pe", lambda e: e.matmul(ps.t[:nk, :Nq], kfn(), Q.t[:, :Nq], start=True, stop=True), [Kb, Q], [ps])
                            pt = ptring.next()
                            if mbap is not None:
                                tm = tmring.next()
                                S.op("dve", lambda e: e.scalar_tensor_tensor(out=tm.t[:nk, :Nq], in0=ps.t[:nk, :Nq], scalar=scale, in1=mbap[1][:nk, :Nq],
                                                                              op0=ALU.mult, op1=ALU.add), [ps, mbap[0]], [tm])
                                S.op("act", lambda e: e.activation(out=pt.t[:nk, :Nq], in_=tm.t[:nk, :Nq], func=AF.Exp), [tm], [pt])
                            else:
                                S.op("act", lambda e: e.activation(out=pt.t[:nk, :Nq], in_=ps.t[:nk, :Nq], func=AF.Exp, scale=scale), [ps], [pt])
                            res.append(pt)
                        return res

                    ahead = max(1, 4 // len(streams) - 1)
                    fifo = [qk(j) for j in range(min(ahead, n))]
                    for i in range(n):
                        if i + ahead < n:
                            fifo.append(qk(i + ahead))
                        cur = fifo.pop(0)
                        Kb, kfn, Vb, vap, nk, mbap = ktl[i]
                        for si in range(len(streams)):
                            S.op("pe", lambda e: e.matmul(O[si].t[:128, :Nq], vap, cur[si].t[:nk, :Nq], start=(i == 0), stop=(i == n - 1)),
                                 [Vb, cur[si]], [O[si]], sig=True)
                        drain(i)
                    drain(10 ** 9)
                    return O

                def post_stages(kind, O, Nq, oo, orow, qc0, key):
                    st = []
                    osb = [osring.next() for _ in O]
                    rd = [rdring.next() for _ in O]

                    def s_evac():
                        for si, Ob in enumerate(O):
                            S.op("dve", lambda e: e.tensor_copy(osb[si].t[:65, :Nq], Ob.t[:65, :Nq]), [Ob], [osb[si]])

                    def s_den(si):
                        def f():
                            Ob = O[si]
                            S.op("pe", lambda e: e.matmul(Ob.t[:64, :Nq], sel65.t[:65, :64], osb[si].t[:65, :Nq], start=True, stop=True), [sel65, osb[si]], [Ob])
                            if kind == "na":
                                lt = rdring.next()
                                S.op("act", lambda e: e.activation(out=lt.t[:64, :Nq], in_=Ob.t[:64, :Nq], func=AF.Ln), [Ob], [lt])
                                S.op("act", lambda e: e.activation(out=rd[si].t[:64, :Nq], in_=lt.t[:64, :Nq], func=AF.Exp, scale=-1.0), [lt], [rd[si]])
                            else:
                                S.op("dve", lambda e: e.reciprocal(rd[si].t[:64, :Nq], Ob.t[:64, :Nq]), [Ob], [rd[si]])
                        return f

                    def s_store():
                        S.dma("pool", oT_d[orow:orow + 64, qc0:qc0 + Nq], oo.t[:64, :Nq], [oo], [BoT[key]], oo)

                    st.append((1, s_evac))
                    st.append((3, s_den(0)))
                    if kind != "da":
                        def s_fin():
                            S.op("pool" if kind == "na" else "dve", lambda e: e.tensor_tensor(oo.t[:64, :Nq], osb[0].t[:64, :Nq], rd[0].t[:64, :Nq], ALU.mult), [osb[0], rd[0]], [oo])
                            s_store()
                        st.append((8, s_fin))
                        return st
                    st.append((6, s_den(1)))
                    a = aring.next()
                    b2 = aring.next()
                    sq = aring.next()
                    rs = rdring.next()
                    rs2 = rdring.next()

                    def s_comb():
                        S.op("dve", lambda e: e.tensor_tensor(a.t[:64, :Nq], osb[0].t[:64, :Nq], rd[0].t[:64, :Nq], ALU.mult), [osb[0], rd[0]], [a])
                        S.op("pool", lambda e: e.tensor_tensor(b2.t[:64, :Nq], osb[1].t[:64, :Nq], rd[1].t[:64, :Nq], ALU.mult), [osb[1], rd[1]], [b2])
                        S.op("dve", lambda e: e.scalar_tensor_tensor(out=a.t[:64, :Nq], in0=b2.t[:64, :Nq], scalar=neglam.t[:64, l:l + 1], in1=a.t[:64, :Nq],
                                                                      op0=ALU.mult, op1=ALU.add), [a, b2, neglam], [a])
                        S.op("pool", lambda e: e.tensor_tensor(sq.t[:64, :Nq], a.t[:64, :Nq], a.t[:64, :Nq], ALU.mult), [a], [sq])

                    def s_ss():
                        Ob = O[0]
                        S.op("pe", lambda e: e.matmul(Ob.t[:64, :Nq], ones64.t[:64, :64], sq.t[:64, :Nq], start=True, stop=True), [ones64, sq], [Ob])
                        S.op("act", lambda e: e.activation(out=rs.t[:64, :Nq], in_=Ob.t[:64, :Nq], func=AF.Ln, bias=epscol.t[:64, 0:1], scale=1.0 / 64), [Ob, epscol], [rs])
                        S.op("act", lambda e: e.activation(out=rs2.t[:64, :Nq], in_=rs.t[:64, :Nq], func=AF.Exp, scale=-0.5), [rs], [rs2])
                        S.op("dve", lambda e: e.scalar_tensor_tensor(out=oo.t[:64, :Nq], in0=a.t[:64, :Nq], scalar=gsub.t[:64, l:l + 1], in1=rs2.t[:64, :Nq],
                                                                      op0=ALU.mult, op1=ALU.mult), [a, rs2, gsub], [oo])
                        s_store()

                    st.append((11, s_comb))
                    st.append((14, s_ss))
                    return st

                heads = [("na", hh) for hh in range(6)] + [("da", hh) for hh in range(6)] + [("ml", hh) for hh in range(4)]
                for kind, hh in heads:
                    if kind == "na":
                        scale = 64 ** -0.5
                        mb = mbring.next()
                        S.op("pool", lambda e: e.memset(mb.t[:].rearrange("p a b c -> p (a b c)"), NEG), [], [mb])
                        for v in range(3):
                            delta = (0, 4, 8)[v]
                            for kr in range(16):
                                qs = []
                                for qr in range(8):
                                    if v == 0:
                                        lo = max(qr - 4, 0)
                                    elif v == 1:
                                        lo = qr
                                    else:
                                        lo = 8 + min(qr - 4, 0)
                                    if lo <= kr < lo + 8:
                                        qs.append(qr)
                                if not qs:
                                    continue
                                qa, qb = qs[0], qs[-1] + 1
                                dra = 7 - kr + qa + delta
                                row0 = ((l * 6 + hh) * 15 + dra) * 64
                                src = rbx_d[row0:row0 + (qb - qa) * 64, :].rearrange("(q k) c -> k q c", k=64)
                                krb = kr % 2
                                S.dma("pool", mb.t[krb * 64:(krb + 1) * 64, v, kr // 2, qa * 64:qb * 64].rearrange("p (q c) -> p q c", c=64),
                                      src, [Bconst], [mb], mb)
                        qsrc, ksrc, row0, nrows, voff = naq_d, nak_d, hh * 64, 64, hh * 64
                    elif kind == "da":
                        scale = 32 ** -0.5
                        qsrc, ksrc, row0, nrows, voff = daq_d, dak_d, hh * 64, 64, 384 + hh * 64
                    else:
                        scale = 96 ** -0.5
                        qsrc, ksrc, row0, nrows, voff = mlq_d, mlk_d, hh * 96, 96, 768 + hh * 64
                    orow = voff
                    for s in range(3):
                        n = cfg.seqn[s]
                        st = cfg.start[s]
                        mcol = R + NMETA * s
                        nt = n // 128
                        Kt = ktring.next()
                        Km = kmring.next()
                        Vv = vring.next()
                        Vm = vmring.next()
                        if kind == "ml":
                            S.dma("sp", Kt.t[0:64, :n], mlk_d[hh * 64:(hh + 1) * 64, st:st + n], [Bqk], [Kt], Kt)
                            S.dma("sp", Kt.t[64:96, :n], mlr_d[0:32, st:st + n], [Bqk], [Kt], Kt)
                            S.dma("sp", Km.t[0:64, :], mlk_d[hh * 64:(hh + 1) * 64, mcol:mcol + NMETA], [Bqk], [Km], Km)
                            S.dma("sp", Km.t[64:96, :], mlr_d[0:32, mcol:mcol + NMETA], [Bqk], [Km], Km)
                        else:
                            S.dma("sp", Kt.t[0:64, :n], ksrc[row0:row0 + 64, st:st + n], [Bqk], [Kt], Kt)
                            S.dma("sp", Km.t[0:64, :], ksrc[row0:row0 + 64, mcol:mcol + NMETA], [Bqk], [Km], Km)
                        for t0 in range(0, nt, 16):
                            t1_ = min(nt, t0 + 16)
                            S.dma("sp", Vv.t[:, t0:t1_, 0:64],
                                  v_d[st + t0 * 128:st + t1_ * 128, voff:voff + 64].rearrange("(t p) e -> p t e", p=128), [Bv], [Vv], Vv)
                        S.dma("sp", Vm.t[:, 0:64], v_d[mcol:mcol + NMETA, voff:voff + 64], [Bv], [Vm], Vm)

                        qtiles = [(st + q0, 512, q0 // 512) for q0 in range(0, n, 512)]
                        if not last:
                            qtiles.append((mcol, NMETA, -1))
                        nblk = n // 512
                        rows = n // 64
                        for (qc0, Nq, qb_) in qtiles:
                            if kind == "da":
                                Q0 = q0ring.next()
                                Q1 = q1ring.next()
                                S.dma("sp", Q0.t[0:32, :Nq], qsrc[row0:row0 + 32, qc0:qc0 + Nq], [Bqk], [Q0], Q0)
                                S.dma("sp", Q1.t[32:64, :Nq], qsrc[row0 + 32:row0 + 64, qc0:qc0 + Nq], [Bqk], [Q1], Q1)
                                streams = [Q0, Q1]
                            else:
                                Q = qaring.next()
                                S.dma("sp", Q.t[:nrows, :Nq], qsrc[row0:row0 + nrows, qc0:qc0 + Nq], [Bqk], [Q], Q)
                                streams = [Q]
                            meta_kt = (Km, (lambda Km=Km: Km.t[:, 0:NMETA]), Vm, Vm.t[:NMETA, 0:128], NMETA, None)
                            if kind == "na":
                                if qb_ < 0:
                                    ktl = [meta_kt]
                                else:
                                    v = 0 if qb_ == 0 else (2 if qb_ == nblk - 1 else 1)
                                    w0 = min(max(8 * qb_ - 4, 0), rows - 16)
                                    kt0 = w0 // 2
                                    ktl = []
                                    for j in range(8):
                                        kt = kt0 + j
                                        ktl.append((Kt, (lambda kt=kt, Kt=Kt: Kt.t[:, kt * 128:(kt + 1) * 128]), Vv, Vv.t[:, kt, 0:128], 128,
                                                    (mb, mb.t[:, v, j, :])))
                                    ktl.append(meta_kt)
                            else:
                                ktl = [(Kt, (lambda kt=kt, Kt=Kt: Kt.t[:, kt * 128:(kt + 1) * 128]), Vv, Vv.t[:, kt, 0:128], 128, None)
                                       for kt in range(nt)]
                                ktl.append(meta_kt)
                            O = run_qtile(streams, Nq, ktl, scale)
                            oo = ooring.next()
                            key = qc0 if qc0 < R else R
                            pending.extend(post_stages(kind, O, Nq, oo, orow, qc0, key))
                        emit_late(4)
                drain(10 ** 9)
                emit_late(10 ** 9)
                S.barrier()
                S.release(phase_bufs)
                del phase_bufs[:]

        if STOP >= 1:
            tok_phase(0)
        if STOP >= 2:
            att_phase(0, False)
        if STOP >= 3:
            tok_phase(1)
        if STOP >= 4:
            att_phase(1, True)
        if STOP >= 5:
            tok_phase(2)
        S.barrier()
    return nc


_CACHE = {}


def run(cfg, inp):
    sh = _host_shared(cfg, inp)
    if "nc" not in _CACHE or _CACHE.get("cfg") != (cfg.NP, cfg.NS, cfg.DFF):
        _CACHE["nc"] = build_program(cfg)
        _CACHE["cfg"] = (cfg.NP, cfg.NS, cfg.DFF)
    nc = _CACHE["nc"]
    xp, xs = inp["x_prompt"], inp["x_sample"]
    in_maps = []
    for c in range(NCORES):
        m = dict(sh)
        m["xtok"] = np.ascontiguousarray(np.concatenate([xp[c], xs[2 * c], xs[2 * c + 1]], axis=0).astype(np.float32))
        in_maps.append(m)
    res = run_bass_kernel_spmd(nc, in_maps, core_ids=list(range(NCORES)))
    _CACHE["res"] = res
    yp = np.empty(xp.shape, np.float32)
    ys = np.empty(xs.shape, np.float32)
    for c in range(NCORES):
        y = res.results[c]["y"]
        yp[c] = y[:cfg.NP]
        ys[2 * c] = y[cfg.NP:cfg.NP + cfg.NS]
        ys[2 * c + 1] = y[cfg.NP + cfg.NS:]
    return yp, ys


def kernel(**inputs):
    inp = {k: np.asarray(v) for k, v in inputs.items()}
    cfg = Cfg(inp["x_prompt"].shape[1], inp["x_sample"].shape[1], inp["ffn_w_gate"].shape[-1])
    return run(cfg, inp)
```
